# Optimizing a Trainium2 kernel written in Bass

```python
import jax, jax.numpy as jnp
from jax import lax
import numpy as np

D_MODEL = 2048
BATCH = 4
SEQ = 2048
DEPTH = 2

HEAD_DIM = 128
A_HEADS = 6
A_KV_HEADS = 2
A_ROPE_THETA = 10000.0
B_HEADS = 4
B_PATTERNS = ((128, 1), (512, 4), (2048, 16))
B_N_GROUPS = 3
C_HEADS = 6
C_Q_RANK = 512
C_KV_RANK = 512
C_NOPE_DIM = 128
C_ROPE_DIM = 64
C_V_DIM = 128
C_ROPE_THETA = 10000.0
PARTIAL_ROPE_DIM = HEAD_DIM // 4
PARTIAL_ROPE_THETA = 500000.0
GRID_W = 64
Q_BLOCK = 128
D_FF = 5632
CONV_W = 3
EPS = 1e-6

A_WIDTH = A_HEADS * HEAD_DIM
B_WIDTH = B_HEADS * HEAD_DIM
C_WIDTH = C_HEADS * C_V_DIM
MIX_WIDTH = A_WIDTH + B_WIDTH + C_WIDTH

A_Q_COLS = A_HEADS * HEAD_DIM
A_KV_COLS = A_KV_HEADS * HEAD_DIM
B_Q_COLS = B_N_GROUPS * B_HEADS * HEAD_DIM
B_KV_COLS = B_HEADS * HEAD_DIM
IN_SPLITS = (A_Q_COLS, A_KV_COLS, A_KV_COLS, B_Q_COLS, B_KV_COLS, B_KV_COLS, C_Q_RANK, C_KV_RANK, C_ROPE_DIM)
IN_WIDTH = A_Q_COLS + 2 * A_KV_COLS + B_Q_COLS + 2 * B_KV_COLS + C_Q_RANK + C_KV_RANK + C_ROPE_DIM

kernel_name = "hybrid_parallel_gqa_dilated_mla_convffn"


def rms_norm(x, g):
    xf = x.astype(jnp.float32)
    y = xf * lax.rsqrt(jnp.mean(xf * xf, axis=-1, keepdims=True) + EPS)
    return (y * g.astype(jnp.float32)).astype(x.dtype)


def rope_table(pos, dim, theta):
    inv = theta ** (-jnp.arange(0, dim, 2, dtype=jnp.float32) / dim)
    ang = pos.astype(jnp.float32)[:, None] * inv[None, :]
    return jnp.cos(ang), jnp.sin(ang)


def apply_rope(x, cs):
    cos, sin = cs
    cos = cos[:, None, :]
    sin = sin[:, None, :]
    xf = x.astype(jnp.float32)
    x1, x2 = jnp.split(xf, 2, axis=-1)
    return jnp.concatenate([x1 * cos - x2 * sin, x1 * sin + x2 * cos], axis=-1).astype(x.dtype)


def axial_rope(x, cs_row, cs_col):
    half = x.shape[-1] // 2
    return jnp.concatenate([apply_rope(x[..., :half], cs_row), apply_rope(x[..., half:], cs_col)], axis=-1)


def partial_rope(x, cs):
    return jnp.concatenate([apply_rope(x[..., :PARTIAL_ROPE_DIM], cs), x[..., PARTIAL_ROPE_DIM:]], axis=-1)


def dense_block_attention(q, k, v):
    b, s, hq, dk = q.shape
    hkv = k.shape[2]
    g = hq // hkv
    nb = s // Q_BLOCK
    scale = dk ** -0.5
    qb = q.reshape(b, nb, Q_BLOCK, hkv, g, dk).transpose(1, 0, 2, 3, 4, 5)

    def one_block(qblk):
        sc = jnp.einsum('bqhgd,bkhd->bhgqk', qblk, k).astype(jnp.float32) * scale
        p = jax.nn.softmax(sc, axis=-1).astype(v.dtype)
        return jnp.einsum('bhgqk,bkhd->bqhgd', p, v)

    o = lax.map(one_block, qb)
    return o.transpose(1, 0, 2, 3, 4, 5).reshape(b, s, hq, v.shape[-1])


def fold_stride(x, d):
    b, s, h, dh = x.shape
    return x.reshape(b, s // d, d, h, dh).transpose(0, 2, 1, 3, 4).reshape(b * d, s // d, h, dh)


def unfold_stride(x, b, d):
    l = x.shape[1]
    rest = x.shape[2:]
    perm = (0, 2, 1) + tuple(range(3, 3 + len(rest)))
    return x.reshape((b, d, l) + rest).transpose(perm).reshape((b, l * d) + rest)


def banded_attention(q, k, v, half):
    n, l, h, dh = q.shape
    blk = half
    nb = -(-l // blk)
    lp = nb * blk
    qp = jnp.pad(q, ((0, 0), (0, lp - l), (0, 0), (0, 0))).reshape(n, nb, blk, h, dh)

    def key_blocks(t):
        tp = jnp.pad(t, ((0, 0), (blk, lp - l + blk), (0, 0), (0, 0))).reshape(n, nb + 2, blk, h, dh)
        return jnp.concatenate([tp[:, :-2], tp[:, 1:-1], tp[:, 2:]], axis=2)

    kb = key_blocks(k)
    vb = key_blocks(v)
    qpos = jnp.arange(nb)[:, None] * blk + jnp.arange(blk)[None, :]
    kpos = jnp.arange(nb)[:, None] * blk - blk + jnp.arange(3 * blk)[None, :]
    rel = kpos[:, None, :] - qpos[:, :, None]
    valid = (jnp.abs(rel) <= half) & (kpos[:, None, :] >= 0) & (kpos[:, None, :] < l)
    sc = jnp.einsum('nbqhd,nbkhd->nbhqk', qp, kb).astype(jnp.float32) * (dh ** -0.5)
    sc = jnp.where(valid[None, :, None], sc, -jnp.inf)
    mx = jnp.max(sc, axis=-1, keepdims=True)
    e = jnp.exp(sc - mx)
    den = jnp.sum(e, axis=-1, keepdims=True)
    o = jnp.einsum('nbhqk,nbkhd->nbqhd', (e / den).astype(v.dtype), vb)
    lse = (mx + jnp.log(den))[..., 0]
    o = o.reshape(n, lp, h, dh)[:, :l]
    lse = lse.transpose(0, 1, 3, 2).reshape(n, lp, h)[:, :l]
    return o, lse


def dilated_mixture(q_groups, k, v):
    b = k.shape[0]
    outs = []
    lses = []
    for g, (window, dil) in enumerate(B_PATTERNS):
        half = window // (2 * dil)
        o, lse = banded_attention(fold_stride(q_groups[:, :, g], dil), fold_stride(k, dil),
                                  fold_stride(v, dil), half)
        outs.append(unfold_stride(o, b, dil))
        lses.append(unfold_stride(lse, b, dil))
    w = jax.nn.softmax(jnp.stack(lses, axis=0), axis=0)
    y = jnp.einsum('gbsh,gbshd->bshd', w, jnp.stack(outs, axis=0).astype(jnp.float32))
    return y.astype(k.dtype)


def conv_ffn(x, w_up, conv_w, conv_b, w_down):
    h = x @ w_up
    s = h.shape[1]
    hp = jnp.pad(h, ((0, 0), (1, 1), (0, 0)))
    h = hp[:, :s] * conv_w[0] + hp[:, 1:s + 1] * conv_w[1] + hp[:, 2:] * conv_w[2] + conv_b
    gate, up = jnp.split(h, 2, axis=-1)
    return (jax.nn.silu(gate) * up) @ w_down


def setup_inputs(seed: int = 0) -> dict:
    key = jax.random.key(seed)
    ks = jax.random.split(key, 20)
    f32 = jnp.float32

    def nrm(k, shape, scale):
        return jax.random.normal(k, shape, f32) * scale

    def gain(k, shape):
        return 1.0 + 0.02 * jax.random.normal(k, shape, f32)

    return {
        "x": jax.random.normal(ks[0], (BATCH, SEQ, D_MODEL), f32),
        "attn_norm": gain(ks[1], (DEPTH, D_MODEL)),
        "w_in": nrm(ks[2], (DEPTH, D_MODEL, IN_WIDTH), D_MODEL ** -0.5),
        "a_q_norm": gain(ks[3], (DEPTH, HEAD_DIM)),
        "a_k_norm": gain(ks[4], (DEPTH, HEAD_DIM)),
        "c_q_norm": gain(ks[5], (DEPTH, C_Q_RANK)),
        "c_kv_norm": gain(ks[6], (DEPTH, C_KV_RANK)),
        "w_uq": nrm(ks[7], (DEPTH, C_Q_RANK, C_HEADS * (C_NOPE_DIM + C_ROPE_DIM)), C_Q_RANK ** -0.5),
        "w_ukv": nrm(ks[8], (DEPTH, C_KV_RANK, C_HEADS * (C_NOPE_DIM + C_V_DIM)), C_KV_RANK ** -0.5),
        "out_norm": gain(ks[9], (DEPTH, MIX_WIDTH)),
        "w_out": nrm(ks[10], (DEPTH, MIX_WIDTH, D_MODEL), MIX_WIDTH ** -0.5),
        "ffn_norm": gain(ks[11], (DEPTH, D_MODEL)),
        "w_up": nrm(ks[12], (DEPTH, D_MODEL, 2 * D_FF), D_MODEL ** -0.5),
        "conv_w": nrm(ks[13], (DEPTH, CONV_W, 2 * D_FF), CONV_W ** -0.5),
        "conv_b": nrm(ks[14], (DEPTH, 2 * D_FF), 0.01),
        "w_down": nrm(ks[15], (DEPTH, D_FF, D_MODEL), D_FF ** -0.5),
        "final_norm": gain(ks[16], (D_MODEL,)),
    }


def reference(x, attn_norm, w_in, a_q_norm, a_k_norm, c_q_norm, c_kv_norm, w_uq, w_ukv, out_norm,
              w_out, ffn_norm, w_up, conv_w, conv_b, w_down, final_norm):
    b, s, _ = x.shape
    rows = s // GRID_W
    row_pos = jnp.repeat(jnp.arange(rows), GRID_W)
    col_pos = jnp.tile(jnp.arange(GRID_W), rows)
    t_pos = jnp.arange(s)
    cs_row = rope_table(row_pos, HEAD_DIM // 2, A_ROPE_THETA)
    cs_col = rope_table(col_pos, HEAD_DIM // 2, A_ROPE_THETA)
    cs_part = rope_table(t_pos, PARTIAL_ROPE_DIM, PARTIAL_ROPE_THETA)
    cs_mla = rope_table(t_pos, C_ROPE_DIM, C_ROPE_THETA)
    split_idx = np.cumsum(np.array(IN_SPLITS))[:-1].tolist()

    for l in range(DEPTH):
        hn = rms_norm(x, attn_norm[l])
        proj = hn @ w_in[l]
        aq, ak, av, bq, bk, bv, cq, ckv, ckr = jnp.split(proj, split_idx, axis=-1)

        aq = axial_rope(rms_norm(aq.reshape(b, s, A_HEADS, HEAD_DIM), a_q_norm[l]), cs_row, cs_col)
        ak = axial_rope(rms_norm(ak.reshape(b, s, A_KV_HEADS, HEAD_DIM), a_k_norm[l]), cs_row, cs_col)
        av = av.reshape(b, s, A_KV_HEADS, HEAD_DIM)
        ya = dense_block_attention(aq, ak, av).reshape(b, s, A_WIDTH)

        bq = partial_rope(bq.reshape(b, s, B_N_GROUPS * B_HEADS, HEAD_DIM), cs_part)
        bq = bq.reshape(b, s, B_N_GROUPS, B_HEADS, HEAD_DIM)
        bk = partial_rope(bk.reshape(b, s, B_HEADS, HEAD_DIM), cs_part)
        bv = bv.reshape(b, s, B_HEADS, HEAD_DIM)
        yb = dilated_mixture(bq, bk, bv).reshape(b, s, B_WIDTH)

        q = (rms_norm(cq, c_q_norm[l]) @ w_uq[l]).reshape(b, s, C_HEADS, C_NOPE_DIM + C_ROPE_DIM)
        q = jnp.concatenate([q[..., :C_NOPE_DIM], apply_rope(q[..., C_NOPE_DIM:], cs_mla)], axis=-1)
        kv = (rms_norm(ckv, c_kv_norm[l]) @ w_ukv[l]).reshape(b, s, C_HEADS, C_NOPE_DIM + C_V_DIM)
        k_nope = kv[..., :C_NOPE_DIM]
        cv = kv[..., C_NOPE_DIM:]
        k_rope = apply_rope(ckr[:, :, None, :], cs_mla)
        k = jnp.concatenate([k_nope, jnp.broadcast_to(k_rope, (b, s, C_HEADS, C_ROPE_DIM))], axis=-1)
        yc = dense_block_attention(q, k, cv).reshape(b, s, C_WIDTH)

        g_a, g_b, g_c = jnp.split(out_norm[l], [A_WIDTH, A_WIDTH + B_WIDTH])
        y = jnp.concatenate([rms_norm(ya, g_a), rms_norm(yb, g_b), rms_norm(yc, g_c)], axis=-1)
        x = x + y @ w_out[l]

        x = x + conv_ffn(rms_norm(x, ffn_norm[l]), w_up[l], conv_w[l], conv_b[l], w_down[l])

    return rms_norm(x, final_norm)
```

```python
import math
from contextlib import ExitStack

import numpy as np
import concourse.bass as bass
import concourse.mybir as mybir
from concourse.bass_utils import run_bass_kernel_spmd

F32 = mybir.dt.float32
BF16 = mybir.dt.bfloat16
AF = mybir.ActivationFunctionType
ALU = mybir.AluOpType
AX = mybir.AxisListType

D = 2048
S = 2048
T = 1024
NT = 8
DEPTH = 2
HD = 128
INW = 4928
DFF = 5632
EPS = 1e-6
NCORES = 8
PAIRS = [[0, 1], [2, 3], [4, 5], [6, 7]]

SAME_ENGINE_SYNC = True
STRICT_SAME_ENGINE = True


class Res:
    __slots__ = ("name", "w", "r", "dsem", "dcount", "excl")

    def __init__(self, name):
        self.name = name
        self.excl = False
        self.w = None
        self.r = []
        self.dsem = None
        self.dcount = 0


class Eng:
    def __init__(self, name, h, sem):
        self.name = name
        self.h = h
        self.sem = sem
        self.count = 0
        self.seen = {}
        self.seen_d = {}


class Ctx:
    def __init__(self, nc, stack):
        self.nc = nc
        self.stack = stack
        self.engs = {}
        for name, h in (("pe", nc.tensor), ("act", nc.scalar), ("dve", nc.vector),
                        ("pool", nc.gpsimd), ("sp", nc.sync)):
            sem = stack.enter_context(nc.semaphore("sem_" + name))
            self.engs[name] = Eng(name, h, sem)
        self.nsem = 5
        self.nwaits = 0
        self.nops = 0
        self.all_res = []

    def res(self, name):
        r = Res(name)
        self.all_res.append(r)
        return r

    def _need(self, e, toks):
        best_e = {}
        best_d = {}
        for item in toks:
            t, raw = item
            if t is None:
                continue
            if t[0] == "e":
                _, en, tick = t
                if en == e.name and (en in ("pe", "sp") or not raw or not SAME_ENGINE_SYNC):
                    continue
                if tick > best_e.get(en, 0):
                    best_e[en] = tick
            else:
                _, sem, val = t
                k = id(sem)
                if val > best_d.get(k, (None, 0))[1]:
                    best_d[k] = (sem, val)
        for en, tick in best_e.items():
            if e.seen.get(en, 0) >= tick:
                continue
            e.h.wait_ge(self.engs[en].sem, tick)
            e.seen[en] = tick
            self.nwaits += 1
        for k, (sem, val) in best_d.items():
            if e.seen_d.get(k, 0) >= val:
                continue
            e.h.wait_ge(sem, val)
            e.seen_d[k] = val
            self.nwaits += 1

    @staticmethod
    def _deps(reads, writes, eng=None):
        toks = []
        for r in reads:
            toks.append((r.w, True))
            if r.excl:
                toks.extend((t, False) for t in r.r if t[0] == "e" and t[1] != eng)
        for w in writes:
            toks.append((w.w, STRICT_SAME_ENGINE))
            toks.extend((t, STRICT_SAME_ENGINE) for t in w.r)
        return toks

    @staticmethod
    def _commit(tok, reads, writes):
        for r in reads:
            r.r.append(tok)
        for w in writes:
            w.w = tok
            w.r = []

    def op(self, eng, fn, reads=(), writes=(), inc=True):
        e = self.engs[eng]
        self._need(e, self._deps(reads, writes, eng))
        ins = fn(e.h)
        self.nops += 1
        if inc:
            ins.then_inc(e.sem, 1)
            e.count += 1
            tok = ("e", eng, e.count)
        else:
            tok = ("e", eng, e.count + 1)
        self._commit(tok, reads, writes)
        return tok

    def _dsem(self, wres):
        if wres.dsem is None:
            wres.dsem = self.stack.enter_context(self.nc.semaphore("dsem_%d" % self.nsem))
            self.nsem += 1
        return wres.dsem

    def dma(self, queue, pairs, reads=(), writes=()):
        e = self.engs[queue]
        self._need(e, self._deps(reads, writes))
        wres = writes[0]
        sem = self._dsem(wres)
        for (o, i) in pairs:
            e.h.dma_start(out=o, in_=i).then_inc(sem, 16)
            wres.dcount += 16
            self.nops += 1
        tok = ("d", sem, wres.dcount)
        self._commit(tok, reads, writes)
        return tok

    def collective(self, in_ap, out_ap, reads=(), writes=()):
        e = self.engs["pool"]
        self._need(e, self._deps(reads, writes))
        wres = writes[0]
        sem = self._dsem(wres)
        e.h.collective_compute("AllGather", ALU.bypass, replica_groups=PAIRS,
                               ins=[in_ap], outs=[out_ap]).then_inc(sem, 1)
        wres.dcount += 1
        tok = ("d", sem, wres.dcount)
        self._commit(tok, reads, writes)
        return tok

    def wait_all(self, eng, ress):
        e = self.engs[eng]
        toks = []
        for r in ress:
            toks.append((r.w, True))
            toks.extend((t, True) for t in r.r)
        self._need(e, toks)

    def barrier(self, ress):
        for en in ("pe", "act", "dve", "sp"):
            self.wait_all(en, ress)


def _rope_tab(pos, dim, theta):
    inv = theta ** (-np.arange(0, dim, 2, dtype=np.float64) / dim)
    ang = pos.astype(np.float64)[:, None] * inv[None, :]
    return np.cos(ang), np.sin(ang)


def _tables_for_core(h):
    t = np.arange(h * T, (h + 1) * T)
    cr, sr = _rope_tab(t // 64, 64, 10000.0)
    cc, sc = _rope_tab(t % 64, 64, 10000.0)
    CA = np.concatenate([cr, cr, cc, cc], 1)
    SA = np.concatenate([-sr, sr, -sc, sc], 1)
    cp, sp_ = _rope_tab(t, 32, 500000.0)
    CB = np.concatenate([cp, cp], 1)
    SB = np.concatenate([-sp_, sp_], 1)
    cm, sm = _rope_tab(t, 64, 10000.0)
    CM = np.concatenate([cm, cm], 1)
    SM = np.concatenate([-sm, sm], 1)
    tab = np.concatenate([CA, SA, CB, SB, CM, SM], 1).astype(np.float32)
    return np.ascontiguousarray(tab.reshape(NT, 128, 448).transpose(1, 0, 2))


TAB_CA, TAB_SA, TAB_CB, TAB_SB, TAB_CM, TAB_SM = 0, 128, 256, 288, 320, 384


def _consts_for_core(h):
    c = {}
    c["tab"] = _tables_for_core(h)
    c["ident"] = np.eye(128, dtype=np.float32)
    p = np.arange(128)[:, None]
    n = np.arange(128)[None, :]
    NEG = -30000.0
    bm = np.concatenate([np.where(p >= n, 0.0, NEG), np.where(p <= n, 0.0, NEG)], 1).astype(np.float32)
    c["bmask"] = np.ascontiguousarray(bm)
    mL = 1.0 if h == 1 else 0.0
    mR = 1.0 if h == 0 else 0.0
    c["mlr_rep"] = np.tile(np.array([[mL, mR]], np.float32), (128, 1))
    c["mlr_col"] = np.array([[mL], [mR]], np.float32)
    return c


IN_CHUNKS = [
    (4352, 512, "ckv"), (4864, 64, "ckr"), (2816, 512, "bk"), (3328, 512, "bv"), (1024, 256, "av"),
    (512, 512, "a1"), (0, 512, "a0"), (1280, 512, "bq0"), (1792, 512, "bq1"), (2304, 512, "bq2"),
    (3840, 512, "cq"),
]

FFN_STEPS = []
for _g in range(11):
    FFN_STEPS += [("up", _g, 0), ("up", _g, 1)]
    if _g >= 1:
        FFN_STEPS.append(("down", _g - 1))
FFN_STEPS.append(("down", 10))

KVG_COLS = [6144, 6144, 6656, 6144]
KVREG = {
    "ka": (0, 0), "kb": (0, 2 * 128 * T),
    "kc": (1, 0),
    "kr": (2, 0), "va": (2, 64 * T), "vb": (2, 64 * T + T * 256),
    "vc": (3, 0),
}
KVGRP_KEYS = [("ka", "kb"), ("kc",), ("kr", "va", "vb"), ("vc",)]


class Arena:
    def __init__(self, t, nwords):
        self.t = t
        self.n = nwords
        self.off = 0

    def reset(self):
        self.off = 0

    def f32(self, n):
        assert self.off + n <= self.n, ("arena overflow", self.off, n, self.n)
        v = self.t[:, self.off:self.off + n]
        self.off += n
        return v

    def bf(self, n):
        w = (n + 1) // 2
        v = self.f32(w).bitcast(BF16)
        return v[:, 0:n]


class Builder:
    def __init__(self, depth=DEPTH, dbg=None, stop=None):
        self.depth = depth
        self.dbg = dbg or set()
        self.stop = stop
        self.nc = bass.Bass("TRN2", target_bir_lowering=False)
        self.stack = ExitStack()

    def dram_in(self, name, shape, dt=F32):
        return self.nc.dram_tensor(name, list(shape), dt, kind="ExternalInput").ap()

    def dram_tmp(self, name, shape, dt, out=False):
        kind = "ExternalOutput" if out else "Internal"
        return self.nc.dram_tensor(name, list(shape), dt, kind=kind).ap()

    def sb(self, name, shape, dt):
        return self.stack.enter_context(self.nc.sbuf_tensor("s_" + name, list(shape), dt))

    def build(self):
        nc = self.nc
        st = self.stack
        cx = self.cx = Ctx(nc, st)
        L = self.depth
        self.x_in = self.dram_in("x", [T, D])
        self.w_in = self.dram_in("w_in", [L, D, INW])
        self.w_uq = self.dram_in("w_uq", [L, 512, 1152])
        self.w_ukv = self.dram_in("w_ukv", [L, 512, 1536])
        self.need_out = self.stop not in ("p1", "p1a", "p2")
        self.need_ffn = self.stop not in ("p1", "p1a", "p2", "p3")
        if self.need_out:
            self.w_out = self.dram_in("w_out", [L, D, D])
        if self.need_ffn:
            self.w_up = self.dram_in("w_up", [L, D, 2 * DFF])
            self.w_down = self.dram_in("w_down", [L, DFF, D])
        self.g_attn = self.dram_in("attn_norm", [DEPTH, D])
        self.g_ffn = self.dram_in("ffn_norm", [DEPTH, D])
        self.g_final = self.dram_in("final_norm", [1, D])
        self.g_aq = self.dram_in("a_q_norm", [DEPTH, 128])
        self.g_ak = self.dram_in("a_k_norm", [DEPTH, 128])
        self.g_cq = self.dram_in("c_q_norm", [DEPTH, 512])
        self.g_ckv = self.dram_in("c_kv_norm", [DEPTH, 512])
        self.g_out = self.dram_in("out_norm_t", [DEPTH, 128, 16])
        self.convp = self.dram_in("convp", [DEPTH, 128, 88 * 4])
        self.tab_d = self.dram_in("tab", [128, NT, 448])
        self.ident_d = self.dram_in("ident", [128, 128])
        self.bmask_d = self.dram_in("bmask", [128, 256])
        self.mlr_rep_d = self.dram_in("mlr_rep", [128, 2])
        self.mlr_col_d = self.dram_in("mlr_col", [2, 1])
        self.out_d = self.nc.dram_tensor("out", [T, D], F32, kind="ExternalOutput").ap()
        dq = "q" in self.dbg
        self.qa_d = self.dram_tmp("qa_d", [6, 128, T], BF16, dq)
        self.qb_d = self.dram_tmp("qb_d", [12, 128, T], BF16, dq)
        self.qcn_d = self.dram_tmp("qcn_d", [6, 128, T], BF16, dq)
        self.qcr_d = self.dram_tmp("qcr_d", [6, 64, T], BF16, dq)
        self.kvb = [self.dram_tmp("kvb%d" % i, [128, c], BF16) for i, c in enumerate(KVG_COLS)]
        self.kvg = [self.dram_tmp("kvg%d" % i, [256, c], BF16) for i, c in enumerate(KVG_COLS)]
        self.halo_b = self.dram_tmp("halo_b", [2, D], F32)
        self.halo_g = self.dram_tmp("halo_g", [4, D], F32)
        if "kv" in self.dbg:
            self.kvdbg = [self.dram_tmp("kvdbg%d" % i, [256, c], BF16, True) for i, c in enumerate(KVG_COLS)]
        if "x" in self.dbg or "x2" in self.dbg:
            self.xdbg = self.dram_tmp("xdbg", [T, D], F32, True)
        if "y" in self.dbg:
            self.ydbg = self.dram_tmp("ydbg", [16, 128, T], BF16, True)
        self.r_qd = {k: cx.res("qd_" + k) for k in ("qa", "qb", "qcn", "qcr")}
        self.r_kvb = {k: cx.res("kvb_" + k) for k in ("ka", "kb", "kc", "kr", "va", "vb", "vc")}
        self.r_kvg = [cx.res("kvg%d" % i) for i in range(4)]
        self.r_halo_b = cx.res("halo_b")
        self.r_halo_g = cx.res("halo_g")
        self.X = self.sb("X", [128, NT, D], F32)
        self.rX = [cx.res("X%d" % i) for i in range(NT)]
        self.actT_full = self.sb("actT", [128, 16, T + 2], BF16)
        self.actT = self.actT_full[:, :, 1:T + 1]
        self.rAT = [cx.res("actT%d" % i) for i in range(NT)]
        self.rAH = cx.res("actH")
        self.NSLOT = 3
        self.wsl = [self.sb("wsl%d" % i, [128, 8192], BF16) for i in range(self.NSLOT)]
        self.rW = [cx.res("wsl%d" % i) for i in range(self.NSLOT)]
        self.grep = self.sb("grep", [128, D], F32)
        self.rG = cx.res("grep")
        self.ident = self.sb("ident", [128, 128], BF16)
        self.ones = self.sb("ones", [128, 128], BF16)
        self.ones_f = self.sb("ones_f", [128, 128], F32)
        self.onesLR = self.sb("onesLR", [64, 2, 128], BF16)
        self.bmask = self.sb("bmask", [128, 256], BF16)
        self.onesE = self.sb("onesE", [128, 3, 128], BF16)
        self.bmask01 = self.sb("bmask01", [128, 256], BF16)
        self.mlr_rep = self.sb("mlr_rep_s", [128, 2], F32)
        self.mlr_col = self.sb("mlr_col_s", [2, 1], F32)
        self.rC = cx.res("consts")
        self.gqk = self.sb("gqk", [128, 2, 128], F32)
        self.gout = self.sb("gout", [128, 16], F32)
        self.cvp = self.sb("cvp", [128, 88, 4], F32)
        self.rLP = cx.res("layer_params")
        self.ARW = 12800
        self.ar_t = self.sb("arena", [128, self.ARW], F32)
        self.ar = Arena(self.ar_t, self.ARW)
        self.ps = [st.enter_context(nc.psum_tensor("ps%d" % i, [128, 512], F32)) for i in range(8)]
        self.rP = [cx.res("ps%d" % i) for i in range(8)]
        for r in self.rP:
            r.excl = True
        self.scratch_res = []

        self.setup_weight_stream()
        self.prologue()
        for l in range(L):
            self.layer_params(l)
            self.phase1(l)
            if self.stop in ("p1", "p1a"):
                break
            self.phase2(l)
            if self.stop == "p2":
                break
            self.phase3(l)
            if self.stop == "p3":
                break
            self.phase4(l)
        if self.stop is None:
            self.epilogue()
        self.finish()
        return nc

    def warm_pe(self, n=48, bank=7):
        cx = self.cx
        for i in range(n):
            cx.op("pe", lambda h: h.matmul(self.ps[bank][:, 0:256], self.ident[:, :], self.bmask[:, 0:256], start=True, stop=True),
                  reads=[self.rC], writes=[self.rP[bank]], inc=(i == n - 1))

    def sres(self, name):
        r = self.cx.res(name)
        self.scratch_res.append(r)
        return r

    def phase_barrier(self):
        self.cx.barrier(self.cx.all_res)
        self.scratch_res = []
        self.ar.reset()

    def setup_weight_stream(self):
        ch = []
        for l in range(self.depth):
            for (c0, ncol, kind) in IN_CHUNKS:
                src = self.w_in[l, :, c0:c0 + ncol].rearrange("(kt p) n -> p kt n", p=128)
                ch.append([(lambda s, ncol=ncol: s[:, 0:16 * ncol].rearrange("p (k n) -> p k n", n=ncol), src)])
                if kind == "a1":
                    src2 = self.w_ukv[l].rearrange("(kt p) n -> p kt n", p=128)
                    ch.append([(lambda s: s[:, 0:4 * 1536].rearrange("p (k n) -> p k n", n=1536), src2)])
            src = self.w_uq[l].rearrange("(kt p) n -> p kt n", p=128)
            ch.append([(lambda s: s[:, 0:4 * 1152].rearrange("p (k n) -> p k n", n=1152), src)])
            for c in range(4 if self.need_out else 0):
                src = self.w_out[l, :, c * 512:(c + 1) * 512].rearrange("(kt p) n -> p kt n", p=128)
                ch.append([(lambda s: s[:, 0:8192].rearrange("p (k n) -> p k n", n=512), src)])
            for step in (FFN_STEPS if self.need_ffn else []):
                if step[0] == "up":
                    bi = 2 * step[1] + step[2]
                    sg = self.w_up[l, :, 256 * bi:256 * bi + 256].rearrange("(kt p) n -> p kt n", p=128)
                    su = self.w_up[l, :, DFF + 256 * bi:DFF + 256 * bi + 256].rearrange("(kt p) n -> p kt n", p=128)
                    ch.append([
                        (lambda s: s[:, 0:8192].rearrange("p (k n) -> p k n", n=512)[:, :, 0:256], sg),
                        (lambda s: s[:, 0:8192].rearrange("p (k n) -> p k n", n=512)[:, :, 256:512], su),
                    ])
                else:
                    gg = step[1]
                    sd = self.w_down[l, 512 * gg:512 * gg + 512, :].rearrange("(ct p) n -> p ct n", p=128)
                    ch.append([(lambda s: s[:, 0:8192].rearrange("p (k n) -> p k n", n=2048), sd)])
        self.wchunks = ch
        self.w_next_load = 0
        self.w_next_use = 0

    def _wload(self, k):
        slot = k % self.NSLOT
        pairs = [(fn(self.wsl[slot]), src) for (fn, src) in self.wchunks[k]]
        self.cx.dma("pool", pairs, writes=[self.rW[slot]])

    def wget(self):
        k = self.w_next_use
        self.w_next_use += 1
        while self.w_next_load < len(self.wchunks) and self.w_next_load <= k + self.NSLOT - 1:
            self._wload(self.w_next_load)
            self.w_next_load += 1
        slot = k % self.NSLOT
        return self.wsl[slot], self.rW[slot]

    def prologue(self):
        cx = self.cx
        for tt in range(NT):
            cx.dma("sp", [(self.X[:, tt, :], self.x_in[tt * 128:(tt + 1) * 128, :])], writes=[self.rX[tt]])
        rc = self.rC
        r1 = cx.res("c1")
        self.ident_f = self.ar.f32(128)
        self.bmask_f = self.ar.f32(256)
        cx.dma("sp", [(self.ident_f, self.ident_d[:, :]), (self.bmask_f, self.bmask_d[:, :]),
                      (self.mlr_rep[:], self.mlr_rep_d[:, :]), (self.mlr_col[:], self.mlr_col_d[:, :])],
               writes=[r1])
        cx.op("dve", lambda h: h.tensor_copy(self.ident[:], self.ident_f), reads=[r1], writes=[rc])
        cx.op("dve", lambda h: h.tensor_copy(self.bmask[:], self.bmask_f), reads=[r1], writes=[rc])
        cx.op("dve", lambda h: h.tensor_scalar(out=self.bmask01[:], in0=self.bmask_f, scalar1=0.0, scalar2=None, op0=ALU.is_equal),
              reads=[r1], writes=[rc])
        cx.op("dve", lambda h: h.memset(self.ones[:], 1.0), writes=[rc])
        cx.op("dve", lambda h: h.memset(self.ones_f[:], 1.0), writes=[rc])
        for j in range(3):
            cx.op("dve", lambda h: h.tensor_copy(self.onesE[:, j, :], self.ones[:]), reads=[rc], writes=[rc])
        cx.op("dve", lambda h: h.tensor_scalar(out=self.onesE[0:64, 2, :], in0=self.onesE[0:64, 2, :],
                                               scalar1=self.mlr_rep[0:64, 1:2], scalar2=None, op0=ALU.mult), reads=[r1, rc], writes=[rc])
        cx.op("dve", lambda h: h.memset(self.onesE[64:128, 2, :], 0.0), writes=[rc])
        cx.op("dve", lambda h: h.tensor_scalar(out=self.onesE[0:64, 0, :], in0=self.onesE[0:64, 0, :],
                                               scalar1=self.mlr_rep[0:64, 0:1], scalar2=None, op0=ALU.mult), reads=[r1, rc], writes=[rc])
        cx.op("dve", lambda h: h.tensor_scalar(out=self.onesE[64:128, 1, :], in0=self.onesE[64:128, 1, :],
                                               scalar1=self.mlr_rep[64:128, 1:2], scalar2=None, op0=ALU.mult), reads=[r1, rc], writes=[rc])
        for j in range(2):
            cx.op("dve", lambda h: h.tensor_scalar(out=self.onesLR[:, j, :], in0=self.ones[0:64, :],
                                                   scalar1=self.mlr_rep[0:64, j:j + 1], scalar2=None,
                                                   op0=ALU.mult), reads=[r1, rc], writes=[rc])

    def layer_params(self, l):
        cx = self.cx
        cx.dma("sp", [
            (self.gqk[:, 0, :], self.g_aq[l:l + 1, :].partition_broadcast(128)),
            (self.gqk[:, 1, :], self.g_ak[l:l + 1, :].partition_broadcast(128)),
            (self.gout[:], self.g_out[l]),
            (self.cvp[:].rearrange("p a b -> p (a b)"), self.convp[l]),
        ], writes=[self.rLP])

    def load_gain(self, src_row):
        self.cx.dma("sp", [(self.grep[:], src_row.partition_broadcast(128))], writes=[self.rG])

    def norm_transpose(self, xin, rxin, np_, dstT_fn, rdst, tmp, defer=None):
        cx = self.cx
        junk, rj, ss, rss, hn, rhn, pbanks = tmp["junk"], tmp["rj"], tmp["ss"], tmp["rss"], tmp["hn"], tmp["rhn"], tmp["pb"]
        cx.op("act", lambda h: h.activation(out=junk[0:np_, :], in_=xin, func=AF.Square, accum_out=ss[0:np_, 0:1]),
              reads=[rxin], writes=[rj, rss])
        cx.op("act", lambda h: h.activation(out=ss[0:np_, 1:2], in_=ss[0:np_, 0:1], func=AF.Sqrt, scale=1.0 / D, bias=EPS),
              reads=[rss], writes=[rss])
        cx.op("dve", lambda h: h.reciprocal(out=ss[0:np_, 2:3], in_=ss[0:np_, 1:2]), reads=[rss], writes=[rss])
        if tmp.get("rowscale") is not None:
            rs = tmp["rowscale"]
            cx.op("dve", lambda h: h.tensor_scalar(out=ss[0:np_, 2:3], in0=ss[0:np_, 2:3], scalar1=rs, scalar2=None, op0=ALU.mult),
                  reads=[rss, self.rC], writes=[rss])
        cx.op("dve", lambda h: h.scalar_tensor_tensor(out=hn[0:np_, :], in0=xin, scalar=ss[0:np_, 2:3], in1=self.grep[0:np_, :],
                                                      op0=ALU.mult, op1=ALU.mult), reads=[rxin, rss, self.rG], writes=[rhn])
        def stage_b():
            for half in range(2):
                bi = pbanks[half]
                pb = self.ps[bi][:].bitcast(BF16)
                for k in range(8):
                    kt = half * 8 + k
                    cx.op("pe", lambda h: h.transpose(out=pb[:, k * 128:k * 128 + np_], in_=hn[0:np_, kt * 128:(kt + 1) * 128],
                                                      identity=self.ident[0:np_, 0:np_]),
                          reads=[rhn, self.rC], writes=[self.rP[bi]], inc=(k == 7))
                src = pb[:, 0:1024].rearrange("p (a b) -> p a b", b=128)[:, :, 0:np_]
                dst = dstT_fn(half * 8, 8)
                eng = "act" if half == 0 else "dve"
                if eng == "act":
                    cx.op("act", lambda h: h.activation(out=dst, in_=src, func=AF.Copy), reads=[self.rP[bi]], writes=[rdst])
                else:
                    cx.op("dve", lambda h: h.tensor_copy(dst, src), reads=[self.rP[bi]], writes=[rdst])
        if defer is None:
            stage_b()
        else:
            defer.append(stage_b)

    def norm_tmp(self):
        ar = self.ar
        t = {}
        t["junk"] = ar.bf(D)
        t["rj"] = self.sres("junk")
        t["ss"] = ar.f32(8)
        t["rss"] = self.sres("ss")
        return t

    def finish(self):
        cx = self.cx
        cx.barrier(cx.all_res)
        cx.wait_all("pool", cx.all_res)

    def tt_copy_T(self, src_bf, rsrc, nblk, w, dst_fn, rdst, bank, eng):
        cx = self.cx
        pb = self.ps[bank][:].bitcast(BF16)
        for j in range(nblk):
            cx.op("pe", lambda h: h.transpose(out=pb[0:w, j * 128:(j + 1) * 128], in_=src_bf[:, j, :], identity=self.ident[:]),
                  reads=[rsrc, self.rC], writes=[self.rP[bank]], inc=(j == nblk - 1))
        src = pb[0:w, 0:nblk * 128].rearrange("p (a b) -> p a b", b=128)
        if eng == "act":
            cx.op("act", lambda h: h.activation(out=dst_fn(), in_=src, func=AF.Copy), reads=[self.rP[bank]], writes=[rdst])
        else:
            cx.op("dve", lambda h: h.tensor_copy(dst_fn(), src), reads=[self.rP[bank]], writes=[rdst])

    def _defer_T(self, defer, src_bf, rsrc, nblk, w, dst_fn, rdst, bank, eng):
        qsi = self._qsi

        def run():
            saved = self._qsi
            self._qsi = qsi
            self.tt_copy_T(src_bf, rsrc, nblk, w, dst_fn, rdst, bank, eng)
            self._qsi = saved
        defer.append(run)

    def phase1(self, l):
        cx, ar = self.cx, self.ar
        self.phase_barrier()
        if l == 0:
            self.load_gain(self.g_attn[l:l + 1, :])
        t = self.norm_tmp()
        hns = [ar.bf(D) for _ in range(2)]
        rhn = [self.sres("hn%d" % i) for i in range(2)]
        for tt in range(NT if l == 0 else 0):
            t["hn"], t["rhn"] = hns[tt % 2], rhn[tt % 2]
            t["pb"] = (6, 7) if tt % 2 == 0 else (4, 5)
            self.norm_transpose(self.X[:, tt, :], self.rX[tt], 128,
                                lambda k0, n, tt=tt: self.actT[:, k0:k0 + n, tt * 128:(tt + 1) * 128],
                                self.rAT[tt], t)
        if self.stop == "p1a":
            return
        self.phase_barrier()
        tabs = [ar.f32(448) for _ in range(3)]
        rtab = [self.sres("tab%d" % i) for i in range(3)]
        gc = ar.f32(1024).rearrange("p (a b) -> p a b", b=512)
        rgc = self.sres("gc")
        cx.dma("sp", [(gc[:, 0, :], self.g_cq[l:l + 1, :].partition_broadcast(128)),
                      (gc[:, 1, :], self.g_ckv[l:l + 1, :].partition_broadcast(128))], writes=[rgc])
        stg = [ar.f32(512) for _ in range(3)]
        rstg = [self.sres("stg%d" % i) for i in range(3)]
        tmp = [ar.f32(512) for _ in range(3)]
        rtmp = [self.sres("tmp%d" % i) for i in range(3)]
        roped = [ar.bf(512) for _ in range(3)]
        rrop = [self.sres("roped%d" % i) for i in range(3)]
        st = ar.f32(64)
        rst = self.sres("st")
        qstage_l = [ar.bf(4096) for _ in range(2)]
        rqs_l = [self.sres("qstage%d" % i) for i in range(2)]
        ckvnT = ar.bf(4096).rearrange("p (a b) -> p a b", b=T)
        cqnT = ckvnT
        rckv = self.sres("ckvnT")
        rcq = rckv
        self._qsi = 0
        self._qsn = 0

        class _QS:
            def __getitem__(s_, key):
                return qstage_l[self._qsi][key]

            def rearrange(s_, *a, **k):
                return qstage_l[self._qsi].rearrange(*a, **k)
        qstage = _QS()

        class _QV:
            def __init__(s_, b):
                s_.b = b

            def __getitem__(s_, key):
                return qstage_l[self._qsi].rearrange("p (a b) -> p a b", b=s_.b)[key]
        qs4 = _QV(T)
        vs8 = _QV(512)

        class _RQ:
            pass

        def cur_rqs():
            return rqs_l[self._qsi]

        def next_stage():
            self._qsn += 1
            self._qsi = self._qsn % 2
            return self._qsi
        self._cnt = 0

        def region(key, sub, n):
            g, off = KVREG[key]
            return self.kvb[g].rearrange("p c -> (p c)")[off + sub:off + sub + n]

        def exchange(g):
            cx.collective(self.kvb[g][:, :], self.kvg[g][:, :], reads=[self.r_kvb[k] for k in KVGRP_KEYS[g]],
                          writes=[self.r_kvg[g]])

        def load_tab(tt):
            i = self._cnt % 3
            self._cnt += 1
            cx.dma("sp", [(tabs[i], self.tab_d[:, tt, :])], writes=[rtab[i]])
            return tabs[i], rtab[i]

        mm_banks = [0, 1, 2, 3]
        tr_banks = [4, 5, 6, 7]
        self._mb = 0
        self._tb = 0

        def next_mb():
            b = mm_banks[self._mb % 4]
            self._mb += 1
            return b

        def next_tb():
            b = tr_banks[self._tb % 4]
            self._tb += 1
            return b

        def rope_small(src3, rsrc_list, nh, w, ctab, stab, tb, rtb, out3, rout, i):
            hw = w // 2
            t1 = tmp[0][:, 0:nh * w].rearrange("p (a b) -> p a b", b=w)
            t2 = tmp[1][:, 0:nh * w].rearrange("p (a b) -> p a b", b=w)
            cb = tb[:, ctab:ctab + w].unsqueeze(1).to_broadcast([128, nh, w])
            s0 = tb[:, stab:stab + hw].unsqueeze(1).to_broadcast([128, nh, hw])
            s1 = tb[:, stab + hw:stab + w].unsqueeze(1).to_broadcast([128, nh, hw])
            cx.op("dve", lambda h: h.tensor_tensor(out=t1, in0=src3, in1=cb, op=ALU.mult),
                  reads=rsrc_list + [rtb], writes=[rtmp[0]])
            cx.op("dve", lambda h: h.tensor_tensor(out=t2[:, :, 0:hw], in0=src3[:, :, hw:w], in1=s0, op=ALU.mult),
                  reads=rsrc_list + [rtb], writes=[rtmp[1]])
            cx.op("dve", lambda h: h.tensor_tensor(out=t2[:, :, hw:w], in0=src3[:, :, 0:hw], in1=s1, op=ALU.mult),
                  reads=rsrc_list + [rtb], writes=[rtmp[1]])
            cx.op("dve", lambda h: h.tensor_tensor(out=out3, in0=t1, in1=t2, op=ALU.add),
                  reads=[rtmp[0], rtmp[1]], writes=[rout])

        def latent_norm(bank, tt, i, gidx, cT, rcT, defer):
            s_, rs_ = stg[i], rstg[i]
            cx.op("act", lambda h: h.activation(out=s_, in_=self.ps[bank][:, 0:512], func=AF.Copy),
                  reads=[self.rP[bank]], writes=[rs_])
            cx.op("dve", lambda h: h.tensor_tensor(out=tmp[0], in0=s_, in1=s_, op=ALU.mult), reads=[rs_], writes=[rtmp[0]])
            cx.op("dve", lambda h: h.tensor_reduce(out=st[:, 0:1], in_=tmp[0], axis=AX.X, op=ALU.add), reads=[rtmp[0]], writes=[rst])
            cx.op("act", lambda h: h.activation(out=st[:, 1:2], in_=st[:, 0:1], func=AF.Sqrt, scale=1.0 / 512, bias=EPS),
                  reads=[rst], writes=[rst])
            cx.op("dve", lambda h: h.reciprocal(out=st[:, 2:3], in_=st[:, 1:2]), reads=[rst], writes=[rst])
            cx.op("dve", lambda h: h.scalar_tensor_tensor(out=roped[i], in0=s_, scalar=st[:, 2:3], in1=gc[:, gidx, :],
                                                          op0=ALU.mult, op1=ALU.mult), reads=[rs_, rst, rgc], writes=[rrop[i]])
            self._defer_T(defer, roped[i].rearrange("p (a b) -> p a b", b=128), rrop[i], 4, 128,
                          lambda: cT[:, 0:4, tt * 128:(tt + 1) * 128], rcT, next_tb(), "act")

        cstate = {}
        self._pc = 0

        def chunk_mm(c0, ncol, kind, tt):
            if tt == 0:
                W, rWs = self.wget()
                cstate[kind] = (W[:, 0:16 * ncol].rearrange("p (k n) -> p k n", n=ncol), rWs, {}, [None])
            Wv, rWs, banks, _q = cstate[kind]
            bank = next_mb()
            banks[tt] = bank
            for kt in range(16):
                cx.op("pe", lambda h: h.matmul(self.ps[bank][:, 0:ncol], self.actT[:, kt, tt * 128:(tt + 1) * 128], Wv[:, kt, :],
                                               start=(kt == 0), stop=(kt == 15)),
                      reads=[self.rAT[tt], rWs], writes=[self.rP[bank]], inc=(kt == 15))

        def chunk_post(c0, ncol, kind, tt, defer):
            if True:
                i = self._pc % 3
                self._pc += 1
                bank = cstate[kind][2][tt]
                if cstate[kind][3][0] is None:
                    cstate[kind][3][0] = next_stage()
                self._qsi = cstate[kind][3][0]
                psb = self.ps[bank]
                rpb = self.rP[bank]
                if kind in ("av", "bv"):
                    cx.op("act", lambda h: h.activation(out=vs8[:, tt, 0:ncol], in_=psb[:, 0:ncol], func=AF.Copy),
                          reads=[rpb], writes=[cur_rqs()])
                elif kind == "ckv":
                    latent_norm(bank, tt, i, 1, ckvnT, rckv, defer)
                elif kind == "cq":
                    latent_norm(bank, tt, i, 0, cqnT, rcq, defer)
                elif kind == "ckr":
                    tb, rtb = load_tab(tt)
                    src3 = psb[:, 0:64].rearrange("p (a b) -> p a b", b=64)
                    out3 = roped[i][:, 0:64].rearrange("p (a b) -> p a b", b=64)
                    rope_small(src3, [rpb], 1, 64, TAB_CM, TAB_SM, tb, rtb, out3, rrop[i], i)
                    self._defer_T(defer, out3, rrop[i], 1, 64,
                                   lambda: qstage[0:64, tt * 128:(tt + 1) * 128].rearrange("p (a b) -> p a b", b=128),
                                   cur_rqs(), next_tb(), "act")
                elif kind in ("bk", "bq0", "bq1", "bq2"):
                    tb, rtb = load_tab(tt)
                    ps3 = psb[:, 0:512].rearrange("p (a b) -> p a b", b=128)
                    r3 = roped[i].rearrange("p (a b) -> p a b", b=128)
                    cx.op("act", lambda h: h.activation(out=r3, in_=ps3, func=AF.Copy), reads=[rpb], writes=[rrop[i]])
                    rope_small(ps3[:, :, 0:32], [rpb], 4, 32, TAB_CB, TAB_SB, tb, rtb, r3[:, :, 0:32], rrop[i], i)
                    self._defer_T(defer, r3, rrop[i], 4, 128, lambda: qs4[:, 0:4, tt * 128:(tt + 1) * 128], cur_rqs(), next_tb(), "act")
                elif kind in ("a0", "a1"):
                    tb, rtb = load_tab(tt)
                    s_, rs_ = stg[i], rstg[i]
                    s3 = s_.rearrange("p (a b) -> p a b", b=128)
                    t0 = tmp[0].rearrange("p (a b) -> p a b", b=128)
                    cx.op("act", lambda h: h.activation(out=s_, in_=psb[:, 0:512], func=AF.Copy), reads=[rpb], writes=[rs_])
                    cx.op("dve", lambda h: h.tensor_tensor(out=tmp[0], in0=s_, in1=s_, op=ALU.mult), reads=[rs_], writes=[rtmp[0]])
                    cx.op("dve", lambda h: h.tensor_reduce(out=st[:, 0:4], in_=t0, axis=AX.X, op=ALU.add), reads=[rtmp[0]], writes=[rst])
                    cx.op("act", lambda h: h.activation(out=st[:, 4:8], in_=st[:, 0:4], func=AF.Sqrt, scale=1.0 / 128, bias=EPS),
                          reads=[rst], writes=[rst])
                    cx.op("dve", lambda h: h.reciprocal(out=st[:, 8:12], in_=st[:, 4:8]), reads=[rst], writes=[rst])
                    cx.op("dve", lambda h: h.tensor_tensor(out=t0, in0=s3, in1=st[:, 8:12].unsqueeze(2).to_broadcast([128, 4, 128]),
                                                           op=ALU.mult), reads=[rs_, rst], writes=[rtmp[0]])
                    x3 = tmp[2].rearrange("p (a b) -> p a b", b=128)
                    if kind == "a0":
                        cx.op("dve", lambda h: h.tensor_tensor(out=x3, in0=t0, in1=self.gqk[:, 0, :].unsqueeze(1).to_broadcast([128, 4, 128]),
                                                               op=ALU.mult), reads=[rtmp[0], self.rLP], writes=[rtmp[2]])
                    else:
                        for jj in range(2):
                            cx.op("dve", lambda h: h.tensor_tensor(out=x3[:, 2 * jj:2 * jj + 2, :], in0=t0[:, 2 * jj:2 * jj + 2, :],
                                                                   in1=self.gqk[:, jj, :].unsqueeze(1).to_broadcast([128, 2, 128]),
                                                                   op=ALU.mult), reads=[rtmp[0], self.rLP], writes=[rtmp[2]])
                    t1 = tmp[0].rearrange("p (a b) -> p a b", b=128)
                    cx.op("dve", lambda h: h.tensor_tensor(out=t1, in0=x3, in1=tb[:, TAB_CA:TAB_CA + 128].unsqueeze(1).to_broadcast([128, 4, 128]),
                                                           op=ALU.mult), reads=[rtmp[2], rtb], writes=[rtmp[0]])
                    x5 = tmp[2].rearrange("p (h f q e) -> p h f q e", h=4, f=2, q=2)
                    t5 = tmp[1].rearrange("p (h f q e) -> p h f q e", h=4, f=2, q=2)
                    sa = tb[:, TAB_SA:TAB_SA + 128].rearrange("p (f q e) -> p f q e", f=2, q=2)
                    for q in range(2):
                        sab = sa[:, :, q, :].unsqueeze(1).to_broadcast([128, 4, 2, 32])
                        cx.op("dve", lambda h: h.tensor_tensor(out=t5[:, :, :, q, :], in0=x5[:, :, :, 1 - q, :], in1=sab, op=ALU.mult),
                              reads=[rtmp[2], rtb], writes=[rtmp[1]])
                    r3 = roped[i].rearrange("p (a b) -> p a b", b=128)
                    cx.op("dve", lambda h: h.tensor_tensor(out=roped[i], in0=tmp[0], in1=tmp[1], op=ALU.add),
                          reads=[rtmp[0], rtmp[1]], writes=[rrop[i]])
                    self._defer_T(defer, r3, rrop[i], 4, 128, lambda: qs4[:, 0:4, tt * 128:(tt + 1) * 128], cur_rqs(), next_tb(), "act")
                else:
                    raise ValueError(kind)
        def chunk_store(c0, ncol, kind):
            self._qsi = cstate[kind][3][0]
            if kind == "av":
                cx.dma("sp", [(region("va", 0, T * 256).rearrange("(tt p c) -> p tt c", p=128, c=256), vs8[:, :, 0:256])],
                       reads=[cur_rqs()], writes=[self.r_kvb["va"]])
            elif kind == "bv":
                cx.dma("sp", [(region("vb", 0, T * 512).rearrange("(tt p c) -> p tt c", p=128, c=512), vs8[:, :, 0:512])],
                       reads=[cur_rqs()], writes=[self.r_kvb["vb"]])
            elif kind == "ckr":
                cx.dma("sp", [(region("kr", 0, 64 * T).rearrange("(p t) -> p t", t=T), qstage[0:64, 0:T])],
                       reads=[cur_rqs()], writes=[self.r_kvb["kr"]])
            elif kind == "bk":
                cx.dma("sp", [(region("kb", 0, 4 * 128 * T).rearrange("(h p t) -> p h t", p=128, t=T), qs4[:, 0:4, :])],
                       reads=[cur_rqs()], writes=[self.r_kvb["kb"]])
            elif kind in ("bq0", "bq1", "bq2"):
                g = int(kind[2])
                cx.dma("sp", [(self.qb_d[4 * g:4 * g + 4].rearrange("h p t -> p h t"), qs4[:, 0:4, :])],
                       reads=[cur_rqs()], writes=[self.r_qd["qb"]])
            elif kind == "a0":
                cx.dma("sp", [(self.qa_d[0:4].rearrange("h p t -> p h t"), qs4[:, 0:4, :])],
                       reads=[cur_rqs()], writes=[self.r_qd["qa"]])
            elif kind == "a1":
                cx.dma("sp", [(self.qa_d[4:6].rearrange("h p t -> p h t"), qs4[:, 0:2, :])],
                       reads=[cur_rqs()], writes=[self.r_qd["qa"]])
                cx.dma("sp", [(region("ka", 0, 2 * 128 * T).rearrange("(h p t) -> p h t", p=128, t=T), qs4[:, 2:4, :])],
                       reads=[cur_rqs()], writes=[self.r_kvb["ka"]])

        def latent_kv_up():
            W, rWs = self.wget()
            Wv = W[:, 0:4 * 1536].rearrange("p (k n) -> p k n", n=1536)
            for hg, heads in enumerate(((0, 1, 2, 3), (4, 5))):
                next_stage()
                for j, hd in enumerate(heads):
                    for c2 in range(2):
                        bank = next_mb()
                        for kt in range(4):
                            cx.op("pe", lambda h: h.matmul(self.ps[bank][:, 0:512], Wv[:, kt, hd * 256:hd * 256 + 128],
                                                           ckvnT[:, kt, c2 * 512:(c2 + 1) * 512], start=(kt == 0), stop=(kt == 3)),
                                  reads=[rckv, rWs], writes=[self.rP[bank]], inc=(kt == 3))
                        cx.op("act", lambda h: h.activation(out=qs4[:, j, c2 * 512:(c2 + 1) * 512], in_=self.ps[bank][:, 0:512], func=AF.Copy),
                              reads=[self.rP[bank]], writes=[cur_rqs()])
                nh = len(heads)
                cx.dma("sp", [(region("kc", hg * 4 * 128 * T, nh * 128 * T).rearrange("(h p t) -> p h t", p=128, t=T), qs4[:, 0:nh, :])],
                       reads=[cur_rqs()], writes=[self.r_kvb["kc"]])
            exchange(1)
            W6 = Wv.rearrange("p k (h c) -> p k h c", c=256)
            vc_reg = region("vc", 0, T * 768).rearrange("(tt p c) -> p tt c", p=128, c=768)
            for half in range(2):
                next_stage()
                vs = qstage[:, 0:8 * 384].rearrange("p (a b) -> p a b", b=384)
                for tt in range(NT):
                    bank = next_mb()
                    for kt in range(4):
                        cx.op("pe", lambda h: h.matmul(self.ps[bank][:, 0:384].rearrange("p (a b) -> p a b", b=128),
                                                       ckvnT[:, kt, tt * 128:(tt + 1) * 128],
                                                       W6[:, kt, 3 * half:3 * half + 3, 128:256], start=(kt == 0), stop=(kt == 3)),
                              reads=[rckv, rWs], writes=[self.rP[bank]], inc=(kt == 3))
                    cx.op("act", lambda h: h.activation(out=vs[:, tt, :], in_=self.ps[bank][:, 0:384], func=AF.Copy),
                          reads=[self.rP[bank]], writes=[cur_rqs()])
                cx.dma("sp", [(vc_reg[:, :, half * 384:(half + 1) * 384], vs)], reads=[cur_rqs()], writes=[self.r_kvb["vc"]])
            exchange(3)

        def latent_q_up():
            W, rWs = self.wget()
            Wv = W[:, 0:4 * 1152].rearrange("p (k n) -> p k n", n=1152)
            for hg, heads in enumerate(((0, 1, 2, 3), (4, 5))):
                next_stage()
                for j, hd in enumerate(heads):
                    for c2 in range(2):
                        bank = next_mb()
                        for kt in range(4):
                            cx.op("pe", lambda h: h.matmul(self.ps[bank][:, 0:512], Wv[:, kt, hd * 192:hd * 192 + 128],
                                                           cqnT[:, kt, c2 * 512:(c2 + 1) * 512], start=(kt == 0), stop=(kt == 3)),
                                  reads=[rcq, rWs], writes=[self.rP[bank]], inc=(kt == 3))
                        cx.op("act", lambda h: h.activation(out=qs4[:, j, c2 * 512:(c2 + 1) * 512], in_=self.ps[bank][:, 0:512], func=AF.Copy),
                              reads=[self.rP[bank]], writes=[cur_rqs()])
                nh = len(heads)
                cx.dma("sp", [(self.qcn_d[4 * hg:4 * hg + nh].rearrange("h p t -> p h t"), qs4[:, 0:nh, :])],
                       reads=[cur_rqs()], writes=[self.r_qd["qcn"]])
            W6 = Wv.rearrange("p k (h c) -> p k h c", c=192)
            for tt in range(NT):
                i = tt % 3
                tb, rtb = load_tab(tt)
                bank = next_mb()
                for kt in range(4):
                    cx.op("pe", lambda h: h.matmul(self.ps[bank][:, 0:384].rearrange("p (a b) -> p a b", b=64),
                                                   cqnT[:, kt, tt * 128:(tt + 1) * 128], W6[:, kt, :, 128:192],
                                                   start=(kt == 0), stop=(kt == 3)),
                          reads=[rcq, rWs], writes=[self.rP[bank]], inc=(kt == 3))
                src3 = self.ps[bank][:, 0:384].rearrange("p (a b) -> p a b", b=64)
                out3 = roped[i][:, 0:384].rearrange("p (a b) -> p a b", b=64)
                rope_small(src3, [self.rP[bank]], 6, 64, TAB_CM, TAB_SM, tb, rtb, out3, rrop[i], i)
                self.tt_copy_T(out3, rrop[i], 6, 64, lambda: self.actT[0:64, 0:6, tt * 128:(tt + 1) * 128],
                               self.rAT[tt], next_tb(), "act")
            cx.dma("sp", [(self.qcr_d.rearrange("h p t -> p h t"), self.actT[0:64, 0:6, :])],
                   reads=self.rAT, writes=[self.r_qd["qcr"]])

        lim = getattr(self, "p1_limit", None)
        items = []
        for ci_, (c0, ncol, kind) in enumerate(IN_CHUNKS):
            if lim is not None and ci_ >= lim:
                break
            for tt in range(NT):
                items.append((c0, ncol, kind, tt))

        def after_chunk(kind):
            if kind == "av":
                exchange(2)
            if kind == "a1":
                exchange(0)
                latent_kv_up()

        pendA = []
        pendB = []

        def do_A(it):
            d_ = []
            chunk_post(*it, d_)
            pendB.append((it, d_))

        def do_B():
            it, d_ = pendB.pop(0)
            for f in d_:
                f()
            if it[3] == NT - 1:
                chunk_store(it[0], it[1], it[2])
                after_chunk(it[2])

        def drain():
            while pendA:
                do_A(pendA.pop(0))
            while pendB:
                do_B()

        for it in items:
            if pendA and pendA[-1][2] == "a1" and it[2] != "a1":
                drain()
            chunk_mm(*it)
            pendA.append(it)
            while len(pendA) > 1:
                do_A(pendA.pop(0))
            while len(pendB) > 1:
                do_B()
        drain()
        if lim is not None:
            return
        latent_q_up()
        if "kv" in self.dbg:
            for g in range(4):
                r = cx.res("kvdbg%d" % g)
                cx.dma("sp", [(self.kvdbg[g][:, :], self.kvg[g][:, :])], reads=[self.r_kvg[g]], writes=[r])

    def kv_region(self, rank, key, sub, n):
        g, off = KVREG[key]
        off += sub
        if rank is None:
            return self.kvb[g].rearrange("p c -> (p c)")[off:off + n]
        base = rank * 128 * KVG_COLS[g]
        return self.kvg[g].rearrange("p c -> (p c)")[base + off:base + off + n]

    def rkvg(self, key):
        return self.r_kvg[KVREG[key][0]]

    def group_norm_out(self, stat_bank, width, heads_ft, on_fn, ron, c2, tmpA, rtmpA):
        cx = self.cx
        lnb, rstd = tmpA
        rlnb, rrstd = rtmpA
        cx.op("act", lambda h: h.activation(out=lnb, in_=self.ps[stat_bank][:, 0:512], func=AF.Ln, scale=1.0 / width, bias=EPS),
              reads=[self.rP[stat_bank]], writes=[rlnb])
        cx.op("act", lambda h: h.activation(out=rstd, in_=lnb, func=AF.Exp, scale=-0.5), reads=[rlnb], writes=[rrstd])
        for (hh, ft) in heads_ft:
            cx.op("dve", lambda h: h.scalar_tensor_tensor(out=self.actT[:, ft, c2 * 512:(c2 + 1) * 512], in0=on_fn(hh),
                                                          scalar=self.gout[:, ft:ft + 1], in1=rstd, op0=ALU.mult, op1=ALU.mult),
                  reads=[ron, rrstd, self.rLP], writes=self.rAT[4 * c2:4 * c2 + 4])

    def phase2(self, l):
        cx, ar = self.cx, self.ar
        self.phase_barrier()
        KT = [ar.bf(2048) for _ in range(2)]
        rKT = [self.sres("KT%d" % i) for i in range(2)]
        KrT = ar.bf(2048)
        rKr = self.sres("KrT")
        V = [ar.bf(2048).rearrange("p (a b) -> p a b", b=128) for _ in range(2)]
        rV = [self.sres("V%d" % i) for i in range(2)]
        Qn = [ar.bf(512) for _ in range(2)]
        Qr = [ar.bf(512) for _ in range(2)]
        rQ = [self.sres("Q%d" % i) for i in range(2)]
        NPT = 3
        PT = [ar.bf(512) for _ in range(NPT)]
        rPT = [self.sres("PT%d" % i) for i in range(NPT)]
        accP = [ar.f32(512) for _ in range(2)]
        raccP = [self.sres("accP%d" % i) for i in range(2)]
        onb = ar.f32(3072).rearrange("p (a b) -> p a b", b=512)
        ron = self.sres("onb")
        lnb = ar.f32(512)
        rden = ar.f32(512)
        rln = self.sres("lnb")
        rrd = self.sres("rden")
        sq = [ar.bf(512) for _ in range(2)]
        rsq = [self.sres("sq%d" % i) for i in range(2)]
        S_B = [0, 1, 2]
        O_B = [3, 4]
        U_B = [5, 6]
        STAT = 7
        rzp = self.sres("zpad")
        cx.op("dve", lambda h: h.memset(KrT[64:128, :], 0.0), writes=[rKr])
        for i_ in range(2):
            cx.op("dve", lambda h: h.memset(Qr[i_][64:128, :], 0.0), writes=[rQ[i_]])
        self._hc = 0
        self._kvc = 0
        pending = []

        def flush_stat(upto=3):
            for item in pending:
                while item and (3 - len(item)) < upto:
                    item.pop(0)()
            while pending and not pending[0]:
                pending.pop(0)

        def dense_core(hh, nh_in_mixer, first, last, KTb, rKTb, Vb, rVb, qi, scale, rope):
            k = self._hc
            self._hc += 1
            ob, ub = O_B[k % 2], U_B[k % 2]

            def emit_S(j):
                sbk = S_B[j % 3]
                cx.op("pe", lambda h: h.matmul(self.ps[sbk][:, 0:512], KTb[:, j * 128:(j + 1) * 128], Qn[qi][:, 0:512],
                                               start=True, stop=(not rope)),
                      reads=[rKTb, rQ[qi]], writes=[self.rP[sbk]], inc=(not rope))
                if rope:
                    cx.op("pe", lambda h: h.matmul(self.ps[sbk][:, 0:512], KrT[:, j * 128:(j + 1) * 128], Qr[qi][:, 0:512],
                                                   start=False, stop=True),
                          reads=[rKr, rQ[qi]], writes=[self.rP[sbk]])
            emit_S(0)
            emit_S(1)
            accp, raccp = accP[k % 2], raccP[k % 2]
            for j in range(16):
                sbk = S_B[j % 3]
                p = (k * 16 + j) % NPT
                cx.op("act", lambda h: h.activation(out=PT[p], in_=self.ps[sbk][:, 0:512], func=AF.Exp, scale=scale),
                      reads=[self.rP[sbk]], writes=[rPT[p]])
                if j + 2 < 16:
                    emit_S(j + 2)
                cx.op("pe", lambda h: h.matmul(self.ps[ob][:, 0:512], Vb[:, j, :], PT[p], start=(j == 0), stop=(j == 15)),
                      reads=[rVb, rPT[p]], writes=[self.rP[ob]], inc=True)
                if j % 2 == 0:
                    if j == 0:
                        cx.op("dve", lambda h: h.tensor_copy(accp, PT[p]), reads=[rPT[p]], writes=[raccp])
                    else:
                        cx.op("dve", lambda h: h.tensor_tensor(out=accp, in0=PT[p], in1=accp, op=ALU.add), reads=[rPT[p]], writes=[raccp])
                else:
                    cx.op("pe", lambda h: h.matmul(self.ps[ub][:, 0:512], self.ones[:], PT[p], start=(j == 1), stop=False),
                          reads=[self.rC, rPT[p]], writes=[self.rP[ub]], inc=True)
                if j == 3:
                    flush_stat(1)
                elif j == 5:
                    flush_stat(2)
                elif j == 9:
                    flush_stat(3)
            s = k % 2

            def st1(ub=ub, accp=accp, raccp=raccp):
                cx.op("pe", lambda h: h.matmul(self.ps[ub][:, 0:512], self.ones_f[:], accp, start=False, stop=True),
                      reads=[self.rC, raccp], writes=[self.rP[ub]], inc=True)

            def st2(hh=hh, ob=ob, ub=ub):
                cx.op("act", lambda h: h.activation(out=lnb, in_=self.ps[ub][:, 0:512], func=AF.Ln), reads=[self.rP[ub]], writes=[rln])
                cx.op("act", lambda h: h.activation(out=rden, in_=lnb, func=AF.Exp, scale=-1.0), reads=[rln], writes=[rrd])
                cx.op("dve", lambda h: h.tensor_tensor(out=onb[:, hh, :], in0=self.ps[ob][:, 0:512], in1=rden, op=ALU.mult),
                      reads=[self.rP[ob], rrd], writes=[ron])

            def st3(hh=hh, s=s, first=first, last=last):
                cx.op("dve", lambda h: h.tensor_tensor(out=sq[s], in0=onb[:, hh, :], in1=onb[:, hh, :], op=ALU.mult), reads=[ron], writes=[rsq[s]])
                cx.op("pe", lambda h: h.matmul(self.ps[STAT][:, 0:512], self.ones[:], sq[s], start=first, stop=last),
                      reads=[self.rC, rsq[s]], writes=[self.rP[STAT]], inc=True)
            pending.append([st1, st2, st3])

        def load_kv(which, hd):
            i = self._kvc % 2
            self._kvc += 1
            kk = "ka" if which == "a" else "kc"
            vk, wv = ("va", 256) if which == "a" else ("vc", 768)
            pk = []
            pv = []
            for r in range(2):
                ksrc = self.kv_region(r, kk, hd * 128 * T, 128 * T).rearrange("(p t) -> p t", t=T)
                pk.append((KT[i][:, r * T:(r + 1) * T], ksrc))
                vsrc = self.kv_region(r, vk, 0, T * wv).rearrange("(tt p c) -> p tt c", p=128, c=wv)[:, :, hd * 128:(hd + 1) * 128]
                pv.append((V[i][:, r * 8:(r + 1) * 8, :], vsrc))
            cx.dma("sp", pk, reads=[self.rkvg(kk)], writes=[rKT[i]])
            cx.dma("sp", pv, reads=[self.rkvg(vk)], writes=[rV[i]])
            return KT[i], rKT[i], V[i], rV[i]

        self._qc = 0

        def load_q(which, hd, c2):
            i = self._qc % 2
            self._qc += 1
            if which == "a":
                cx.dma("sp", [(Qn[i], self.qa_d[hd, :, c2 * 512:(c2 + 1) * 512])], reads=[self.r_qd["qa"]], writes=[rQ[i]])
            else:
                cx.dma("sp", [(Qn[i], self.qcn_d[hd, :, c2 * 512:(c2 + 1) * 512]),
                              (Qr[i][0:64, :], self.qcr_d[hd, :, c2 * 512:(c2 + 1) * 512])],
                       reads=[self.r_qd["qcn"], self.r_qd["qcr"]], writes=[rQ[i]])
            return i

        if "skipA" not in self.dbg:
            sc_a = 128.0 ** -0.5
            for c2 in range(2):
                for g in range(2):
                    KTb, rKTb, Vb, rVb = load_kv("a", g)
                    for hh in range(3 * g, 3 * g + 3):
                        qi = load_q("a", hh, c2)
                        dense_core(hh, 6, hh == 0, hh == 5, KTb, rKTb, Vb, rVb, qi, sc_a, False)
                flush_stat()
                self.group_norm_out(STAT, 768, [(hh, hh) for hh in range(6)], lambda hh: onb[:, hh, :], ron, c2,
                                    (lnb, rden), (rln, rrd))
        if "skipC" not in self.dbg:
            sc_c = 192.0 ** -0.5
            pk = []
            for r in range(2):
                pk.append((KrT[0:64, r * T:(r + 1) * T], self.kv_region(r, "kr", 0, 64 * T).rearrange("(p t) -> p t", t=T)))
            cx.dma("sp", pk, reads=[self.rkvg("kr")], writes=[rKr])
            for c2 in range(2):
                for hh in range(6):
                    KTb, rKTb, Vb, rVb = load_kv("c", hh)
                    qi = load_q("c", hh, c2)
                    dense_core(hh, 6, hh == 0, hh == 5, KTb, rKTb, Vb, rVb, qi, sc_c, True)
                flush_stat()
                self.group_norm_out(STAT, 768, [(hh, 10 + hh) for hh in range(6)], lambda hh: onb[:, hh, :], ron, c2,
                                    (lnb, rden), (rln, rrd))
        if "skipB" not in self.dbg:
            self.mixer_b(l)
        if "y" in self.dbg:
            r = cx.res("ydbg")
            cx.dma("sp", [(self.ydbg.rearrange("f p t -> p f t"), self.actT[:, :, :])], reads=self.rAT, writes=[r])

    def mixer_b(self, l):
        cx, ar = self.cx, self.ar
        self.phase_barrier()
        kte_w = ar.f32(1536)
        KTe = kte_w.bitcast(BF16)
        rKTe = self.sres("KTe")
        Vt_l = [ar.bf(32 * 128) for _ in range(2)]
        rVt_l = [self.sres("Vt%d" % i) for i in range(2)]
        Qb = [ar.bf(1024) for _ in range(2)]
        rQb = [self.sres("Qb%d" % i) for i in range(2)]
        Pm = [ar.bf(512) for _ in range(2)]
        rPm = [self.sres("Pm%d" % i) for i in range(2)]
        accO = ar.f32(4096).rearrange("p (a b) -> p a b", b=T)
        raccO = [self.sres("accO%d" % i) for i in range(4)]
        accS = ar.f32(1024)
        raccS = self.sres("accS")
        lnb = kte_w[:, 0:512]
        rstd = kte_w[:, 512:1024]
        rln = self.sres("lnbB")
        rrs = self.sres("rstdB")
        sq = [ar.bf(512) for _ in range(2)]
        rsq = [self.sres("sqB%d" % i) for i in range(2)]
        S_B = [0, 1]
        O_B = [2, 3]
        U_B = [4, 5]
        STAT = [6, 7]
        scale = 128.0 ** -0.5
        mb1 = self.bmask[:, 0:128]
        mb2 = self.bmask[:, 128:256]
        nqc = 0
        cnt = {"sb": 0, "bg": 0}
        for hd in range(4):
            pk = [(KTe[:, 0:T], self.kv_region(0, "kb", hd * 128 * T, 128 * T).rearrange("(p t) -> p t", t=T)),
                  (KTe[:, T:2 * T], self.kv_region(None, "kb", hd * 128 * T, 128 * T).rearrange("(p t) -> p t", t=T)),
                  (KTe[:, 2 * T:3 * T], self.kv_region(1, "kb", hd * 128 * T, 128 * T).rearrange("(p t) -> p t", t=T))]
            cx.dma("sp", pk, reads=[self.rkvg("kb"), self.r_kvb["kb"]], writes=[rKTe])
            hc = slice(hd * 128, (hd + 1) * 128)
            for g, d in enumerate((1, 4, 16)):
                Lh = T // d
                nqb = Lh // 64
                QN = min(128, Lh)
                nqt = Lh // QN
                ntile = (nqb + 3) // 2
                qi = nqc % 2
                Vt, rVt = Vt_l[nqc % 2], rVt_l[nqc % 2]
                nqc += 1
                cx.dma("sp", [(Qb[qi], self.qb_d[g * 4 + hd])], reads=[self.r_qd["qb"]], writes=[rQb[qi]])
                Vt4 = Vt[:, 0:d * ntile * 128].rearrange("p (r u c) -> p r u c", r=d, u=ntile)
                vG0 = self.kv_region(0, "vb", 0, T * 512).rearrange("(t c) -> t c", c=512)
                vG1 = self.kv_region(1, "vb", 0, T * 512).rearrange("(t c) -> t c", c=512)
                vOwn = self.kv_region(None, "vb", 0, T * 512)
                mr = nqb + 1
                er, ur = mr % 2, mr // 2
                pairs = [(Vt4[0:64, :, 0, :], vG0[T - 64 * d:T, :].rearrange("(p r) c -> p r c", r=d)[:, :, hc]),
                         (Vt4[64 * er:64 * er + 64, :, ur, :], vG1[0:64 * d, :].rearrange("(p r) c -> p r c", r=d)[:, :, hc])]
                if d < 16:
                    vo = vOwn.rearrange("(u two p r c) -> two p r u c", two=2, p=64, r=d, c=512)
                    for rr in range(d):
                        pairs.append((Vt4[64:128, rr, 0:nqb // 2, :], vo[0, :, rr, :, hc]))
                        pairs.append((Vt4[0:64, rr, 1:nqb // 2 + 1, :], vo[1, :, rr, :, hc]))
                else:
                    pairs.append((Vt4[64:128, :, 0, :], vOwn.rearrange("(p r c) -> p r c", r=d, c=512)[:, :, hc]))
                cx.dma("sp", pairs, reads=[self.rkvg("vb"), self.r_kvb["vb"]], writes=[rVt])
                cx.op("dve", lambda h: h.tensor_scalar(out=Vt4[0:64, :, 0, :], in0=Vt4[0:64, :, 0, :],
                                                       scalar1=self.mlr_rep[0:64, 0:1], scalar2=None, op0=ALU.mult),
                      reads=[rVt, self.rC], writes=[rVt])
                cx.op("dve", lambda h: h.tensor_scalar(out=Vt4[64 * er:64 * er + 64, :, ur, :], in0=Vt4[64 * er:64 * er + 64, :, ur, :],
                                                       scalar1=self.mlr_rep[64 * er:64 * er + 64, 1:2], scalar2=None, op0=ALU.mult),
                      reads=[rVt, self.rC], writes=[rVt])
                if d == 16:
                    cx.op("dve", lambda h: h.memset(Vt4[64:128, :, 1, :], 0.0), writes=[rVt])
                qtiles = [(r, qt) for r in range(d) for qt in range(nqt)]
                nb = 256 // QN
                batches = [qtiles[i:i + nb] for i in range(0, len(qtiles), nb)]
                K2 = 128 if d < 16 else 64

                def ksl(r, u, n):
                    us = T + r + d * (128 * u - 64)
                    return KTe[:, us:us + (n - 1) * d + 1:d]

                def emit_S(bi):
                    sbk = S_B[cnt["sb"] % 2]
                    cnt["sb"] += 1
                    batch = batches[bi]
                    for t_, (r, qt) in enumerate(batch):
                        qs = r + d * QN * qt
                        qsl = Qb[qi][:, qs:qs + (QN - 1) * d + 1:d]
                        c1 = t_ * QN
                        c2 = nb * QN + t_ * QN
                        cx.op("pe", lambda h: h.matmul(self.ps[sbk][:, c1:c1 + QN], ksl(r, qt, 128), qsl, start=True, stop=True),
                              reads=[rKTe, rQb[qi]], writes=[self.rP[sbk]], inc=False)
                        cx.op("pe", lambda h: h.matmul(self.ps[sbk][0:K2, c2:c2 + QN], ksl(r, qt + 1, K2), qsl, start=True, stop=True),
                              reads=[rKTe, rQb[qi]], writes=[self.rP[sbk]], inc=(t_ == len(batch) - 1))
                    return sbk

                def emit_PV(bi, sbk, k_):
                    batch = batches[bi]
                    p_, rp_ = Pm[k_ % 2], rPm[k_ % 2]
                    if K2 == 128:
                        cx.op("act", lambda h: h.activation(out=p_, in_=self.ps[sbk][:, 0:512], func=AF.Exp, scale=scale),
                              reads=[self.rP[sbk]], writes=[rp_])
                    else:
                        cx.op("dve", lambda h: h.memset(p_[64:128, 256:512], 0.0), writes=[rp_])
                        cx.op("act", lambda h: h.activation(out=p_[:, 0:256], in_=self.ps[sbk][:, 0:256], func=AF.Exp, scale=scale),
                              reads=[self.rP[sbk]], writes=[rp_])
                        cx.op("act", lambda h: h.activation(out=p_[0:64, 256:512], in_=self.ps[sbk][0:64, 256:512], func=AF.Exp, scale=scale),
                              reads=[self.rP[sbk]], writes=[rp_])
                    m4 = self.bmask01[:].rearrange("p (m q) -> p m q", m=2)[:, :, 0:QN].unsqueeze(2).to_broadcast([128, 2, nb, QN])
                    p4 = p_.rearrange("p (m t q) -> p m t q", m=2, t=nb)
                    cx.op("dve", lambda h: h.tensor_tensor(out=p4, in0=p4, in1=m4, op=ALU.mult), reads=[self.rC], writes=[rp_])
                    fpos = (bi * 256) % 512
                    if fpos == 0:
                        cnt["bg"] += 1
                    ob, ub = O_B[cnt["bg"] % 2], U_B[cnt["bg"] % 2]
                    for t_, (r, qt) in enumerate(batch):
                        oc = fpos + t_ * QN
                        c1 = t_ * QN
                        c2 = nb * QN + t_ * QN
                        cx.op("pe", lambda h: h.matmul(self.ps[ob][:, oc:oc + QN], Vt4[:, r, qt, :], p_[:, c1:c1 + QN], start=True, stop=False),
                              reads=[rVt, rp_], writes=[self.rP[ob]], inc=False)
                        cx.op("pe", lambda h: h.matmul(self.ps[ob][:, oc:oc + QN], Vt4[:, r, qt + 1, :], p_[:, c2:c2 + QN],
                                                       start=False, stop=True),
                              reads=[rVt, rp_], writes=[self.rP[ob]], inc=False)
                        o1 = self.onesE[:, 0, :] if qt == 0 else self.ones[:]
                        if qt + 1 == ntile - 1:
                            o2 = self.onesE[:, 1, :] if K2 == 128 else self.onesE[:, 2, :]
                        else:
                            o2 = self.ones[:, :]
                        cx.op("pe", lambda h: h.matmul(self.ps[ub][:, oc:oc + QN], o1, p_[:, c1:c1 + QN], start=True, stop=False),
                              reads=[self.rC, rp_], writes=[self.rP[ub]], inc=False)
                        cx.op("pe", lambda h: h.matmul(self.ps[ub][:, oc:oc + QN], o2, p_[:, c2:c2 + QN], start=False, stop=True),
                              reads=[self.rC, rp_], writes=[self.rP[ub]], inc=True)
                    if fpos == 256:
                        evac(bi, ob, ub)

                def evac(bi, ob, ub):
                    f0 = (bi * 256) - 256
                    if d == 1:
                        ov = accO[:, hd, f0:f0 + 512]
                        sv = accS[:, f0:f0 + 512]
                        pso = self.ps[ob][:, 0:512]
                        psu = self.ps[ub][:, 0:512]
                    else:
                        rA = f0 // Lh
                        nres_b = 512 // Lh
                        ov = accO[:, hd, :].rearrange("p (i r) -> p r i", r=d)[:, rA:rA + nres_b, :]
                        sv = accS.rearrange("p (i r) -> p r i", r=d)[:, rA:rA + nres_b, :]
                        pso = self.ps[ob][:, 0:512].rearrange("p (r i) -> p r i", i=Lh)
                        psu = self.ps[ub][:, 0:512].rearrange("p (r i) -> p r i", i=Lh)
                    if g == 0:
                        cx.op("dve", lambda h: h.tensor_copy(ov, pso), reads=[self.rP[ob]], writes=[raccO[hd]])
                        cx.op("act", lambda h: h.activation(out=sv, in_=psu, func=AF.Copy), reads=[self.rP[ub]], writes=[raccS])
                    else:
                        cx.op("dve", lambda h: h.tensor_tensor(out=ov, in0=pso, in1=ov, op=ALU.add),
                              reads=[self.rP[ob]], writes=[raccO[hd]])
                        cx.op("dve", lambda h: h.tensor_tensor(out=sv, in0=psu, in1=sv, op=ALU.add),
                              reads=[self.rP[ub]], writes=[raccS])

                sb_next = emit_S(0)
                for bi in range(len(batches)):
                    sb_cur = sb_next
                    if bi + 1 < len(batches):
                        sb_next = emit_S(bi + 1)
                    emit_PV(bi, sb_cur, cnt["sb"] + bi)
            for c2 in range(2):
                cs = slice(c2 * 512, (c2 + 1) * 512)
                cx.op("act", lambda h: h.activation(out=accS[:, cs], in_=accS[:, cs], func=AF.Ln), reads=[raccS], writes=[raccS])
                cx.op("act", lambda h: h.activation(out=accS[:, cs], in_=accS[:, cs], func=AF.Exp, scale=-1.0), reads=[raccS], writes=[raccS])
                cx.op("dve", lambda h: h.tensor_tensor(out=accO[:, hd, cs], in0=accO[:, hd, cs], in1=accS[:, cs], op=ALU.mult),
                      reads=[raccS], writes=[raccO[hd]])
                s_ = (hd * 2 + c2) % 2
                cx.op("act", lambda h: h.activation(out=sq[s_], in_=accO[:, hd, cs], func=AF.Square), reads=[raccO[hd]], writes=[rsq[s_]])
                cx.op("pe", lambda h: h.matmul(self.ps[STAT[c2]][:, 0:512], self.ones[:], sq[s_], start=(hd == 0), stop=(hd == 3)),
                      reads=[self.rC, rsq[s_]], writes=[self.rP[STAT[c2]]], inc=True)
        ronall = self.sres("accO_all")
        cx.wait_all("dve", raccO)
        cx.wait_all("act", [rKTe])
        for c2 in range(2):
            self.group_norm_out(STAT[c2], 512, [(hh, 6 + hh) for hh in range(4)],
                                lambda hh: accO[:, hh, c2 * 512:(c2 + 1) * 512], ronall, c2, (lnb, rstd), (rln, rrs))

    def phase3(self, l):
        cx, ar = self.cx, self.ar
        self.phase_barrier()
        self.load_gain(self.g_ffn[l:l + 1, :])
        t = self.norm_tmp()
        hns = [ar.bf(D) for _ in range(2)]
        rhn = [self.sres("hn%d" % i) for i in range(2)]
        nb = 0
        dq = []
        for c in range(4):
            W, rWs = self.wget()
            Wv = W[:, 0:8192].rearrange("p (k n) -> p k n", n=512)
            for tt in range(NT):
                bank = nb % 4
                nb += 1
                for ft in range(16):
                    cx.op("pe", lambda h: h.matmul(self.ps[bank][:, 0:512], self.actT[:, ft, tt * 128:(tt + 1) * 128], Wv[:, ft, :],
                                                   start=(ft == 0), stop=(ft == 15)),
                          reads=[self.rAT[tt], rWs], writes=[self.rP[bank]], inc=(ft == 15))
                xs = self.X[:, tt, c * 512:(c + 1) * 512]
                cx.op("dve", lambda h: h.tensor_tensor(out=xs, in0=self.ps[bank][:, 0:512], in1=xs, op=ALU.add),
                      reads=[self.rP[bank]], writes=[self.rX[tt]])
                if c == 3:
                    while len(dq) > 0:
                        dq.pop(0)()
                    self._ffn_norm_tile(tt, t, hns, rhn, dq)
        while dq:
            dq.pop(0)()
        cx.dma("sp", [(self.halo_b[0:1, :], self.X[0:1, 0, :]), (self.halo_b[1:2, :], self.X[127:128, NT - 1, :])],
               reads=[self.rX[0], self.rX[NT - 1]], writes=[self.r_halo_b])
        cx.collective(self.halo_b[:, :], self.halo_g[:, :], reads=[self.r_halo_b], writes=[self.r_halo_g])
        if "x" in self.dbg:
            r = cx.res("xdbg")
            cx.dma("sp", [(self.xdbg.rearrange("(tt p) c -> p tt c", p=128), self.X[:, :, :])], reads=self.rX, writes=[r])

    def _ffn_norm_tile(self, tt, t, hns, rhn, defer=None):
        t = dict(t)
        t["hn"], t["rhn"] = hns[tt % 2], rhn[tt % 2]
        t["pb"] = (6, 7) if tt % 2 == 0 else (4, 5)
        self.norm_transpose(self.X[:, tt, :], self.rX[tt], 128,
                            lambda k0, n, tt=tt: self.actT[:, k0:k0 + n, tt * 128:(tt + 1) * 128],
                            self.rAT[tt], t, defer)

    def phase4(self, l):
        cx, ar = self.cx, self.ar
        self.phase_barrier()
        t = self.norm_tmp()
        hns = [ar.bf(D) for _ in range(1)]
        rhn = [self.sres("hn%d" % i) for i in range(1)]
        hrow = ar.f32(D)
        rhrow = self.sres("hrow")
        cx.dma("sp", [(hrow[0:2, :], self.halo_g[1:3, :])], reads=[self.r_halo_g], writes=[rhrow])
        t["hn"], t["rhn"] = hns[0], rhn[0]
        t["pb"] = (6, 7)
        t["rowscale"] = self.mlr_col[0:2, 0:1]
        self.norm_transpose(hrow[0:2, :], rhrow, 2, lambda k0, n: self.actT_full[:, k0:k0 + n, 0:T + 2:T + 1], self.rAH, t)
        self.phase_barrier()
        hbuf = [ar.f32(1026) for _ in range(2)]
        rhb = [self.sres("hbuf%d" % i) for i in range(2)]
        tg = ar.f32(T)
        tu = ar.f32(T)
        gs = ar.f32(T)
        rtg, rtu, rgs = self.sres("tg"), self.sres("tu"), self.sres("gs")
        aT = [ar.bf(4 * T).rearrange("p (a b) -> p a b", b=T) for _ in range(2)]
        raT = [self.sres("aT%d" % i) for i in range(2)]
        last_layer = (l == self.depth - 1)
        tail_ok = (self.stop is None)
        tn = self.norm_tmp()
        if last_layer:
            obf = [ar.f32(D)]
            robf = [self.sres("obf0")]
        else:
            hns2 = [ar.bf(D) for _ in range(2)]
            rhn2 = [self.sres("hn2_%d" % i) for i in range(2)]

        tdq = []

        def tail_tile(tt):
            if not tail_ok:
                return
            if last_layer:
                self._final_tile(tt, tn, obf[0], robf[0])
            else:
                while tdq:
                    tdq.pop(0)()
                t2 = dict(tn)
                t2["hn"], t2["rhn"] = hns2[tt % 2], rhn2[tt % 2]
                t2["pb"] = (0, 1) if tt % 2 == 0 else (2, 3)
                self.norm_transpose(self.X[:, tt, :], self.rX[tt], 128,
                                    lambda k0, n, tt=tt: self.actT[:, k0:k0 + n, tt * 128:(tt + 1) * 128],
                                    self.rAT[tt], t2, tdq)
        cnt = {"nt": 0, "db": 0}
        CW = (T + 2) // 3

        def up_tile(Wv, rWs, col0, n_idx, tdst, rtd):
            k = cnt["nt"]
            cnt["nt"] += 1
            banks = (0, 1, 2) if k % 2 == 0 else (3, 4, 5)
            hb_, rhb_ = hbuf[k % 2], rhb[k % 2]
            for c3, bank in enumerate(banks):
                for kt in range(16):
                    cx.op("pe", lambda h: h.matmul(self.ps[bank][:, 0:CW], Wv[:, kt, col0:col0 + 128],
                                                   self.actT_full[:, kt, c3 * CW:(c3 + 1) * CW], start=(kt == 0), stop=(kt == 15)),
                          reads=self.rAT + [self.rAH, rWs], writes=[self.rP[bank]], inc=(kt == 15))
            w0 = self.cvp[:, n_idx, 0:1]
            w1 = self.cvp[:, n_idx, 1:2]
            w2 = self.cvp[:, n_idx, 2:3]
            bb = self.cvp[:, n_idx, 3:4]
            for c3, bank in enumerate(banks):
                cx.op("act", lambda h: h.activation(out=hb_[:, c3 * CW:(c3 + 1) * CW], in_=self.ps[bank][:, 0:CW], func=AF.Copy),
                      reads=[self.rP[bank]], writes=[rhb_])
                lo = 1 if c3 == 0 else 0
                hi = CW - 1 if c3 == 2 else CW
                t0_ = c3 * CW + lo - 1
                cx.op("act", lambda h: h.activation(out=tdst[:, t0_:t0_ + (hi - lo)], in_=self.ps[bank][:, lo:hi], func=AF.Identity,
                                                    scale=w1, bias=bb),
                      reads=[self.rP[bank], self.rLP], writes=[rtd])
            cx.op("dve", lambda h: h.scalar_tensor_tensor(out=tdst, in0=hb_[:, 0:T], scalar=w0, in1=tdst, op0=ALU.mult, op1=ALU.add),
                  reads=[rhb_, self.rLP], writes=[rtd])
            cx.op("dve", lambda h: h.scalar_tensor_tensor(out=tdst, in0=hb_[:, 2:T + 2], scalar=w2, in1=tdst, op0=ALU.mult, op1=ALU.add),
                  reads=[rhb_, self.rLP], writes=[rtd])

        for step in FFN_STEPS:
            if step[0] == "up":
                gg, b2 = step[1], step[2]
                bi = 2 * gg + b2
                W, rWs = self.wget()
                Wv = W[:, 0:8192].rearrange("p (k n) -> p k n", n=512)
                for j in range(2):
                    ci = 2 * b2 + j
                    ct = 2 * bi + j
                    up_tile(Wv, rWs, j * 128, ct, tg, rtg)
                    cx.op("act", lambda h: h.activation(out=gs, in_=tg, func=AF.Silu), reads=[rtg], writes=[rgs])
                    up_tile(Wv, rWs, 256 + j * 128, 44 + ct, tu, rtu)
                    cx.op("dve", lambda h: h.tensor_tensor(out=aT[gg % 2][:, ci, :], in0=gs, in1=tu, op=ALU.mult),
                          reads=[rgs, rtu], writes=[raT[gg % 2]])
            else:
                gg = step[1]
                W, rWs = self.wget()
                Wd = W[:, 0:8192].rearrange("p (k n) -> p k n", n=2048)
                if gg == 10 and tail_ok:
                    self.load_gain(self.g_final[0:1, :] if last_layer else self.g_attn[l + 1:l + 2, :])
                for n4 in range(4):
                    for tt in range(NT):
                        bank = 6 + cnt["db"] % 2
                        cnt["db"] += 1
                        for ci in range(4):
                            cx.op("pe", lambda h: h.matmul(self.ps[bank][:, 0:512], aT[gg % 2][:, ci, tt * 128:(tt + 1) * 128],
                                                           Wd[:, ci, n4 * 512:(n4 + 1) * 512], start=(ci == 0), stop=(ci == 3)),
                                  reads=[raT[gg % 2], rWs], writes=[self.rP[bank]], inc=(ci == 3))
                        xs = self.X[:, tt, n4 * 512:(n4 + 1) * 512]
                        cx.op("dve", lambda h: h.tensor_tensor(out=xs, in0=self.ps[bank][:, 0:512], in1=xs, op=ALU.add),
                              reads=[self.rP[bank]], writes=[self.rX[tt]])
                        if gg == 10 and n4 == 3:
                            tail_tile(tt)
        while tdq:
            tdq.pop(0)()
        if "x2" in self.dbg:
            r = cx.res("xdbg2")
            cx.dma("sp", [(self.xdbg.rearrange("(tt p) c -> p tt c", p=128), self.X[:, :, :])], reads=self.rX, writes=[r])

    def _final_tile(self, tt, t, ob, rob):
        cx = self.cx
        ss, rss = t["ss"], t["rss"]
        xin = self.X[:, tt, :]
        cx.op("act", lambda h: h.activation(out=t["junk"], in_=xin, func=AF.Square, accum_out=ss[:, 0:1]),
              reads=[self.rX[tt]], writes=[t["rj"], rss])
        cx.op("act", lambda h: h.activation(out=ss[:, 1:2], in_=ss[:, 0:1], func=AF.Sqrt, scale=1.0 / D, bias=EPS),
              reads=[rss], writes=[rss])
        cx.op("dve", lambda h: h.reciprocal(out=ss[:, 2:3], in_=ss[:, 1:2]), reads=[rss], writes=[rss])
        cx.op("dve", lambda h: h.scalar_tensor_tensor(out=ob, in0=xin, scalar=ss[:, 2:3], in1=self.grep[:], op0=ALU.mult, op1=ALU.mult),
              reads=[self.rX[tt], rss, self.rG], writes=[rob])
        ro = cx.res("out%d" % tt)
        cx.dma("sp", [(self.out_d[tt * 128:(tt + 1) * 128, :], ob)], reads=[rob], writes=[ro])

    def epilogue(self):
        pass


_CACHE = {}


def _get_program(depth=DEPTH, dbg=None, stop=None):
    key = (depth, tuple(sorted(dbg or ())), stop)
    if key not in _CACHE:
        b = Builder(depth=depth, dbg=dbg, stop=stop)
        nc = b.build()
        _CACHE[key] = (nc, b)
    return _CACHE[key]


def make_in_maps(inputs):
    f = lambda a: np.ascontiguousarray(np.asarray(a, dtype=np.float32))
    x = f(inputs["x"])
    shared = {
        "w_in": f(inputs["w_in"]), "w_uq": f(inputs["w_uq"]), "w_ukv": f(inputs["w_ukv"]),
        "w_out": f(inputs["w_out"]), "w_up": f(inputs["w_up"]), "w_down": f(inputs["w_down"]),
        "attn_norm": f(inputs["attn_norm"]), "ffn_norm": f(inputs["ffn_norm"]),
        "final_norm": f(inputs["final_norm"]).reshape(1, D),
        "a_q_norm": f(inputs["a_q_norm"]), "a_k_norm": f(inputs["a_k_norm"]),
        "c_q_norm": f(inputs["c_q_norm"]), "c_kv_norm": f(inputs["c_kv_norm"]),
    }
    on = f(inputs["out_norm"])
    shared["out_norm_t"] = np.ascontiguousarray(on.reshape(DEPTH, 16, 128).transpose(0, 2, 1))
    cw = f(inputs["conv_w"])
    cb = f(inputs["conv_b"])
    cp = np.concatenate([cw, cb[:, None, :]], axis=1)
    cp = cp.reshape(DEPTH, 4, 88, 128).transpose(0, 3, 2, 1)
    shared["convp"] = np.ascontiguousarray(cp.reshape(DEPTH, 128, 88 * 4))
    maps = []
    for c in range(NCORES):
        b, h = c // 2, c % 2
        m = dict(shared)
        m["x"] = np.ascontiguousarray(x[b, h * T:(h + 1) * T, :])
        m.update(_consts_for_core(h))
        maps.append(m)
    return maps


def kernel(**inputs):
    nc, _ = _get_program()
    maps = make_in_maps(inputs)
    res = run_bass_kernel_spmd(nc, maps, core_ids=list(range(NCORES)))
    out = np.empty((4, S, D), np.float32)
    for c in range(NCORES):
        b, h = c // 2, c % 2
        out[b, h * T:(h + 1) * T, :] = res.results[c]["out"]
    return out
```

```python
import math
from contextlib import ExitStack

import numpy as np
import concourse.bass as bass
import concourse.mybir as mybir
from concourse.bass_utils import run_bass_kernel_spmd

F32 = mybir.dt.float32
BF16 = mybir.dt.bfloat16
AF = mybir.ActivationFunctionType
ALU = mybir.AluOpType
AX = mybir.AxisListType

D = 2048
S = 2048
T = 1024
NT = 8
DEPTH = 2
HD = 128
INW = 4928
DFF = 5632
EPS = 1e-6
NCORES = 8
PAIRS = [[0, 1], [2, 3], [4, 5], [6, 7]]

SAME_ENGINE_SYNC = True
STRICT_SAME_ENGINE = True


class Res:
    __slots__ = ("name", "w", "r", "dsem", "dcount", "excl")

    def __init__(self, name):
        self.name = name
        self.excl = False
        self.w = None
        self.r = []
        self.dsem = None
        self.dcount = 0


class Eng:
    def __init__(self, name, h, sem):
        self.name = name
        self.h = h
        self.sem = sem
        self.count = 0
        self.seen = {}
        self.seen_d = {}


class Ctx:
    def __init__(self, nc, stack):
        self.nc = nc
        self.stack = stack
        self.engs = {}
        for name, h in (("pe", nc.tensor), ("act", nc.scalar), ("dve", nc.vector),
                        ("pool", nc.gpsimd), ("sp", nc.sync)):
            sem = stack.enter_context(nc.semaphore("sem_" + name))
            self.engs[name] = Eng(name, h, sem)
        self.nsem = 5
        self.nwaits = 0
        self.nops = 0
        self.all_res = []

    def res(self, name):
        r = Res(name)
        self.all_res.append(r)
        return r

    def _need(self, e, toks):
        best_e = {}
        best_d = {}
        for item in toks:
            t, raw = item
            if t is None:
                continue
            if t[0] == "e":
                _, en, tick = t
                if en == e.name and (en in ("pe", "sp") or not raw or not SAME_ENGINE_SYNC):
                    continue
                if tick > best_e.get(en, 0):
                    best_e[en] = tick
            else:
                _, sem, val = t
                k = id(sem)
                if val > best_d.get(k, (None, 0))[1]:
                    best_d[k] = (sem, val)
        for en, tick in best_e.items():
            if e.seen.get(en, 0) >= tick:
                continue
            e.h.wait_ge(self.engs[en].sem, tick)
            e.seen[en] = tick
            self.nwaits += 1
        for k, (sem, val) in best_d.items():
            if e.seen_d.get(k, 0) >= val:
                continue
            e.h.wait_ge(sem, val)
            e.seen_d[k] = val
            self.nwaits += 1

    @staticmethod
    def _deps(reads, writes, eng=None):
        toks = []
        for r in reads:
            toks.append((r.w, True))
            if r.excl:
                toks.extend((t, False) for t in r.r if t[0] == "e" and t[1] != eng)
        for w in writes:
            toks.append((w.w, STRICT_SAME_ENGINE))
            toks.extend((t, STRICT_SAME_ENGINE) for t in w.r)
        return toks

    @staticmethod
    def _commit(tok, reads, writes):
        for r in reads:
            r.r.append(tok)
        for w in writes:
            w.w = tok
            w.r = []

    def op(self, eng, fn, reads=(), writes=(), inc=True):
        e = self.engs[eng]
        self._need(e, self._deps(reads, writes, eng))
        ins = fn(e.h)
        self.nops += 1
        if inc:
            ins.then_inc(e.sem, 1)
            e.count += 1
            tok = ("e", eng, e.count)
        else:
            tok = ("e", eng, e.count + 1)
        self._commit(tok, reads, writes)
        return tok

    def _dsem(self, wres):
        if wres.dsem is None:
            wres.dsem = self.stack.enter_context(self.nc.semaphore("dsem_%d" % self.nsem))
            self.nsem += 1
        return wres.dsem

    def dma(self, queue, pairs, reads=(), writes=()):
        e = self.engs[queue]
        self._need(e, self._deps(reads, writes))
        wres = writes[0]
        sem = self._dsem(wres)
        for (o, i) in pairs:
            e.h.dma_start(out=o, in_=i).then_inc(sem, 16)
            wres.dcount += 16
            self.nops += 1
        tok = ("d", sem, wres.dcount)
        self._commit(tok, reads, writes)
        return tok

    def collective(self, in_ap, out_ap, reads=(), writes=()):
        e = self.engs["pool"]
        self._need(e, self._deps(reads, writes))
        wres = writes[0]
        sem = self._dsem(wres)
        e.h.collective_compute("AllGather", ALU.bypass, replica_groups=PAIRS,
                               ins=[in_ap], outs=[out_ap]).then_inc(sem, 1)
        wres.dcount += 1
        tok = ("d", sem, wres.dcount)
        self._commit(tok, reads, writes)
        return tok

    def wait_all(self, eng, ress):
        e = self.engs[eng]
        toks = []
        for r in ress:
            toks.append((r.w, True))
            toks.extend((t, True) for t in r.r)
        self._need(e, toks)

    def barrier(self, ress):
        for en in ("pe", "act", "dve", "sp"):
            self.wait_all(en, ress)


def _rope_tab(pos, dim, theta):
    inv = theta ** (-np.arange(0, dim, 2, dtype=np.float64) / dim)
    ang = pos.astype(np.float64)[:, None] * inv[None, :]
    return np.cos(ang), np.sin(ang)


def _tables_for_core(h):
    t = np.arange(h * T, (h + 1) * T)
    cr, sr = _rope_tab(t // 64, 64, 10000.0)
    cc, sc = _rope_tab(t % 64, 64, 10000.0)
    CA = np.concatenate([cr, cr, cc, cc], 1)
    SA = np.concatenate([-sr, sr, -sc, sc], 1)
    cp, sp_ = _rope_tab(t, 32, 500000.0)
    CB = np.concatenate([cp, cp], 1)
    SB = np.concatenate([-sp_, sp_], 1)
    cm, sm = _rope_tab(t, 64, 10000.0)
    CM = np.concatenate([cm, cm], 1)
    SM = np.concatenate([-sm, sm], 1)
    tab = np.concatenate([CA, SA, CB, SB, CM, SM], 1).astype(np.float32)
    return np.ascontiguousarray(tab.reshape(NT, 128, 448).transpose(1, 0, 2))


TAB_CA, TAB_SA, TAB_CB, TAB_SB, TAB_CM, TAB_SM = 0, 128, 256, 288, 320, 384


def _consts_for_core(h):
    c = {}
    c["tab"] = _tables_for_core(h)
    c["ident"] = np.eye(128, dtype=np.float32)
    p = np.arange(128)[:, None]
    n = np.arange(128)[None, :]
    NEG = -30000.0
    bm = np.concatenate([np.where(p >= n, 0.0, NEG), np.where(p <= n, 0.0, NEG)], 1).astype(np.float32)
    c["bmask"] = np.ascontiguousarray(bm)
    mL = 1.0 if h == 1 else 0.0
    mR = 1.0 if h == 0 else 0.0
    c["mlr_rep"] = np.tile(np.array([[mL, mR]], np.float32), (128, 1))
    c["mlr_col"] = np.array([[mL], [mR]], np.float32)
    return c


IN_CHUNKS = [
    (4352, 512, "ckv"), (4864, 64, "ckr"), (2816, 512, "bk"), (3328, 512, "bv"), (1024, 256, "av"),
    (512, 512, "a1"), (0, 512, "a0"), (1280, 512, "bq0"), (1792, 512, "bq1"), (2304, 512, "bq2"),
    (3840, 512, "cq"),
]

FFN_STEPS = []
for _g in range(11):
    FFN_STEPS += [("up", _g, 0), ("up", _g, 1)]
    if _g >= 1:
        FFN_STEPS.append(("down", _g - 1))
FFN_STEPS.append(("down", 10))

KVG_COLS = [6144, 6144, 6656, 6144]
KVREG = {
    "ka": (0, 0), "kb": (0, 2 * 128 * T),
    "kc": (1, 0),
    "kr": (2, 0), "va": (2, 64 * T), "vb": (2, 64 * T + T * 256),
    "vc": (3, 0),
}
KVGRP_KEYS = [("ka", "kb"), ("kc",), ("kr", "va", "vb"), ("vc",)]


class Arena:
    def __init__(self, t, nwords):
        self.t = t
        self.n = nwords
        self.off = 0

    def reset(self):
        self.off = 0

    def f32(self, n):
        assert self.off + n <= self.n, ("arena overflow", self.off, n, self.n)
        v = self.t[:, self.off:self.off + n]
        self.off += n
        return v

    def bf(self, n):
        w = (n + 1) // 2
        v = self.f32(w).bitcast(BF16)
        return v[:, 0:n]


class Builder:
    def __init__(self, depth=DEPTH, dbg=None, stop=None):
        self.depth = depth
        self.dbg = dbg or set()
        self.stop = stop
        self.nc = bass.Bass("TRN2", target_bir_lowering=False)
        self.stack = ExitStack()

    def dram_in(self, name, shape, dt=F32):
        return self.nc.dram_tensor(name, list(shape), dt, kind="ExternalInput").ap()

    def dram_tmp(self, name, shape, dt, out=False):
        kind = "ExternalOutput" if out else "Internal"
        return self.nc.dram_tensor(name, list(shape), dt, kind=kind).ap()

    def sb(self, name, shape, dt):
        return self.stack.enter_context(self.nc.sbuf_tensor("s_" + name, list(shape), dt))

    def build(self):
        nc = self.nc
        st = self.stack
        cx = self.cx = Ctx(nc, st)
        L = self.depth
        self.x_in = self.dram_in("x", [T, D])
        self.w_in = self.dram_in("w_in", [L, D, INW])
        self.w_uq = self.dram_in("w_uq", [L, 512, 1152])
        self.w_ukv = self.dram_in("w_ukv", [L, 512, 1536])
        self.need_out = self.stop not in ("p1", "p1a", "p2")
        self.need_ffn = self.stop not in ("p1", "p1a", "p2", "p3")
        if self.need_out:
            self.w_out = self.dram_in("w_out", [L, D, D])
        if self.need_ffn:
            self.w_up = self.dram_in("w_up", [L, D, 2 * DFF])
            self.w_down = self.dram_in("w_down", [L, DFF, D])
        self.g_attn = self.dram_in("attn_norm", [DEPTH, D])
        self.g_ffn = self.dram_in("ffn_norm", [DEPTH, D])
        self.g_final = self.dram_in("final_norm", [1, D])
        self.g_aq = self.dram_in("a_q_norm", [DEPTH, 128])
        self.g_ak = self.dram_in("a_k_norm", [DEPTH, 128])
        self.g_cq = self.dram_in("c_q_norm", [DEPTH, 512])
        self.g_ckv = self.dram_in("c_kv_norm", [DEPTH, 512])
        self.g_out = self.dram_in("out_norm_t", [DEPTH, 128, 16])
        self.convp = self.dram_in("convp", [DEPTH, 128, 88 * 4])
        self.tab_d = self.dram_in("tab", [128, NT, 448])
        self.ident_d = self.dram_in("ident", [128, 128])
        self.bmask_d = self.dram_in("bmask", [128, 256])
        self.mlr_rep_d = self.dram_in("mlr_rep", [128, 2])
        self.mlr_col_d = self.dram_in("mlr_col", [2, 1])
        self.out_d = self.nc.dram_tensor("out", [T, D], F32, kind="ExternalOutput").ap()
        dq = "q" in self.dbg
        self.qa_d = self.dram_tmp("qa_d", [6, 128, T], BF16, dq)
        self.qb_d = self.dram_tmp("qb_d", [12, 128, T], BF16, dq)
        self.qcn_d = self.dram_tmp("qcn_d", [6, 128, T], BF16, dq)
        self.qcr_d = self.dram_tmp("qcr_d", [6, 64, T], BF16, dq)
        self.kvb = [self.dram_tmp("kvb%d" % i, [128, c], BF16) for i, c in enumerate(KVG_COLS)]
        self.kvg = [self.dram_tmp("kvg%d" % i, [256, c], BF16) for i, c in enumerate(KVG_COLS)]
        self.halo_b = self.dram_tmp("halo_b", [2, D], F32)
        self.halo_g = self.dram_tmp("halo_g", [4, D], F32)
        if "kv" in self.dbg:
            self.kvdbg = [self.dram_tmp("kvdbg%d" % i, [256, c], BF16, True) for i, c in enumerate(KVG_COLS)]
        if "x" in self.dbg or "x2" in self.dbg:
            self.xdbg = self.dram_tmp("xdbg", [T, D], F32, True)
        if "y" in self.dbg:
            self.ydbg = self.dram_tmp("ydbg", [16, 128, T], BF16, True)
        self.r_qd = {k: cx.res("qd_" + k) for k in ("qa", "qb", "qcn", "qcr")}
        self.r_kvb = {k: cx.res("kvb_" + k) for k in ("ka", "kb", "kc", "kr", "va", "vb", "vc")}
        self.r_kvg = [cx.res("kvg%d" % i) for i in range(4)]
        self.r_halo_b = cx.res("halo_b")
        self.r_halo_g = cx.res("halo_g")
        self.X = self.sb("X", [128, NT, D], F32)
        self.rX = [cx.res("X%d" % i) for i in range(NT)]
        self.actT_full = self.sb("actT", [128, 16, T + 2], BF16)
        self.actT = self.actT_full[:, :, 1:T + 1]
        self.rAT = [cx.res("actT%d" % i) for i in range(NT)]
        self.rAH = cx.res("actH")
        self.NSLOT = 3
        self.wsl = [self.sb("wsl%d" % i, [128, 8192], BF16) for i in range(self.NSLOT)]
        self.rW = [cx.res("wsl%d" % i) for i in range(self.NSLOT)]
        self.grep = self.sb("grep", [128, D], F32)
        self.rG = cx.res("grep")
        self.ident = self.sb("ident", [128, 128], BF16)
        self.ones = self.sb("ones", [128, 128], BF16)
        self.ones_f = self.sb("ones_f", [128, 128], F32)
        self.onesLR = self.sb("onesLR", [64, 2, 128], BF16)
        self.bmask = self.sb("bmask", [128, 256], BF16)
        self.onesE = self.sb("onesE", [128, 3, 128], BF16)
        self.bmask01 = self.sb("bmask01", [128, 256], BF16)
        self.mlr_rep = self.sb("mlr_rep_s", [128, 2], F32)
        self.mlr_col = self.sb("mlr_col_s", [2, 1], F32)
        self.rC = cx.res("consts")
        self.gqk = self.sb("gqk", [128, 2, 128], F32)
        self.gout = self.sb("gout", [128, 16], F32)
        self.cvp = self.sb("cvp", [128, 88, 4], F32)
        self.rLP = cx.res("layer_params")
        self.ARW = 12800
        self.ar_t = self.sb("arena", [128, self.ARW], F32)
        self.ar = Arena(self.ar_t, self.ARW)
        self.ps = [st.enter_context(nc.psum_tensor("ps%d" % i, [128, 512], F32)) for i in range(8)]
        self.rP = [cx.res("ps%d" % i) for i in range(8)]
        for r in self.rP:
            r.excl = True
        self.scratch_res = []

        self.setup_weight_stream()
        self.prologue()
        for l in range(L):
            self.layer_params(l)
            self.phase1(l)
            if self.stop in ("p1", "p1a"):
                break
            self.phase2(l)
            if self.stop == "p2":
                break
            self.phase3(l)
            if self.stop == "p3":
                break
            self.phase4(l)
        if self.stop is None:
            self.epilogue()
        self.finish()
        return nc

    def warm_pe(self, n=48, bank=7):
        cx = self.cx
        for i in range(n):
            cx.op("pe", lambda h: h.matmul(self.ps[bank][:, 0:256], self.ident[:, :], self.bmask[:, 0:256], start=True, stop=True),
                  reads=[self.rC], writes=[self.rP[bank]], inc=(i == n - 1))

    def sres(self, name):
        r = self.cx.res(name)
        self.scratch_res.append(r)
        return r

    def phase_barrier(self):
        self.cx.barrier(self.cx.all_res)
        self.scratch_res = []
        self.ar.reset()

    def setup_weight_stream(self):
        ch = []
        for l in range(self.depth):
            for (c0, ncol, kind) in IN_CHUNKS:
                src = self.w_in[l, :, c0:c0 + ncol].rearrange("(kt p) n -> p kt n", p=128)
                ch.append([(lambda s, ncol=ncol: s[:, 0:16 * ncol].rearrange("p (k n) -> p k n", n=ncol), src)])
                if kind == "a1":
                    src2 = self.w_ukv[l].rearrange("(kt p) n -> p kt n", p=128)
                    ch.append([(lambda s: s[:, 0:4 * 1536].rearrange("p (k n) -> p k n", n=1536), src2)])
            src = self.w_uq[l].rearrange("(kt p) n -> p kt n", p=128)
            ch.append([(lambda s: s[:, 0:4 * 1152].rearrange("p (k n) -> p k n", n=1152), src)])
            for c in range(4 if self.need_out else 0):
                src = self.w_out[l, :, c * 512:(c + 1) * 512].rearrange("(kt p) n -> p kt n", p=128)
                ch.append([(lambda s: s[:, 0:8192].rearrange("p (k n) -> p k n", n=512), src)])
            for step in (FFN_STEPS if self.need_ffn else []):
                if step[0] == "up":
                    bi = 2 * step[1] + step[2]
                    sg = self.w_up[l, :, 256 * bi:256 * bi + 256].rearrange("(kt p) n -> p kt n", p=128)
                    su = self.w_up[l, :, DFF + 256 * bi:DFF + 256 * bi + 256].rearrange("(kt p) n -> p kt n", p=128)
                    ch.append([
                        (lambda s: s[:, 0:8192].rearrange("p (k n) -> p k n", n=512)[:, :, 0:256], sg),
                        (lambda s: s[:, 0:8192].rearrange("p (k n) -> p k n", n=512)[:, :, 256:512], su),
                    ])
                else:
                    gg = step[1]
                    sd = self.w_down[l, 512 * gg:512 * gg + 512, :].rearrange("(ct p) n -> p ct n", p=128)
                    ch.append([(lambda s: s[:, 0:8192].rearrange("p (k n) -> p k n", n=2048), sd)])
        self.wchunks = ch
        self.w_next_load = 0
        self.w_next_use = 0

    def _wload(self, k):
        slot = k % self.NSLOT
        pairs = [(fn(self.wsl[slot]), src) for (fn, src) in self.wchunks[k]]
        self.cx.dma("pool", pairs, writes=[self.rW[slot]])

    def wget(self):
        k = self.w_next_use
        self.w_next_use += 1
        while self.w_next_load < len(self.wchunks) and self.w_next_load <= k + self.NSLOT - 1:
            self._wload(self.w_next_load)
            self.w_next_load += 1
        slot = k % self.NSLOT
        return self.wsl[slot], self.rW[slot]

    def prologue(self):
        cx = self.cx
        for tt in range(NT):
            cx.dma("sp", [(self.X[:, tt, :], self.x_in[tt * 128:(tt + 1) * 128, :])], writes=[self.rX[tt]])
        rc = self.rC
        r1 = cx.res("c1")
        self.ident_f = self.ar.f32(128)
        self.bmask_f = self.ar.f32(256)
        cx.dma("sp", [(self.ident_f, self.ident_d[:, :]), (self.bmask_f, self.bmask_d[:, :]),
                      (self.mlr_rep[:], self.mlr_rep_d[:, :]), (self.mlr_col[:], self.mlr_col_d[:, :])],
               writes=[r1])
        cx.op("dve", lambda h: h.tensor_copy(self.ident[:], self.ident_f), reads=[r1], writes=[rc])
        cx.op("dve", lambda h: h.tensor_copy(self.bmask[:], self.bmask_f), reads=[r1], writes=[rc])
        cx.op("dve", lambda h: h.tensor_scalar(out=self.bmask01[:], in0=self.bmask_f, scalar1=0.0, scalar2=None, op0=ALU.is_equal),
              reads=[r1], writes=[rc])
        cx.op("dve", lambda h: h.memset(self.ones[:], 1.0), writes=[rc])
        cx.op("dve", lambda h: h.memset(self.ones_f[:], 1.0), writes=[rc])
        for j in range(3):
            cx.op("dve", lambda h: h.tensor_copy(self.onesE[:, j, :], self.ones[:]), reads=[rc], writes=[rc])
        cx.op("dve", lambda h: h.tensor_scalar(out=self.onesE[0:64, 2, :], in0=self.onesE[0:64, 2, :],
                                               scalar1=self.mlr_rep[0:64, 1:2], scalar2=None, op0=ALU.mult), reads=[r1, rc], writes=[rc])
        cx.op("dve", lambda h: h.memset(self.onesE[64:128, 2, :], 0.0), writes=[rc])
        cx.op("dve", lambda h: h.tensor_scalar(out=self.onesE[0:64, 0, :], in0=self.onesE[0:64, 0, :],
                                               scalar1=self.mlr_rep[0:64, 0:1], scalar2=None, op0=ALU.mult), reads=[r1, rc], writes=[rc])
        cx.op("dve", lambda h: h.tensor_scalar(out=self.onesE[64:128, 1, :], in0=self.onesE[64:128, 1, :],
                                               scalar1=self.mlr_rep[64:128, 1:2], scalar2=None, op0=ALU.mult), reads=[r1, rc], writes=[rc])
        for j in range(2):
            cx.op("dve", lambda h: h.tensor_scalar(out=self.onesLR[:, j, :], in0=self.ones[0:64, :],
                                                   scalar1=self.mlr_rep[0:64, j:j + 1], scalar2=None,
                                                   op0=ALU.mult), reads=[r1, rc], writes=[rc])

    def layer_params(self, l):
        cx = self.cx
        cx.dma("sp", [
            (self.gqk[:, 0, :], self.g_aq[l:l + 1, :].partition_broadcast(128)),
            (self.gqk[:, 1, :], self.g_ak[l:l + 1, :].partition_broadcast(128)),
            (self.gout[:], self.g_out[l]),
            (self.cvp[:].rearrange("p a b -> p (a b)"), self.convp[l]),
        ], writes=[self.rLP])

    def load_gain(self, src_row):
        self.cx.dma("sp", [(self.grep[:], src_row.partition_broadcast(128))], writes=[self.rG])

    def norm_transpose(self, xin, rxin, np_, dstT_fn, rdst, tmp, defer=None):
        cx = self.cx
        junk, rj, ss, rss, hn, rhn, pbanks = tmp["junk"], tmp["rj"], tmp["ss"], tmp["rss"], tmp["hn"], tmp["rhn"], tmp["pb"]
        cx.op("act", lambda h: h.activation(out=junk[0:np_, :], in_=xin, func=AF.Square, accum_out=ss[0:np_, 0:1]),
              reads=[rxin], writes=[rj, rss])
        cx.op("act", lambda h: h.activation(out=ss[0:np_, 1:2], in_=ss[0:np_, 0:1], func=AF.Sqrt, scale=1.0 / D, bias=EPS),
              reads=[rss], writes=[rss])
        cx.op("dve", lambda h: h.reciprocal(out=ss[0:np_, 2:3], in_=ss[0:np_, 1:2]), reads=[rss], writes=[rss])
        if tmp.get("rowscale") is not None:
            rs = tmp["rowscale"]
            cx.op("dve", lambda h: h.tensor_scalar(out=ss[0:np_, 2:3], in0=ss[0:np_, 2:3], scalar1=rs, scalar2=None, op0=ALU.mult),
                  reads=[rss, self.rC], writes=[rss])
        cx.op("dve", lambda h: h.scalar_tensor_tensor(out=hn[0:np_, :], in0=xin, scalar=ss[0:np_, 2:3], in1=self.grep[0:np_, :],
                                                      op0=ALU.mult, op1=ALU.mult), reads=[rxin, rss, self.rG], writes=[rhn])
        def stage_b():
            for half in range(2):
                bi = pbanks[half]
                pb = self.ps[bi][:].bitcast(BF16)
                for k in range(8):
                    kt = half * 8 + k
                    cx.op("pe", lambda h: h.transpose(out=pb[:, k * 128:k * 128 + np_], in_=hn[0:np_, kt * 128:(kt + 1) * 128],
                                                      identity=self.ident[0:np_, 0:np_]),
                          reads=[rhn, self.rC], writes=[self.rP[bi]], inc=(k == 7))
                src = pb[:, 0:1024].rearrange("p (a b) -> p a b", b=128)[:, :, 0:np_]
                dst = dstT_fn(half * 8, 8)
                eng = "act" if half == 0 else "dve"
                if eng == "act":
                    cx.op("act", lambda h: h.activation(out=dst, in_=src, func=AF.Copy), reads=[self.rP[bi]], writes=[rdst])
                else:
                    cx.op("dve", lambda h: h.tensor_copy(dst, src), reads=[self.rP[bi]], writes=[rdst])
        if defer is None:
            stage_b()
        else:
            defer.append(stage_b)

    def norm_tmp(self):
        ar = self.ar
        t = {}
        t["junk"] = ar.bf(D)
        t["rj"] = self.sres("junk")
        t["ss"] = ar.f32(8)
        t["rss"] = self.sres("ss")
        return t

    def finish(self):
        cx = self.cx
        cx.barrier(cx.all_res)
        cx.wait_all("pool", cx.all_res)

    def tt_copy_T(self, src_bf, rsrc, nblk, w, dst_fn, rdst, bank, eng):
        cx = self.cx
        pb = self.ps[bank][:].bitcast(BF16)
        for j in range(nblk):
            cx.op("pe", lambda h: h.transpose(out=pb[0:w, j * 128:(j + 1) * 128], in_=src_bf[:, j, :], identity=self.ident[:]),
                  reads=[rsrc, self.rC], writes=[self.rP[bank]], inc=(j == nblk - 1))
        src = pb[0:w, 0:nblk * 128].rearrange("p (a b) -> p a b", b=128)
        if eng == "act":
            cx.op("act", lambda h: h.activation(out=dst_fn(), in_=src, func=AF.Copy), reads=[self.rP[bank]], writes=[rdst])
        else:
            cx.op("dve", lambda h: h.tensor_copy(dst_fn(), src), reads=[self.rP[bank]], writes=[rdst])

    def _defer_T(self, defer, src_bf, rsrc, nblk, w, dst_fn, rdst, bank, eng):
        qsi = self._qsi

        def run():
            saved = self._qsi
            self._qsi = qsi
            self.tt_copy_T(src_bf, rsrc, nblk, w, dst_fn, rdst, bank, eng)
            self._qsi = saved
        defer.append(run)

    def phase1(self, l):
        cx, ar = self.cx, self.ar
        self.phase_barrier()
        if l == 0:
            self.load_gain(self.g_attn[l:l + 1, :])
        t = self.norm_tmp()
        hns = [ar.bf(D) for _ in range(2)]
        rhn = [self.sres("hn%d" % i) for i in range(2)]
        for tt in range(NT if l == 0 else 0):
            t["hn"], t["rhn"] = hns[tt % 2], rhn[tt % 2]
            t["pb"] = (6, 7) if tt % 2 == 0 else (4, 5)
            self.norm_transpose(self.X[:, tt, :], self.rX[tt], 128,
                                lambda k0, n, tt=tt: self.actT[:, k0:k0 + n, tt * 128:(tt + 1) * 128],
                                self.rAT[tt], t)
        if self.stop == "p1a":
            return
        self.phase_barrier()
        tabs = [ar.f32(448) for _ in range(3)]
        rtab = [self.sres("tab%d" % i) for i in range(3)]
        gc = ar.f32(1024).rearrange("p (a b) -> p a b", b=512)
        rgc = self.sres("gc")
        cx.dma("sp", [(gc[:, 0, :], self.g_cq[l:l + 1, :].partition_broadcast(128)),
                      (gc[:, 1, :], self.g_ckv[l:l + 1, :].partition_broadcast(128))], writes=[rgc])
        stg = [ar.f32(512) for _ in range(3)]
        rstg = [self.sres("stg%d" % i) for i in range(3)]
        tmp = [ar.f32(512) for _ in range(3)]
        rtmp = [self.sres("tmp%d" % i) for i in range(3)]
        roped = [ar.bf(512) for _ in range(3)]
        rrop = [self.sres("roped%d" % i) for i in range(3)]
        st = ar.f32(64)
        rst = self.sres("st")
        qstage_l = [ar.bf(4096) for _ in range(2)]
        rqs_l = [self.sres("qstage%d" % i) for i in range(2)]
        ckvnT = ar.bf(4096).rearrange("p (a b) -> p a b", b=T)
        cqnT = ckvnT
        rckv = self.sres("ckvnT")
        rcq = rckv
        self._qsi = 0
        self._qsn = 0

        class _QS:
            def __getitem__(s_, key):
                return qstage_l[self._qsi][key]

            def rearrange(s_, *a, **k):
                return qstage_l[self._qsi].rearrange(*a, **k)
        qstage = _QS()

        class _QV:
            def __init__(s_, b):
                s_.b = b

            def __getitem__(s_, key):
                return qstage_l[self._qsi].rearrange("p (a b) -> p a b", b=s_.b)[key]
        qs4 = _QV(T)
        vs8 = _QV(512)

        class _RQ:
            pass

        def cur_rqs():
            return rqs_l[self._qsi]

        def next_stage():
            self._qsn += 1
            self._qsi = self._qsn % 2
            return self._qsi
        self._cnt = 0

        def region(key, sub, n):
            g, off = KVREG[key]
            return self.kvb[g].rearrange("p c -> (p c)")[off + sub:off + sub + n]

        def exchange(g):
            cx.collective(self.kvb[g][:, :], self.kvg[g][:, :], reads=[self.r_kvb[k] for k in KVGRP_KEYS[g]],
                          writes=[self.r_kvg[g]])

        def load_tab(tt):
            i = self._cnt % 3
            self._cnt += 1
            cx.dma("sp", [(tabs[i], self.tab_d[:, tt, :])], writes=[rtab[i]])
            return tabs[i], rtab[i]

        mm_banks = [0, 1, 2, 3]
        tr_banks = [4, 5, 6, 7]
        self._mb = 0
        self._tb = 0

        def next_mb():
            b = mm_banks[self._mb % 4]
            self._mb += 1
            return b

        def next_tb():
            b = tr_banks[self._tb % 4]
            self._tb += 1
            return b

        def rope_small(src3, rsrc_list, nh, w, ctab, stab, tb, rtb, out3, rout, i):
            hw = w // 2
            t1 = tmp[0][:, 0:nh * w].rearrange("p (a b) -> p a b", b=w)
            t2 = tmp[1][:, 0:nh * w].rearrange("p (a b) -> p a b", b=w)
            cb = tb[:, ctab:ctab + w].unsqueeze(1).to_broadcast([128, nh, w])
            s0 = tb[:, stab:stab + hw].unsqueeze(1).to_broadcast([128, nh, hw])
            s1 = tb[:, stab + hw:stab + w].unsqueeze(1).to_broadcast([128, nh, hw])
            cx.op("dve", lambda h: h.tensor_tensor(out=t1, in0=src3, in1=cb, op=ALU.mult),
                  reads=rsrc_list + [rtb], writes=[rtmp[0]])
            cx.op("dve", lambda h: h.tensor_tensor(out=t2[:, :, 0:hw], in0=src3[:, :, hw:w], in1=s0, op=ALU.mult),
                  reads=rsrc_list + [rtb], writes=[rtmp[1]])
            cx.op("dve", lambda h: h.tensor_tensor(out=t2[:, :, hw:w], in0=src3[:, :, 0:hw], in1=s1, op=ALU.mult),
                  reads=rsrc_list + [rtb], writes=[rtmp[1]])
            cx.op("dve", lambda h: h.tensor_tensor(out=out3, in0=t1, in1=t2, op=ALU.add),
                  reads=[rtmp[0], rtmp[1]], writes=[rout])

        def latent_norm(bank, tt, i, gidx, cT, rcT, defer):
            s_, rs_ = stg[i], rstg[i]
            cx.op("act", lambda h: h.activation(out=s_, in_=self.ps[bank][:, 0:512], func=AF.Copy),
                  reads=[self.rP[bank]], writes=[rs_])
            cx.op("dve", lambda h: h.tensor_tensor(out=tmp[0], in0=s_, in1=s_, op=ALU.mult), reads=[rs_], writes=[rtmp[0]])
            cx.op("dve", lambda h: h.tensor_reduce(out=st[:, 0:1], in_=tmp[0], axis=AX.X, op=ALU.add), reads=[rtmp[0]], writes=[rst])
            cx.op("act", lambda h: h.activation(out=st[:, 1:2], in_=st[:, 0:1], func=AF.Sqrt, scale=1.0 / 512, bias=EPS),
                  reads=[rst], writes=[rst])
            cx.op("dve", lambda h: h.reciprocal(out=st[:, 2:3], in_=st[:, 1:2]), reads=[rst], writes=[rst])
            cx.op("dve", lambda h: h.scalar_tensor_tensor(out=roped[i], in0=s_, scalar=st[:, 2:3], in1=gc[:, gidx, :],
                                                          op0=ALU.mult, op1=ALU.mult), reads=[rs_, rst, rgc], writes=[rrop[i]])
            self._defer_T(defer, roped[i].rearrange("p (a b) -> p a b", b=128), rrop[i], 4, 128,
                          lambda: cT[:, 0:4, tt * 128:(tt + 1) * 128], rcT, next_tb(), "act")

        cstate = {}
        self._pc = 0

        def chunk_mm(c0, ncol, kind, tt):
            if tt == 0:
                W, rWs = self.wget()
                cstate[kind] = (W[:, 0:16 * ncol].rearrange("p (k n) -> p k n", n=ncol), rWs, {}, [None])
            Wv, rWs, banks, _q = cstate[kind]
            bank = next_mb()
            banks[tt] = bank
            for kt in range(16):
                cx.op("pe", lambda h: h.matmul(self.ps[bank][:, 0:ncol], self.actT[:, kt, tt * 128:(tt + 1) * 128], Wv[:, kt, :],
                                               start=(kt == 0), stop=(kt == 15)),
                      reads=[self.rAT[tt], rWs], writes=[self.rP[bank]], inc=(kt == 15))

        def chunk_post(c0, ncol, kind, tt, defer):
            if True:
                i = self._pc % 3
                self._pc += 1
                bank = cstate[kind][2][tt]
                if cstate[kind][3][0] is None:
                    cstate[kind][3][0] = next_stage()
                self._qsi = cstate[kind][3][0]
                psb = self.ps[bank]
                rpb = self.rP[bank]
                if kind in ("av", "bv"):
                    cx.op("act", lambda h: h.activation(out=vs8[:, tt, 0:ncol], in_=psb[:, 0:ncol], func=AF.Copy),
                          reads=[rpb], writes=[cur_rqs()])
                elif kind == "ckv":
                    latent_norm(bank, tt, i, 1, ckvnT, rckv, defer)
                elif kind == "cq":
                    latent_norm(bank, tt, i, 0, cqnT, rcq, defer)
                elif kind == "ckr":
                    tb, rtb = load_tab(tt)
                    src3 = psb[:, 0:64].rearrange("p (a b) -> p a b", b=64)
                    out3 = roped[i][:, 0:64].rearrange("p (a b) -> p a b", b=64)
                    rope_small(src3, [rpb], 1, 64, TAB_CM, TAB_SM, tb, rtb, out3, rrop[i], i)
                    self._defer_T(defer, out3, rrop[i], 1, 64,
                                   lambda: qstage[0:64, tt * 128:(tt + 1) * 128].rearrange("p (a b) -> p a b", b=128),
                                   cur_rqs(), next_tb(), "act")
                elif kind in ("bk", "bq0", "bq1", "bq2"):
                    tb, rtb = load_tab(tt)
                    ps3 = psb[:, 0:512].rearrange("p (a b) -> p a b", b=128)
                    r3 = roped[i].rearrange("p (a b) -> p a b", b=128)
                    cx.op("act", lambda h: h.activation(out=r3, in_=ps3, func=AF.Copy), reads=[rpb], writes=[rrop[i]])
                    rope_small(ps3[:, :, 0:32], [rpb], 4, 32, TAB_CB, TAB_SB, tb, rtb, r3[:, :, 0:32], rrop[i], i)
                    self._defer_T(defer, r3, rrop[i], 4, 128, lambda: qs4[:, 0:4, tt * 128:(tt + 1) * 128], cur_rqs(), next_tb(), "act")
                elif kind in ("a0", "a1"):
                    tb, rtb = load_tab(tt)
                    s_, rs_ = stg[i], rstg[i]
                    s3 = s_.rearrange("p (a b) -> p a b", b=128)
                    t0 = tmp[0].rearrange("p (a b) -> p a b", b=128)
                    cx.op("act", lambda h: h.activation(out=s_, in_=psb[:, 0:512], func=AF.Copy), reads=[rpb], writes=[rs_])
                    cx.op("dve", lambda h: h.tensor_tensor(out=tmp[0], in0=s_, in1=s_, op=ALU.mult), reads=[rs_], writes=[rtmp[0]])
                    cx.op("dve", lambda h: h.tensor_reduce(out=st[:, 0:4], in_=t0, axis=AX.X, op=ALU.add), reads=[rtmp[0]], writes=[rst])
                    cx.op("act", lambda h: h.activation(out=st[:, 4:8], in_=st[:, 0:4], func=AF.Sqrt, scale=1.0 / 128, bias=EPS),
                          reads=[rst], writes=[rst])
                    cx.op("dve", lambda h: h.reciprocal(out=st[:, 8:12], in_=st[:, 4:8]), reads=[rst], writes=[rst])
                    cx.op("dve", lambda h: h.tensor_tensor(out=t0, in0=s3, in1=st[:, 8:12].unsqueeze(2).to_broadcast([128, 4, 128]),
                                                           op=ALU.mult), reads=[rs_, rst], writes=[rtmp[0]])
                    x3 = tmp[2].rearrange("p (a b) -> p a b", b=128)
                    if kind == "a0":
                        cx.op("dve", lambda h: h.tensor_tensor(out=x3, in0=t0, in1=self.gqk[:, 0, :].unsqueeze(1).to_broadcast([128, 4, 128]),
                                                               op=ALU.mult), reads=[rtmp[0], self.rLP], writes=[rtmp[2]])
                    else:
                        for jj in range(2):
                            cx.op("dve", lambda h: h.tensor_tensor(out=x3[:, 2 * jj:2 * jj + 2, :], in0=t0[:, 2 * jj:2 * jj + 2, :],
                                                                   in1=self.gqk[:, jj, :].unsqueeze(1).to_broadcast([128, 2, 128]),
                                                                   op=ALU.mult), reads=[rtmp[0], self.rLP], writes=[rtmp[2]])
                    t1 = tmp[0].rearrange("p (a b) -> p a b", b=128)
                    cx.op("dve", lambda h: h.tensor_tensor(out=t1, in0=x3, in1=tb[:, TAB_CA:TAB_CA + 128].unsqueeze(1).to_broadcast([128, 4, 128]),
                                                           op=ALU.mult), reads=[rtmp[2], rtb], writes=[rtmp[0]])
                    x5 = tmp[2].rearrange("p (h f q e) -> p h f q e", h=4, f=2, q=2)
                    t5 = tmp[1].rearrange("p (h f q e) -> p h f q e", h=4, f=2, q=2)
                    sa = tb[:, TAB_SA:TAB_SA + 128].rearrange("p (f q e) -> p f q e", f=2, q=2)
                    for q in range(2):
                        sab = sa[:, :, q, :].unsqueeze(1).to_broadcast([128, 4, 2, 32])
                        cx.op("dve", lambda h: h.tensor_tensor(out=t5[:, :, :, q, :], in0=x5[:, :, :, 1 - q, :], in1=sab, op=ALU.mult),
                              reads=[rtmp[2], rtb], writes=[rtmp[1]])
                    r3 = roped[i].rearrange("p (a b) -> p a b", b=128)
                    cx.op("dve", lambda h: h.tensor_tensor(out=roped[i], in0=tmp[0], in1=tmp[1], op=ALU.add),
                          reads=[rtmp[0], rtmp[1]], writes=[rrop[i]])
                    self._defer_T(defer, r3, rrop[i], 4, 128, lambda: qs4[:, 0:4, tt * 128:(tt + 1) * 128], cur_rqs(), next_tb(), "act")
                else:
                    raise ValueError(kind)
        def chunk_store(c0, ncol, kind):
            self._qsi = cstate[kind][3][0]
            if kind == "av":
                cx.dma("sp", [(region("va", 0, T * 256).rearrange("(tt p c) -> p tt c", p=128, c=256), vs8[:, :, 0:256])],
                       reads=[cur_rqs()], writes=[self.r_kvb["va"]])
            elif kind == "bv":
                cx.dma("sp", [(region("vb", 0, T * 512).rearrange("(tt p c) -> p tt c", p=128, c=512), vs8[:, :, 0:512])],
                       reads=[cur_rqs()], writes=[self.r_kvb["vb"]])
            elif kind == "ckr":
                cx.dma("sp", [(region("kr", 0, 64 * T).rearrange("(p t) -> p t", t=T), qstage[0:64, 0:T])],
                       reads=[cur_rqs()], writes=[self.r_kvb["kr"]])
            elif kind == "bk":
                cx.dma("sp", [(region("kb", 0, 4 * 128 * T).rearrange("(h p t) -> p h t", p=128, t=T), qs4[:, 0:4, :])],
                       reads=[cur_rqs()], writes=[self.r_kvb["kb"]])
            elif kind in ("bq0", "bq1", "bq2"):
                g = int(kind[2])
                cx.dma("sp", [(self.qb_d[4 * g:4 * g + 4].rearrange("h p t -> p h t"), qs4[:, 0:4, :])],
                       reads=[cur_rqs()], writes=[self.r_qd["qb"]])
            elif kind == "a0":
                cx.dma("sp", [(self.qa_d[0:4].rearrange("h p t -> p h t"), qs4[:, 0:4, :])],
                       reads=[cur_rqs()], writes=[self.r_qd["qa"]])
            elif kind == "a1":
                cx.dma("sp", [(self.qa_d[4:6].rearrange("h p t -> p h t"), qs4[:, 0:2, :])],
                       reads=[cur_rqs()], writes=[self.r_qd["qa"]])
                cx.dma("sp", [(region("ka", 0, 2 * 128 * T).rearrange("(h p t) -> p h t", p=128, t=T), qs4[:, 2:4, :])],
                       reads=[cur_rqs()], writes=[self.r_kvb["ka"]])

        def latent_kv_up():
            W, rWs = self.wget()
            Wv = W[:, 0:4 * 1536].rearrange("p (k n) -> p k n", n=1536)
            for hg, heads in enumerate(((0, 1, 2, 3), (4, 5))):
                next_stage()
                for j, hd in enumerate(heads):
                    for c2 in range(2):
                        bank = next_mb()
                        for kt in range(4):
                            cx.op("pe", lambda h: h.matmul(self.ps[bank][:, 0:512], Wv[:, kt, hd * 256:hd * 256 + 128],
                                                           ckvnT[:, kt, c2 * 512:(c2 + 1) * 512], start=(kt == 0), stop=(kt == 3)),
                                  reads=[rckv, rWs], writes=[self.rP[bank]], inc=(kt == 3))
                        cx.op("act", lambda h: h.activation(out=qs4[:, j, c2 * 512:(c2 + 1) * 512], in_=self.ps[bank][:, 0:512], func=AF.Copy),
                              reads=[self.rP[bank]], writes=[cur_rqs()])
                nh = len(heads)
                cx.dma("sp", [(region("kc", hg * 4 * 128 * T, nh * 128 * T).rearrange("(h p t) -> p h t", p=128, t=T), qs4[:, 0:nh, :])],
                       reads=[cur_rqs()], writes=[self.r_kvb["kc"]])
            exchange(1)
            W6 = Wv.rearrange("p k (h c) -> p k h c", c=256)
            vc_reg = region("vc", 0, T * 768).rearrange("(tt p c) -> p tt c", p=128, c=768)
            for half in range(2):
                next_stage()
                vs = qstage[:, 0:8 * 384].rearrange("p (a b) -> p a b", b=384)
                for tt in range(NT):
                    bank = next_mb()
                    for kt in range(4):
                        cx.op("pe", lambda h: h.matmul(self.ps[bank][:, 0:384].rearrange("p (a b) -> p a b", b=128),
                                                       ckvnT[:, kt, tt * 128:(tt + 1) * 128],
                                                       W6[:, kt, 3 * half:3 * half + 3, 128:256], start=(kt == 0), stop=(kt == 3)),
                              reads=[rckv, rWs], writes=[self.rP[bank]], inc=(kt == 3))
                    cx.op("act", lambda h: h.activation(out=vs[:, tt, :], in_=self.ps[bank][:, 0:384], func=AF.Copy),
                          reads=[self.rP[bank]], writes=[cur_rqs()])
                cx.dma("sp", [(vc_reg[:, :, half * 384:(half + 1) * 384], vs)], reads=[cur_rqs()], writes=[self.r_kvb["vc"]])
            exchange(3)

        def latent_q_up():
            W, rWs = self.wget()
            Wv = W[:, 0:4 * 1152].rearrange("p (k n) -> p k n", n=1152)
            for hg, heads in enumerate(((0, 1, 2, 3), (4, 5))):
                next_stage()
                for j, hd in enumerate(heads):
                    for c2 in range(2):
                        bank = next_mb()
                        for kt in range(4):
                            cx.op("pe", lambda h: h.matmul(self.ps[bank][:, 0:512], Wv[:, kt, hd * 192:hd * 192 + 128],
                                                           cqnT[:, kt, c2 * 512:(c2 + 1) * 512], start=(kt == 0), stop=(kt == 3)),
                                  reads=[rcq, rWs], writes=[self.rP[bank]], inc=(kt == 3))
                        cx.op("act", lambda h: h.activation(out=qs4[:, j, c2 * 512:(c2 + 1) * 512], in_=self.ps[bank][:, 0:512], func=AF.Copy),
                              reads=[self.rP[bank]], writes=[cur_rqs()])
                nh = len(heads)
                cx.dma("sp", [(self.qcn_d[4 * hg:4 * hg + nh].rearrange("h p t -> p h t"), qs4[:, 0:nh, :])],
                       reads=[cur_rqs()], writes=[self.r_qd["qcn"]])
            W6 = Wv.rearrange("p k (h c) -> p k h c", c=192)
            for tt in range(NT):
                i = tt % 3
                tb, rtb = load_tab(tt)
                bank = next_mb()
                for kt in range(4):
                    cx.op("pe", lambda h: h.matmul(self.ps[bank][:, 0:384].rearrange("p (a b) -> p a b", b=64),
                                                   cqnT[:, kt, tt * 128:(tt + 1) * 128], W6[:, kt, :, 128:192],
                                                   start=(kt == 0), stop=(kt == 3)),
                          reads=[rcq, rWs], writes=[self.rP[bank]], inc=(kt == 3))
                src3 = self.ps[bank][:, 0:384].rearrange("p (a b) -> p a b", b=64)
                out3 = roped[i][:, 0:384].rearrange("p (a b) -> p a b", b=64)
                rope_small(src3, [self.rP[bank]], 6, 64, TAB_CM, TAB_SM, tb, rtb, out3, rrop[i], i)
                self.tt_copy_T(out3, rrop[i], 6, 64, lambda: self.actT[0:64, 0:6, tt * 128:(tt + 1) * 128],
                               self.rAT[tt], next_tb(), "act")
            cx.dma("sp", [(self.qcr_d.rearrange("h p t -> p h t"), self.actT[0:64, 0:6, :])],
                   reads=self.rAT, writes=[self.r_qd["qcr"]])

        lim = getattr(self, "p1_limit", None)
        items = []
        for ci_, (c0, ncol, kind) in enumerate(IN_CHUNKS):
            if lim is not None and ci_ >= lim:
                break
            for tt in range(NT):
                items.append((c0, ncol, kind, tt))

        def after_chunk(kind):
            if kind == "av":
                exchange(2)
            if kind == "a1":
                exchange(0)
                latent_kv_up()

        pendA = []
        pendB = []

        def do_A(it):
            d_ = []
            chunk_post(*it, d_)
            pendB.append((it, d_))

        def do_B():
            it, d_ = pendB.pop(0)
            for f in d_:
                f()
            if it[3] == NT - 1:
                chunk_store(it[0], it[1], it[2])
                after_chunk(it[2])

        def drain():
            while pendA:
                do_A(pendA.pop(0))
            while pendB:
                do_B()

        for it in items:
            if pendA and pendA[-1][2] == "a1" and it[2] != "a1":
                drain()
            chunk_mm(*it)
            pendA.append(it)
            while len(pendA) > 1:
                do_A(pendA.pop(0))
            while len(pendB) > 1:
                do_B()
        drain()
        if lim is not None:
            return
        latent_q_up()
        if "kv" in self.dbg:
            for g in range(4):
                r = cx.res("kvdbg%d" % g)
                cx.dma("sp", [(self.kvdbg[g][:, :], self.kvg[g][:, :])], reads=[self.r_kvg[g]], writes=[r])

    def kv_region(self, rank, key, sub, n):
        g, off = KVREG[key]
        off += sub
        if rank is None:
            return self.kvb[g].rearrange("p c -> (p c)")[off:off + n]
        base = rank * 128 * KVG_COLS[g]
        return self.kvg[g].rearrange("p c -> (p c)")[base + off:base + off + n]

    def rkvg(self, key):
        return self.r_kvg[KVREG[key][0]]

    def group_norm_out(self, stat_bank, width, heads_ft, on_fn, ron, c2, tmpA, rtmpA):
        cx = self.cx
        lnb, rstd = tmpA
        rlnb, rrstd = rtmpA
        cx.op("act", lambda h: h.activation(out=lnb, in_=self.ps[stat_bank][:, 0:512], func=AF.Ln, scale=1.0 / width, bias=EPS),
              reads=[self.rP[stat_bank]], writes=[rlnb])
        cx.op("act", lambda h: h.activation(out=rstd, in_=lnb, func=AF.Exp, scale=-0.5), reads=[rlnb], writes=[rrstd])
        for (hh, ft) in heads_ft:
            cx.op("dve", lambda h: h.scalar_tensor_tensor(out=self.actT[:, ft, c2 * 512:(c2 + 1) * 512], in0=on_fn(hh),
                                                          scalar=self.gout[:, ft:ft + 1], in1=rstd, op0=ALU.mult, op1=ALU.mult),
                  reads=[ron, rrstd, self.rLP], writes=self.rAT[4 * c2:4 * c2 + 4])

    def phase2(self, l):
        cx, ar = self.cx, self.ar
        self.phase_barrier()
        KT = [ar.bf(2048) for _ in range(2)]
        rKT = [self.sres("KT%d" % i) for i in range(2)]
        KrT = ar.bf(2048)
        rKr = self.sres("KrT")
        V = [ar.bf(2048).rearrange("p (a b) -> p a b", b=128) for _ in range(2)]
        rV = [self.sres("V%d" % i) for i in range(2)]
        Qn = [ar.bf(512) for _ in range(2)]
        Qr = [ar.bf(512) for _ in range(2)]
        rQ = [self.sres("Q%d" % i) for i in range(2)]
        NPT = 3
        PT = [ar.bf(512) for _ in range(NPT)]
        rPT = [self.sres("PT%d" % i) for i in range(NPT)]
        accP = [ar.f32(512) for _ in range(2)]
        raccP = [self.sres("accP%d" % i) for i in range(2)]
        onb = ar.f32(3072).rearrange("p (a b) -> p a b", b=512)
        ron = self.sres("onb")
        lnb = ar.f32(512)
        rden = ar.f32(512)
        rln = self.sres("lnb")
        rrd = self.sres("rden")
        sq = [ar.bf(512) for _ in range(2)]
        rsq = [self.sres("sq%d" % i) for i in range(2)]
        S_B = [0, 1, 2]
        O_B = [3, 4]
        U_B = [5, 6]
        STAT = 7
        rzp = self.sres("zpad")
        cx.op("dve", lambda h: h.memset(KrT[64:128, :], 0.0), writes=[rKr])
        for i_ in range(2):
            cx.op("dve", lambda h: h.memset(Qr[i_][64:128, :], 0.0), writes=[rQ[i_]])
        self._hc = 0
        self._kvc = 0
        pending = []

        def flush_stat(upto=3):
            for item in pending:
                while item and (3 - len(item)) < upto:
                    item.pop(0)()
            while pending and not pending[0]:
                pending.pop(0)

        def dense_core(hh, nh_in_mixer, first, last, KTb, rKTb, Vb, rVb, qi, scale, rope):
            k = self._hc
            self._hc += 1
            ob, ub = O_B[k % 2], U_B[k % 2]

            def emit_S(j):
                sbk = S_B[j % 3]
                cx.op("pe", lambda h: h.matmul(self.ps[sbk][:, 0:512], KTb[:, j * 128:(j + 1) * 128], Qn[qi][:, 0:512],
                                               start=True, stop=(not rope)),
                      reads=[rKTb, rQ[qi]], writes=[self.rP[sbk]], inc=(not rope))
                if rope:
                    cx.op("pe", lambda h: h.matmul(self.ps[sbk][:, 0:512], KrT[:, j * 128:(j + 1) * 128], Qr[qi][:, 0:512],
                                                   start=False, stop=True),
                          reads=[rKr, rQ[qi]], writes=[self.rP[sbk]])
            emit_S(0)
            emit_S(1)
            accp, raccp = accP[k % 2], raccP[k % 2]
            for j in range(16):
                sbk = S_B[j % 3]
                p = (k * 16 + j) % NPT
                cx.op("act", lambda h: h.activation(out=PT[p], in_=self.ps[sbk][:, 0:512], func=AF.Exp, scale=scale),
                      reads=[self.rP[sbk]], writes=[rPT[p]])
                if j + 2 < 16:
                    emit_S(j + 2)
                cx.op("pe", lambda h: h.matmul(self.ps[ob][:, 0:512], Vb[:, j, :], PT[p], start=(j == 0), stop=(j == 15)),
                      reads=[rVb, rPT[p]], writes=[self.rP[ob]], inc=True)
                if j % 2 == 0:
                    if j == 0:
                        cx.op("dve", lambda h: h.tensor_copy(accp, PT[p]), reads=[rPT[p]], writes=[raccp])
                    else:
                        cx.op("dve", lambda h: h.tensor_tensor(out=accp, in0=PT[p], in1=accp, op=ALU.add), reads=[rPT[p]], writes=[raccp])
                else:
                    cx.op("pe", lambda h: h.matmul(self.ps[ub][:, 0:512], self.ones[:], PT[p], start=(j == 1), stop=False),
                          reads=[self.rC, rPT[p]], writes=[self.rP[ub]], inc=True)
                if j == 3:
                    flush_stat(1)
                elif j == 5:
                    flush_stat(2)
                elif j == 9:
                    flush_stat(3)
            s = k % 2

            def st1(ub=ub, accp=accp, raccp=raccp):
                cx.op("pe", lambda h: h.matmul(self.ps[ub][:, 0:512], self.ones_f[:], accp, start=False, stop=True),
                      reads=[self.rC, raccp], writes=[self.rP[ub]], inc=True)

            def st2(hh=hh, ob=ob, ub=ub):
                cx.op("act", lambda h: h.activation(out=lnb, in_=self.ps[ub][:, 0:512], func=AF.Ln), reads=[self.rP[ub]], writes=[rln])
                cx.op("act", lambda h: h.activation(out=rden, in_=lnb, func=AF.Exp, scale=-1.0), reads=[rln], writes=[rrd])
                cx.op("dve", lambda h: h.tensor_tensor(out=onb[:, hh, :], in0=self.ps[ob][:, 0:512], in1=rden, op=ALU.mult),
                      reads=[self.rP[ob], rrd], writes=[ron])

            def st3(hh=hh, s=s, first=first, last=last):
                cx.op("dve", lambda h: h.tensor_tensor(out=sq[s], in0=onb[:, hh, :], in1=onb[:, hh, :], op=ALU.mult), reads=[ron], writes=[rsq[s]])
                cx.op("pe", lambda h: h.matmul(self.ps[STAT][:, 0:512], self.ones[:], sq[s], start=first, stop=last),
                      reads=[self.rC, rsq[s]], writes=[self.rP[STAT]], inc=True)
            pending.append([st1, st2, st3])

        def load_kv(which, hd):
            i = self._kvc % 2
            self._kvc += 1
            kk = "ka" if which == "a" else "kc"
            vk, wv = ("va", 256) if which == "a" else ("vc", 768)
            pk = []
            pv = []
            for r in range(2):
                ksrc = self.kv_region(r, kk, hd * 128 * T, 128 * T).rearrange("(p t) -> p t", t=T)
                pk.append((KT[i][:, r * T:(r + 1) * T], ksrc))
                vsrc = self.kv_region(r, vk, 0, T * wv).rearrange("(tt p c) -> p tt c", p=128, c=wv)[:, :, hd * 128:(hd + 1) * 128]
                pv.append((V[i][:, r * 8:(r + 1) * 8, :], vsrc))
            cx.dma("sp", pk, reads=[self.rkvg(kk)], writes=[rKT[i]])
            cx.dma("sp", pv, reads=[self.rkvg(vk)], writes=[rV[i]])
            return KT[i], rKT[i], V[i], rV[i]

        self._qc = 0

        def load_q(which, hd, c2):
            i = self._qc % 2
            self._qc += 1
            if which == "a":
                cx.dma("sp", [(Qn[i], self.qa_d[hd, :, c2 * 512:(c2 + 1) * 512])], reads=[self.r_qd["qa"]], writes=[rQ[i]])
            else:
                cx.dma("sp", [(Qn[i], self.qcn_d[hd, :, c2 * 512:(c2 + 1) * 512]),
                              (Qr[i][0:64, :], self.qcr_d[hd, :, c2 * 512:(c2 + 1) * 512])],
                       reads=[self.r_qd["qcn"], self.r_qd["qcr"]], writes=[rQ[i]])
            return i

        if "skipA" not in self.dbg:
            sc_a = 128.0 ** -0.5
            for c2 in range(2):
                for g in range(2):
                    KTb, rKTb, Vb, rVb = load_kv("a", g)
                    for hh in range(3 * g, 3 * g + 3):
                        qi = load_q("a", hh, c2)
                        dense_core(hh, 6, hh == 0, hh == 5, KTb, rKTb, Vb, rVb, qi, sc_a, False)
                flush_stat()
                self.group_norm_out(STAT, 768, [(hh, hh) for hh in range(6)], lambda hh: onb[:, hh, :], ron, c2,
                                    (lnb, rden), (rln, rrd))
        if "skipC" not in self.dbg:
            sc_c = 192.0 ** -0.5
            pk = []
            for r in range(2):
                pk.append((KrT[0:64, r * T:(r + 1) * T], self.kv_region(r, "kr", 0, 64 * T).rearrange("(p t) -> p t", t=T)))
            cx.dma("sp", pk, reads=[self.rkvg("kr")], writes=[rKr])
            for c2 in range(2):
                for hh in range(6):
                    KTb, rKTb, Vb, rVb = load_kv("c", hh)
                    qi = load_q("c", hh, c2)
                    dense_core(hh, 6, hh == 0, hh == 5, KTb, rKTb, Vb, rVb, qi, sc_c, True)
                flush_stat()
                self.group_norm_out(STAT, 768, [(hh, 10 + hh) for hh in range(6)], lambda hh: onb[:, hh, :], ron, c2,
                                    (lnb, rden), (rln, rrd))
        if "skipB" not in self.dbg:
            self.mixer_b(l)
        if "y" in self.dbg:
            r = cx.res("ydbg")
            cx.dma("sp", [(self.ydbg.rearrange("f p t -> p f t"), self.actT[:, :, :])], reads=self.rAT, writes=[r])

    def mixer_b(self, l):
        cx, ar = self.cx, self.ar
        self.phase_barrier()
        kte_w = ar.f32(1536)
        KTe = kte_w.bitcast(BF16)
        rKTe = self.sres("KTe")
        Vt_l = [ar.bf(32 * 128) for _ in range(2)]
        rVt_l = [self.sres("Vt%d" % i) for i in range(2)]
        Qb = [ar.bf(1024) for _ in range(2)]
        rQb = [self.sres("Qb%d" % i) for i in range(2)]
        Pm = [ar.bf(512) for _ in range(2)]
        rPm = [self.sres("Pm%d" % i) for i in range(2)]
        accO = ar.f32(4096).rearrange("p (a b) -> p a b", b=T)
        raccO = [self.sres("accO%d" % i) for i in range(4)]
        accS = ar.f32(1024)
        raccS = self.sres("accS")
        lnb = kte_w[:, 0:512]
        rstd = kte_w[:, 512:1024]
        rln = self.sres("lnbB")
        rrs = self.sres("rstdB")
        sq = [ar.bf(512) for _ in range(2)]
        rsq = [self.sres("sqB%d" % i) for i in range(2)]
        S_B = [0, 1]
        O_B = [2, 3]
        U_B = [4, 5]
        STAT = [6, 7]
        scale = 128.0 ** -0.5
        mb1 = self.bmask[:, 0:128]
        mb2 = self.bmask[:, 128:256]
        nqc = 0
        cnt = {"sb": 0, "bg": 0}
        pend_norm = []
        for hd in range(4):
            pk = [(KTe[:, 0:T], self.kv_region(0, "kb", hd * 128 * T, 128 * T).rearrange("(p t) -> p t", t=T)),
                  (KTe[:, T:2 * T], self.kv_region(None, "kb", hd * 128 * T, 128 * T).rearrange("(p t) -> p t", t=T)),
                  (KTe[:, 2 * T:3 * T], self.kv_region(1, "kb", hd * 128 * T, 128 * T).rearrange("(p t) -> p t", t=T))]
            cx.dma("sp", pk, reads=[self.rkvg("kb"), self.r_kvb["kb"]], writes=[rKTe])
            hc = slice(hd * 128, (hd + 1) * 128)
            for g, d in enumerate((1, 4, 16)):
                Lh = T // d
                nqb = Lh // 64
                QN = min(128, Lh)
                nqt = Lh // QN
                ntile = (nqb + 3) // 2
                qi = nqc % 2
                Vt, rVt = Vt_l[nqc % 2], rVt_l[nqc % 2]
                nqc += 1
                cx.dma("sp", [(Qb[qi], self.qb_d[g * 4 + hd])], reads=[self.r_qd["qb"]], writes=[rQb[qi]])
                Vt4 = Vt[:, 0:d * ntile * 128].rearrange("p (r u c) -> p r u c", r=d, u=ntile)
                vG0 = self.kv_region(0, "vb", 0, T * 512).rearrange("(t c) -> t c", c=512)
                vG1 = self.kv_region(1, "vb", 0, T * 512).rearrange("(t c) -> t c", c=512)
                vOwn = self.kv_region(None, "vb", 0, T * 512)
                mr = nqb + 1
                er, ur = mr % 2, mr // 2
                pairs = [(Vt4[0:64, :, 0, :], vG0[T - 64 * d:T, :].rearrange("(p r) c -> p r c", r=d)[:, :, hc]),
                         (Vt4[64 * er:64 * er + 64, :, ur, :], vG1[0:64 * d, :].rearrange("(p r) c -> p r c", r=d)[:, :, hc])]
                if d < 16:
                    vo = vOwn.rearrange("(u two p r c) -> two p r u c", two=2, p=64, r=d, c=512)
                    for rr in range(d):
                        pairs.append((Vt4[64:128, rr, 0:nqb // 2, :], vo[0, :, rr, :, hc]))
                        pairs.append((Vt4[0:64, rr, 1:nqb // 2 + 1, :], vo[1, :, rr, :, hc]))
                else:
                    pairs.append((Vt4[64:128, :, 0, :], vOwn.rearrange("(p r c) -> p r c", r=d, c=512)[:, :, hc]))
                cx.dma("sp", pairs, reads=[self.rkvg("vb"), self.r_kvb["vb"]], writes=[rVt])
                cx.op("dve", lambda h: h.tensor_scalar(out=Vt4[0:64, :, 0, :], in0=Vt4[0:64, :, 0, :],
                                                       scalar1=self.mlr_rep[0:64, 0:1], scalar2=None, op0=ALU.mult),
                      reads=[rVt, self.rC], writes=[rVt])
                cx.op("dve", lambda h: h.tensor_scalar(out=Vt4[64 * er:64 * er + 64, :, ur, :], in0=Vt4[64 * er:64 * er + 64, :, ur, :],
                                                       scalar1=self.mlr_rep[64 * er:64 * er + 64, 1:2], scalar2=None, op0=ALU.mult),
                      reads=[rVt, self.rC], writes=[rVt])
                if d == 16:
                    cx.op("dve", lambda h: h.memset(Vt4[64:128, :, 1, :], 0.0), writes=[rVt])
                qtiles = [(r, qt) for r in range(d) for qt in range(nqt)]
                nb = 256 // QN
                batches = [qtiles[i:i + nb] for i in range(0, len(qtiles), nb)]
                K2 = 128 if d < 16 else 64

                def ksl(r, u, n):
                    us = T + r + d * (128 * u - 64)
                    return KTe[:, us:us + (n - 1) * d + 1:d]

                def emit_S(bi):
                    sbk = S_B[cnt["sb"] % 2]
                    cnt["sb"] += 1
                    batch = batches[bi]
                    for t_, (r, qt) in enumerate(batch):
                        qs = r + d * QN * qt
                        qsl = Qb[qi][:, qs:qs + (QN - 1) * d + 1:d]
                        c1 = t_ * QN
                        c2 = nb * QN + t_ * QN
                        cx.op("pe", lambda h: h.matmul(self.ps[sbk][:, c1:c1 + QN], ksl(r, qt, 128), qsl, start=True, stop=True),
                              reads=[rKTe, rQb[qi]], writes=[self.rP[sbk]], inc=False)
                        cx.op("pe", lambda h: h.matmul(self.ps[sbk][0:K2, c2:c2 + QN], ksl(r, qt + 1, K2), qsl, start=True, stop=True),
                              reads=[rKTe, rQb[qi]], writes=[self.rP[sbk]], inc=(t_ == len(batch) - 1))
                    return sbk

                def emit_PV(bi, sbk, k_):
                    batch = batches[bi]
                    p_, rp_ = Pm[k_ % 2], rPm[k_ % 2]
                    if K2 == 128:
                        cx.op("act", lambda h: h.activation(out=p_, in_=self.ps[sbk][:, 0:512], func=AF.Exp, scale=scale),
                              reads=[self.rP[sbk]], writes=[rp_])
                    else:
                        cx.op("dve", lambda h: h.memset(p_[64:128, 256:512], 0.0), writes=[rp_])
                        cx.op("act", lambda h: h.activation(out=p_[:, 0:256], in_=self.ps[sbk][:, 0:256], func=AF.Exp, scale=scale),
                              reads=[self.rP[sbk]], writes=[rp_])
                        cx.op("act", lambda h: h.activation(out=p_[0:64, 256:512], in_=self.ps[sbk][0:64, 256:512], func=AF.Exp, scale=scale),
                              reads=[self.rP[sbk]], writes=[rp_])
                    m4 = self.bmask01[:].rearrange("p (m q) -> p m q", m=2)[:, :, 0:QN].unsqueeze(2).to_broadcast([128, 2, nb, QN])
                    p4 = p_.rearrange("p (m t q) -> p m t q", m=2, t=nb)
                    cx.op("dve", lambda h: h.tensor_tensor(out=p4, in0=p4, in1=m4, op=ALU.mult), reads=[self.rC], writes=[rp_])
                    fpos = (bi * 256) % 512
                    if fpos == 0:
                        cnt["bg"] += 1
                    ob, ub = O_B[cnt["bg"] % 2], U_B[cnt["bg"] % 2]
                    for t_, (r, qt) in enumerate(batch):
                        oc = fpos + t_ * QN
                        c1 = t_ * QN
                        c2 = nb * QN + t_ * QN
                        cx.op("pe", lambda h: h.matmul(self.ps[ob][:, oc:oc + QN], Vt4[:, r, qt, :], p_[:, c1:c1 + QN], start=True, stop=False),
                              reads=[rVt, rp_], writes=[self.rP[ob]], inc=False)
                        cx.op("pe", lambda h: h.matmul(self.ps[ob][:, oc:oc + QN], Vt4[:, r, qt + 1, :], p_[:, c2:c2 + QN],
                                                       start=False, stop=True),
                              reads=[rVt, rp_], writes=[self.rP[ob]], inc=False)
                        o1 = self.onesE[:, 0, :] if qt == 0 else self.ones[:]
                        if qt + 1 == ntile - 1:
                            o2 = self.onesE[:, 1, :] if K2 == 128 else self.onesE[:, 2, :]
                        else:
                            o2 = self.ones[:, :]
                        cx.op("pe", lambda h: h.matmul(self.ps[ub][:, oc:oc + QN], o1, p_[:, c1:c1 + QN], start=True, stop=False),
                              reads=[self.rC, rp_], writes=[self.rP[ub]], inc=False)
                        cx.op("pe", lambda h: h.matmul(self.ps[ub][:, oc:oc + QN], o2, p_[:, c2:c2 + QN], start=False, stop=True),
                              reads=[self.rC, rp_], writes=[self.rP[ub]], inc=True)
                    if fpos == 256:
                        evac(bi, ob, ub)

                def evac(bi, ob, ub):
                    f0 = (bi * 256) - 256
                    if d == 1:
                        ov = accO[:, hd, f0:f0 + 512]
                        sv = accS[:, f0:f0 + 512]
                        pso = self.ps[ob][:, 0:512]
                        psu = self.ps[ub][:, 0:512]
                    else:
                        rA = f0 // Lh
                        nres_b = 512 // Lh
                        ov = accO[:, hd, :].rearrange("p (i r) -> p r i", r=d)[:, rA:rA + nres_b, :]
                        sv = accS.rearrange("p (i r) -> p r i", r=d)[:, rA:rA + nres_b, :]
                        pso = self.ps[ob][:, 0:512].rearrange("p (r i) -> p r i", i=Lh)
                        psu = self.ps[ub][:, 0:512].rearrange("p (r i) -> p r i", i=Lh)
                    if g == 0:
                        cx.op("dve", lambda h: h.tensor_copy(ov, pso), reads=[self.rP[ob]], writes=[raccO[hd]])
                        cx.op("act", lambda h: h.activation(out=sv, in_=psu, func=AF.Copy), reads=[self.rP[ub]], writes=[raccS])
                    else:
                        cx.op("dve", lambda h: h.tensor_tensor(out=ov, in0=pso, in1=ov, op=ALU.add),
                              reads=[self.rP[ob]], writes=[raccO[hd]])
                        cx.op("dve", lambda h: h.tensor_tensor(out=sv, in0=psu, in1=sv, op=ALU.add),
                              reads=[self.rP[ub]], writes=[raccS])

                sb_next = emit_S(0)
                for bi in range(len(batches)):
                    sb_cur = sb_next
                    if bi + 1 < len(batches):
                        sb_next = emit_S(bi + 1)
                    emit_PV(bi, sb_cur, cnt["sb"] + bi)
                    if g == 0 and bi == 0:
                        while pend_norm:
                            pend_norm.pop(0)()
            def head_norm(hd=hd):
                for c2 in range(2):
                    cs = slice(c2 * 512, (c2 + 1) * 512)
                    cx.op("act", lambda h: h.activation(out=accS[:, cs], in_=accS[:, cs], func=AF.Ln), reads=[raccS], writes=[raccS])
                    cx.op("act", lambda h: h.activation(out=accS[:, cs], in_=accS[:, cs], func=AF.Exp, scale=-1.0), reads=[raccS], writes=[raccS])
                    cx.op("dve", lambda h: h.tensor_tensor(out=accO[:, hd, cs], in0=accO[:, hd, cs], in1=accS[:, cs], op=ALU.mult),
                          reads=[raccS], writes=[raccO[hd]])
                    s_ = (hd * 2 + c2) % 2
                    cx.op("dve", lambda h: h.tensor_tensor(out=sq[s_], in0=accO[:, hd, cs], in1=accO[:, hd, cs], op=ALU.mult),
                          reads=[raccO[hd]], writes=[rsq[s_]])
                    cx.op("pe", lambda h: h.matmul(self.ps[STAT[c2]][:, 0:512], self.ones[:], sq[s_], start=(hd == 0), stop=(hd == 3)),
                          reads=[self.rC, rsq[s_]], writes=[self.rP[STAT[c2]]], inc=True)
            pend_norm.append(head_norm)
        while pend_norm:
            pend_norm.pop(0)()
        ronall = self.sres("accO_all")
        cx.wait_all("dve", raccO)
        cx.wait_all("act", [rKTe])
        for c2 in range(2):
            self.group_norm_out(STAT[c2], 512, [(hh, 6 + hh) for hh in range(4)],
                                lambda hh: accO[:, hh, c2 * 512:(c2 + 1) * 512], ronall, c2, (lnb, rstd), (rln, rrs))

    def phase3(self, l):
        cx, ar = self.cx, self.ar
        self.phase_barrier()
        self.load_gain(self.g_ffn[l:l + 1, :])
        t = self.norm_tmp()
        hns = [ar.bf(D) for _ in range(2)]
        rhn = [self.sres("hn%d" % i) for i in range(2)]
        nb = 0
        dq = []
        for c in range(4):
            W, rWs = self.wget()
            Wv = W[:, 0:8192].rearrange("p (k n) -> p k n", n=512)
            for tt in range(NT):
                bank = nb % 4
                nb += 1
                for ft in range(16):
                    cx.op("pe", lambda h: h.matmul(self.ps[bank][:, 0:512], self.actT[:, ft, tt * 128:(tt + 1) * 128], Wv[:, ft, :],
                                                   start=(ft == 0), stop=(ft == 15)),
                          reads=[self.rAT[tt], rWs], writes=[self.rP[bank]], inc=(ft == 15))
                xs = self.X[:, tt, c * 512:(c + 1) * 512]
                cx.op("dve", lambda h: h.tensor_tensor(out=xs, in0=self.ps[bank][:, 0:512], in1=xs, op=ALU.add),
                      reads=[self.rP[bank]], writes=[self.rX[tt]])
                if c == 3:
                    while len(dq) > 0:
                        dq.pop(0)()
                    self._ffn_norm_tile(tt, t, hns, rhn, dq)
        while dq:
            dq.pop(0)()
        cx.dma("sp", [(self.halo_b[0:1, :], self.X[0:1, 0, :]), (self.halo_b[1:2, :], self.X[127:128, NT - 1, :])],
               reads=[self.rX[0], self.rX[NT - 1]], writes=[self.r_halo_b])
        cx.collective(self.halo_b[:, :], self.halo_g[:, :], reads=[self.r_halo_b], writes=[self.r_halo_g])
        if "x" in self.dbg:
            r = cx.res("xdbg")
            cx.dma("sp", [(self.xdbg.rearrange("(tt p) c -> p tt c", p=128), self.X[:, :, :])], reads=self.rX, writes=[r])

    def _ffn_norm_tile(self, tt, t, hns, rhn, defer=None):
        t = dict(t)
        t["hn"], t["rhn"] = hns[tt % 2], rhn[tt % 2]
        t["pb"] = (6, 7) if tt % 2 == 0 else (4, 5)
        self.norm_transpose(self.X[:, tt, :], self.rX[tt], 128,
                            lambda k0, n, tt=tt: self.actT[:, k0:k0 + n, tt * 128:(tt + 1) * 128],
                            self.rAT[tt], t, defer)

    def phase4(self, l):
        cx, ar = self.cx, self.ar
        self.phase_barrier()
        t = self.norm_tmp()
        hns = [ar.bf(D) for _ in range(1)]
        rhn = [self.sres("hn%d" % i) for i in range(1)]
        hrow = ar.f32(D)
        rhrow = self.sres("hrow")
        cx.dma("sp", [(hrow[0:2, :], self.halo_g[1:3, :])], reads=[self.r_halo_g], writes=[rhrow])
        t["hn"], t["rhn"] = hns[0], rhn[0]
        t["pb"] = (6, 7)
        t["rowscale"] = self.mlr_col[0:2, 0:1]
        self.norm_transpose(hrow[0:2, :], rhrow, 2, lambda k0, n: self.actT_full[:, k0:k0 + n, 0:T + 2:T + 1], self.rAH, t)
        self.phase_barrier()
        hbuf = [ar.f32(1026) for _ in range(2)]
        rhb = [self.sres("hbuf%d" % i) for i in range(2)]
        tg = ar.f32(T)
        tu = ar.f32(T)
        gs = ar.f32(T)
        rtg, rtu, rgs = self.sres("tg"), self.sres("tu"), self.sres("gs")
        aT = [ar.bf(4 * T).rearrange("p (a b) -> p a b", b=T) for _ in range(2)]
        raT = [self.sres("aT%d" % i) for i in range(2)]
        last_layer = (l == self.depth - 1)
        tail_ok = (self.stop is None)
        tn = self.norm_tmp()
        if last_layer:
            obf = [ar.f32(D)]
            robf = [self.sres("obf0")]
        else:
            hns2 = [ar.bf(D) for _ in range(2)]
            rhn2 = [self.sres("hn2_%d" % i) for i in range(2)]

        tdq = []

        def tail_tile(tt):
            if not tail_ok:
                return
            if last_layer:
                self._final_tile(tt, tn, obf[0], robf[0])
            else:
                while tdq:
                    tdq.pop(0)()
                t2 = dict(tn)
                t2["hn"], t2["rhn"] = hns2[tt % 2], rhn2[tt % 2]
                t2["pb"] = (0, 1) if tt % 2 == 0 else (2, 3)
                self.norm_transpose(self.X[:, tt, :], self.rX[tt], 128,
                                    lambda k0, n, tt=tt: self.actT[:, k0:k0 + n, tt * 128:(tt + 1) * 128],
                                    self.rAT[tt], t2, tdq)
        cnt = {"nt": 0, "db": 0}
        CW = (T + 2) // 3

        def up_tile(Wv, rWs, col0, n_idx, tdst, rtd):
            k = cnt["nt"]
            cnt["nt"] += 1
            banks = (0, 1, 2) if k % 2 == 0 else (3, 4, 5)
            hb_, rhb_ = hbuf[k % 2], rhb[k % 2]
            for c3, bank in enumerate(banks):
                for kt in range(16):
                    cx.op("pe", lambda h: h.matmul(self.ps[bank][:, 0:CW], Wv[:, kt, col0:col0 + 128],
                                                   self.actT_full[:, kt, c3 * CW:(c3 + 1) * CW], start=(kt == 0), stop=(kt == 15)),
                          reads=self.rAT + [self.rAH, rWs], writes=[self.rP[bank]], inc=(kt == 15))
            w0 = self.cvp[:, n_idx, 0:1]
            w1 = self.cvp[:, n_idx, 1:2]
            w2 = self.cvp[:, n_idx, 2:3]
            bb = self.cvp[:, n_idx, 3:4]
            for c3, bank in enumerate(banks):
                cx.op("act", lambda h: h.activation(out=hb_[:, c3 * CW:(c3 + 1) * CW], in_=self.ps[bank][:, 0:CW], func=AF.Copy),
                      reads=[self.rP[bank]], writes=[rhb_])
                lo = 1 if c3 == 0 else 0
                hi = CW - 1 if c3 == 2 else CW
                t0_ = c3 * CW + lo - 1
                cx.op("act", lambda h: h.activation(out=tdst[:, t0_:t0_ + (hi - lo)], in_=self.ps[bank][:, lo:hi], func=AF.Identity,
                                                    scale=w1, bias=bb),
                      reads=[self.rP[bank], self.rLP], writes=[rtd])
            cx.op("dve", lambda h: h.scalar_tensor_tensor(out=tdst, in0=hb_[:, 0:T], scalar=w0, in1=tdst, op0=ALU.mult, op1=ALU.add),
                  reads=[rhb_, self.rLP], writes=[rtd])
            cx.op("dve", lambda h: h.scalar_tensor_tensor(out=tdst, in0=hb_[:, 2:T + 2], scalar=w2, in1=tdst, op0=ALU.mult, op1=ALU.add),
                  reads=[rhb_, self.rLP], writes=[rtd])

        for step in FFN_STEPS:
            if step[0] == "up":
                gg, b2 = step[1], step[2]
                bi = 2 * gg + b2
                W, rWs = self.wget()
                Wv = W[:, 0:8192].rearrange("p (k n) -> p k n", n=512)
                for j in range(2):
                    ci = 2 * b2 + j
                    ct = 2 * bi + j
                    up_tile(Wv, rWs, j * 128, ct, tg, rtg)
                    cx.op("act", lambda h: h.activation(out=gs, in_=tg, func=AF.Silu), reads=[rtg], writes=[rgs])
                    up_tile(Wv, rWs, 256 + j * 128, 44 + ct, tu, rtu)
                    cx.op("dve", lambda h: h.tensor_tensor(out=aT[gg % 2][:, ci, :], in0=gs, in1=tu, op=ALU.mult),
                          reads=[rgs, rtu], writes=[raT[gg % 2]])
            else:
                gg = step[1]
                W, rWs = self.wget()
                Wd = W[:, 0:8192].rearrange("p (k n) -> p k n", n=2048)
                if gg == 10 and tail_ok:
                    self.load_gain(self.g_final[0:1, :] if last_layer else self.g_attn[l + 1:l + 2, :])
                for n4 in range(4):
                    for tt in range(NT):
                        bank = 6 + cnt["db"] % 2
                        cnt["db"] += 1
                        for ci in range(4):
                            cx.op("pe", lambda h: h.matmul(self.ps[bank][:, 0:512], aT[gg % 2][:, ci, tt * 128:(tt + 1) * 128],
                                                           Wd[:, ci, n4 * 512:(n4 + 1) * 512], start=(ci == 0), stop=(ci == 3)),
                                  reads=[raT[gg % 2], rWs], writes=[self.rP[bank]], inc=(ci == 3))
                        xs = self.X[:, tt, n4 * 512:(n4 + 1) * 512]
                        cx.op("dve", lambda h: h.tensor_tensor(out=xs, in0=self.ps[bank][:, 0:512], in1=xs, op=ALU.add),
                              reads=[self.rP[bank]], writes=[self.rX[tt]])
                        if gg == 10 and n4 == 3:
                            tail_tile(tt)
        while tdq:
            tdq.pop(0)()
        if "x2" in self.dbg:
            r = cx.res("xdbg2")
            cx.dma("sp", [(self.xdbg.rearrange("(tt p) c -> p tt c", p=128), self.X[:, :, :])], reads=self.rX, writes=[r])

    def _final_tile(self, tt, t, ob, rob):
        cx = self.cx
        ss, rss = t["ss"], t["rss"]
        xin = self.X[:, tt, :]
        cx.op("act", lambda h: h.activation(out=t["junk"], in_=xin, func=AF.Square, accum_out=ss[:, 0:1]),
              reads=[self.rX[tt]], writes=[t["rj"], rss])
        cx.op("act", lambda h: h.activation(out=ss[:, 1:2], in_=ss[:, 0:1], func=AF.Sqrt, scale=1.0 / D, bias=EPS),
              reads=[rss], writes=[rss])
        cx.op("dve", lambda h: h.reciprocal(out=ss[:, 2:3], in_=ss[:, 1:2]), reads=[rss], writes=[rss])
        cx.op("dve", lambda h: h.scalar_tensor_tensor(out=ob, in0=xin, scalar=ss[:, 2:3], in1=self.grep[:], op0=ALU.mult, op1=ALU.mult),
              reads=[self.rX[tt], rss, self.rG], writes=[rob])
        ro = cx.res("out%d" % tt)
        cx.dma("sp", [(self.out_d[tt * 128:(tt + 1) * 128, :], ob)], reads=[rob], writes=[ro])

    def epilogue(self):
        pass


_CACHE = {}


def _get_program(depth=DEPTH, dbg=None, stop=None):
    key = (depth, tuple(sorted(dbg or ())), stop)
    if key not in _CACHE:
        b = Builder(depth=depth, dbg=dbg, stop=stop)
        nc = b.build()
        _CACHE[key] = (nc, b)
    return _CACHE[key]


def make_in_maps(inputs):
    f = lambda a: np.ascontiguousarray(np.asarray(a, dtype=np.float32))
    x = f(inputs["x"])
    shared = {
        "w_in": f(inputs["w_in"]), "w_uq": f(inputs["w_uq"]), "w_ukv": f(inputs["w_ukv"]),
        "w_out": f(inputs["w_out"]), "w_up": f(inputs["w_up"]), "w_down": f(inputs["w_down"]),
        "attn_norm": f(inputs["attn_norm"]), "ffn_norm": f(inputs["ffn_norm"]),
        "final_norm": f(inputs["final_norm"]).reshape(1, D),
        "a_q_norm": f(inputs["a_q_norm"]), "a_k_norm": f(inputs["a_k_norm"]),
        "c_q_norm": f(inputs["c_q_norm"]), "c_kv_norm": f(inputs["c_kv_norm"]),
    }
    on = f(inputs["out_norm"])
    shared["out_norm_t"] = np.ascontiguousarray(on.reshape(DEPTH, 16, 128).transpose(0, 2, 1))
    cw = f(inputs["conv_w"])
    cb = f(inputs["conv_b"])
    cp = np.concatenate([cw, cb[:, None, :]], axis=1)
    cp = cp.reshape(DEPTH, 4, 88, 128).transpose(0, 3, 2, 1)
    shared["convp"] = np.ascontiguousarray(cp.reshape(DEPTH, 128, 88 * 4))
    maps = []
    for c in range(NCORES):
        b, h = c // 2, c % 2
        m = dict(shared)
        m["x"] = np.ascontiguousarray(x[b, h * T:(h + 1) * T, :])
        m.update(_consts_for_core(h))
        maps.append(m)
    return maps


def kernel(**inputs):
    nc, _ = _get_program()
    maps = make_in_maps(inputs)
    res = run_bass_kernel_spmd(nc, maps, core_ids=list(range(NCORES)))
    out = np.empty((4, S, D), np.float32)
    for c in range(NCORES):
        b, h = c // 2, c % 2
        out[b, h * T:(h + 1) * T, :] = res.results[c]["out"]
    return out
```

```python
import math
from contextlib import ExitStack

import numpy as np
import concourse.bass as bass
import concourse.mybir as mybir
from concourse.bass_utils import run_bass_kernel_spmd

F32 = mybir.dt.float32
BF16 = mybir.dt.bfloat16
AF = mybir.ActivationFunctionType
ALU = mybir.AluOpType
AX = mybir.AxisListType

D = 2048
S = 2048
T = 1024
NT = 8
DEPTH = 2
HD = 128
INW = 4928
DFF = 5632
EPS = 1e-6
NCORES = 8
PAIRS = [[0, 1], [2, 3], [4, 5], [6, 7]]

SAME_ENGINE_SYNC = True
STRICT_SAME_ENGINE = True


class Res:
    __slots__ = ("name", "w", "r", "dsem", "dcount", "excl")

    def __init__(self, name):
        self.name = name
        self.excl = False
        self.w = None
        self.r = []
        self.dsem = None
        self.dcount = 0


class Eng:
    def __init__(self, name, h, sem):
        self.name = name
        self.h = h
        self.sem = sem
        self.count = 0
        self.seen = {}
        self.seen_d = {}


class Ctx:
    def __init__(self, nc, stack):
        self.nc = nc
        self.stack = stack
        self.engs = {}
        for name, h in (("pe", nc.tensor), ("act", nc.scalar), ("dve", nc.vector),
                        ("pool", nc.gpsimd), ("sp", nc.sync)):
            sem = stack.enter_context(nc.semaphore("sem_" + name))
            self.engs[name] = Eng(name, h, sem)
        self.nsem = 5
        self.nwaits = 0
        self.nops = 0
        self.all_res = []

    def res(self, name):
        r = Res(name)
        self.all_res.append(r)
        return r

    def _need(self, e, toks):
        best_e = {}
        best_d = {}
        for item in toks:
            t, raw = item
            if t is None:
                continue
            if t[0] == "e":
                _, en, tick = t
                if en == e.name and (en in ("pe", "sp") or not raw or not SAME_ENGINE_SYNC):
                    continue
                if tick > best_e.get(en, 0):
                    best_e[en] = tick
            else:
                _, sem, val = t
                k = id(sem)
                if val > best_d.get(k, (None, 0))[1]:
                    best_d[k] = (sem, val)
        for en, tick in best_e.items():
            if e.seen.get(en, 0) >= tick:
                continue
            e.h.wait_ge(self.engs[en].sem, tick)
            e.seen[en] = tick
            self.nwaits += 1
        for k, (sem, val) in best_d.items():
            if e.seen_d.get(k, 0) >= val:
                continue
            e.h.wait_ge(sem, val)
            e.seen_d[k] = val
            self.nwaits += 1

    @staticmethod
    def _deps(reads, writes, eng=None):
        toks = []
        for r in reads:
            toks.append((r.w, True))
            if r.excl:
                toks.extend((t, False) for t in r.r if t[0] == "e" and t[1] != eng)
        for w in writes:
            toks.append((w.w, STRICT_SAME_ENGINE))
            toks.extend((t, STRICT_SAME_ENGINE) for t in w.r)
        return toks

    @staticmethod
    def _commit(tok, reads, writes):
        for r in reads:
            r.r.append(tok)
        for w in writes:
            w.w = tok
            w.r = []

    def op(self, eng, fn, reads=(), writes=(), inc=True):
        e = self.engs[eng]
        self._need(e, self._deps(reads, writes, eng))
        ins = fn(e.h)
        self.nops += 1
        if inc:
            ins.then_inc(e.sem, 1)
            e.count += 1
            tok = ("e", eng, e.count)
        else:
            tok = ("e", eng, e.count + 1)
        self._commit(tok, reads, writes)
        return tok

    def _dsem(self, wres):
        if wres.dsem is None:
            wres.dsem = self.stack.enter_context(self.nc.semaphore("dsem_%d" % self.nsem))
            self.nsem += 1
        return wres.dsem

    def dma(self, queue, pairs, reads=(), writes=()):
        e = self.engs[queue]
        self._need(e, self._deps(reads, writes))
        wres = writes[0]
        sem = self._dsem(wres)
        for (o, i) in pairs:
            e.h.dma_start(out=o, in_=i).then_inc(sem, 16)
            wres.dcount += 16
            self.nops += 1
        tok = ("d", sem, wres.dcount)
        self._commit(tok, reads, writes)
        return tok

    def collective(self, in_ap, out_ap, reads=(), writes=()):
        e = self.engs["pool"]
        self._need(e, self._deps(reads, writes))
        wres = writes[0]
        sem = self._dsem(wres)
        e.h.collective_compute("AllGather", ALU.bypass, replica_groups=PAIRS,
                               ins=[in_ap], outs=[out_ap]).then_inc(sem, 1)
        wres.dcount += 1
        tok = ("d", sem, wres.dcount)
        self._commit(tok, reads, writes)
        return tok

    def wait_all(self, eng, ress):
        e = self.engs[eng]
        toks = []
        for r in ress:
            toks.append((r.w, True))
            toks.extend((t, True) for t in r.r)
        self._need(e, toks)

    def barrier(self, ress):
        for en in ("pe", "act", "dve", "sp"):
            self.wait_all(en, ress)


def _rope_tab(pos, dim, theta):
    inv = theta ** (-np.arange(0, dim, 2, dtype=np.float64) / dim)
    ang = pos.astype(np.float64)[:, None] * inv[None, :]
    return np.cos(ang), np.sin(ang)


def _tables_for_core(h):
    t = np.arange(h * T, (h + 1) * T)
    cr, sr = _rope_tab(t // 64, 64, 10000.0)
    cc, sc = _rope_tab(t % 64, 64, 10000.0)
    CA = np.concatenate([cr, cr, cc, cc], 1)
    SA = np.concatenate([-sr, sr, -sc, sc], 1)
    cp, sp_ = _rope_tab(t, 32, 500000.0)
    CB = np.concatenate([cp, cp], 1)
    SB = np.concatenate([-sp_, sp_], 1)
    cm, sm = _rope_tab(t, 64, 10000.0)
    CM = np.concatenate([cm, cm], 1)
    SM = np.concatenate([-sm, sm], 1)
    tab = np.concatenate([CA, SA, CB, SB, CM, SM], 1).astype(np.float32)
    return np.ascontiguousarray(tab.reshape(NT, 128, 448).transpose(1, 0, 2))


TAB_CA, TAB_SA, TAB_CB, TAB_SB, TAB_CM, TAB_SM = 0, 128, 256, 288, 320, 384


def _consts_for_core(h):
    c = {}
    c["tab"] = _tables_for_core(h)
    c["ident"] = np.eye(128, dtype=np.float32)
    p = np.arange(128)[:, None]
    n = np.arange(128)[None, :]
    NEG = -30000.0
    bm = np.concatenate([np.where(p >= n, 0.0, NEG), np.where(p <= n, 0.0, NEG)], 1).astype(np.float32)
    c["bmask"] = np.ascontiguousarray(bm)
    mL = 1.0 if h == 1 else 0.0
    mR = 1.0 if h == 0 else 0.0
    c["mlr_rep"] = np.tile(np.array([[mL, mR]], np.float32), (128, 1))
    c["mlr_col"] = np.array([[mL], [mR]], np.float32)
    return c


IN_CHUNKS = [
    (4352, 512, "ckv"), (4864, 64, "ckr"), (2816, 512, "bk"), (3328, 512, "bv"), (1024, 256, "av"),
    (512, 512, "a1"), (0, 512, "a0"), (1280, 512, "bq0"), (1792, 512, "bq1"), (2304, 512, "bq2"),
    (3840, 512, "cq"),
]

FFN_STEPS = []
for _g in range(11):
    FFN_STEPS += [("up", _g, 0), ("up", _g, 1)]
    if _g >= 1:
        FFN_STEPS.append(("down", _g - 1))
FFN_STEPS.append(("down", 10))

KVG_COLS = [6144, 6144, 6656, 6144]
KVREG = {
    "ka": (0, 0), "kb": (0, 2 * 128 * T),
    "kc": (1, 0),
    "kr": (2, 0), "va": (2, 64 * T), "vb": (2, 64 * T + T * 256),
    "vc": (3, 0),
}
KVGRP_KEYS = [("ka", "kb"), ("kc",), ("kr", "va", "vb"), ("vc",)]


class Arena:
    def __init__(self, t, nwords):
        self.t = t
        self.n = nwords
        self.off = 0

    def reset(self):
        self.off = 0

    def f32(self, n):
        assert self.off + n <= self.n, ("arena overflow", self.off, n, self.n)
        v = self.t[:, self.off:self.off + n]
        self.off += n
        return v

    def bf(self, n):
        w = (n + 1) // 2
        v = self.f32(w).bitcast(BF16)
        return v[:, 0:n]


class Builder:
    def __init__(self, depth=DEPTH, dbg=None, stop=None):
        self.depth = depth
        self.dbg = dbg or set()
        self.stop = stop
        self.nc = bass.Bass("TRN2", target_bir_lowering=False)
        self.stack = ExitStack()

    def dram_in(self, name, shape, dt=F32):
        return self.nc.dram_tensor(name, list(shape), dt, kind="ExternalInput").ap()

    def dram_tmp(self, name, shape, dt, out=False):
        kind = "ExternalOutput" if out else "Internal"
        return self.nc.dram_tensor(name, list(shape), dt, kind=kind).ap()

    def sb(self, name, shape, dt):
        return self.stack.enter_context(self.nc.sbuf_tensor("s_" + name, list(shape), dt))

    def build(self):
        nc = self.nc
        st = self.stack
        cx = self.cx = Ctx(nc, st)
        L = self.depth
        self.x_in = self.dram_in("x", [T, D])
        self.w_in = self.dram_in("w_in", [L, D, INW])
        self.w_uq = self.dram_in("w_uq", [L, 512, 1152])
        self.w_ukv = self.dram_in("w_ukv", [L, 512, 1536])
        self.need_out = self.stop not in ("p1", "p1a", "p2")
        self.need_ffn = self.stop not in ("p1", "p1a", "p2", "p3")
        if self.need_out:
            self.w_out = self.dram_in("w_out", [L, D, D])
        if self.need_ffn:
            self.w_up = self.dram_in("w_up", [L, D, 2 * DFF])
            self.w_down = self.dram_in("w_down", [L, DFF, D])
        self.g_attn = self.dram_in("attn_norm", [DEPTH, D])
        self.g_ffn = self.dram_in("ffn_norm", [DEPTH, D])
        self.g_final = self.dram_in("final_norm", [1, D])
        self.g_aq = self.dram_in("a_q_norm", [DEPTH, 128])
        self.g_ak = self.dram_in("a_k_norm", [DEPTH, 128])
        self.g_cq = self.dram_in("c_q_norm", [DEPTH, 512])
        self.g_ckv = self.dram_in("c_kv_norm", [DEPTH, 512])
        self.g_out = self.dram_in("out_norm_t", [DEPTH, 128, 16])
        self.convp = self.dram_in("convp", [DEPTH, 128, 88 * 4])
        self.tab_d = self.dram_in("tab", [128, NT, 448])
        self.ident_d = self.dram_in("ident", [128, 128])
        self.bmask_d = self.dram_in("bmask", [128, 256])
        self.mlr_rep_d = self.dram_in("mlr_rep", [128, 2])
        self.mlr_col_d = self.dram_in("mlr_col", [2, 1])
        self.out_d = self.nc.dram_tensor("out", [T, D], F32, kind="ExternalOutput").ap()
        dq = "q" in self.dbg
        self.qa_d = self.dram_tmp("qa_d", [6, 128, T], BF16, dq)
        self.qb_d = self.dram_tmp("qb_d", [12, 128, T], BF16, dq)
        self.qcn_d = self.dram_tmp("qcn_d", [6, 128, T], BF16, dq)
        self.qcr_d = self.dram_tmp("qcr_d", [6, 64, T], BF16, dq)
        self.kvb = [self.dram_tmp("kvb%d" % i, [128, c], BF16) for i, c in enumerate(KVG_COLS)]
        self.kvg = [self.dram_tmp("kvg%d" % i, [256, c], BF16) for i, c in enumerate(KVG_COLS)]
        self.halo_b = self.dram_tmp("halo_b", [2, D], F32)
        self.halo_g = self.dram_tmp("halo_g", [4, D], F32)
        if "kv" in self.dbg:
            self.kvdbg = [self.dram_tmp("kvdbg%d" % i, [256, c], BF16, True) for i, c in enumerate(KVG_COLS)]
        if "x" in self.dbg or "x2" in self.dbg:
            self.xdbg = self.dram_tmp("xdbg", [T, D], F32, True)
        if "y" in self.dbg:
            self.ydbg = self.dram_tmp("ydbg", [16, 128, T], BF16, True)
        self.r_qd = {k: cx.res("qd_" + k) for k in ("qa", "qb", "qcn", "qcr")}
        self.r_kvb = {k: cx.res("kvb_" + k) for k in ("ka", "kb", "kc", "kr", "va", "vb", "vc")}
        self.r_kvg = [cx.res("kvg%d" % i) for i in range(4)]
        self.r_halo_b = cx.res("halo_b")
        self.r_halo_g = cx.res("halo_g")
        self.X = self.sb("X", [128, NT, D], F32)
        self.rX = [cx.res("X%d" % i) for i in range(NT)]
        self.actT_full = self.sb("actT", [128, 16, T + 2], BF16)
        self.actT = self.actT_full[:, :, 1:T + 1]
        self.rAT = [cx.res("actT%d" % i) for i in range(NT)]
        self.rAH = cx.res("actH")
        self.NSLOT = 3
        self.wsl = [self.sb("wsl%d" % i, [128, 8192], BF16) for i in range(self.NSLOT)]
        self.rW = [cx.res("wsl%d" % i) for i in range(self.NSLOT)]
        self.grep = self.sb("grep", [128, D], F32)
        self.rG = cx.res("grep")
        self.ident = self.sb("ident", [128, 128], BF16)
        self.ones = self.sb("ones", [128, 128], BF16)
        self.ones_f = self.sb("ones_f", [128, 128], F32)
        self.onesLR = self.sb("onesLR", [64, 2, 128], BF16)
        self.bmask = self.sb("bmask", [128, 256], BF16)
        self.onesE = self.sb("onesE", [128, 3, 128], BF16)
        self.bmask01 = self.sb("bmask01", [128, 256], BF16)
        self.mlr_rep = self.sb("mlr_rep_s", [128, 2], F32)
        self.mlr_col = self.sb("mlr_col_s", [2, 1], F32)
        self.rC = cx.res("consts")
        self.gqk = self.sb("gqk", [128, 2, 128], F32)
        self.gout = self.sb("gout", [128, 16], F32)
        self.cvp = self.sb("cvp", [128, 88, 4], F32)
        self.rLP = cx.res("layer_params")
        self.ARW = 12800
        self.ar_t = self.sb("arena", [128, self.ARW], F32)
        self.ar = Arena(self.ar_t, self.ARW)
        self.ps = [st.enter_context(nc.psum_tensor("ps%d" % i, [128, 512], F32)) for i in range(8)]
        self.rP = [cx.res("ps%d" % i) for i in range(8)]
        for r in self.rP:
            r.excl = True
        self.scratch_res = []

        self.setup_weight_stream()
        self.prologue()
        for l in range(L):
            self.layer_params(l)
            self.phase1(l)
            if self.stop in ("p1", "p1a"):
                break
            self.phase2(l)
            if self.stop == "p2":
                break
            self.phase3(l)
            if self.stop == "p3":
                break
            self.phase4(l)
        if self.stop is None:
            self.epilogue()
        self.finish()
        return nc

    def warm_pe(self, n=48, bank=7):
        cx = self.cx
        for i in range(n):
            cx.op("pe", lambda h: h.matmul(self.ps[bank][:, 0:256], self.ident[:, :], self.bmask[:, 0:256], start=True, stop=True),
                  reads=[self.rC], writes=[self.rP[bank]], inc=(i == n - 1))

    def sres(self, name):
        r = self.cx.res(name)
        self.scratch_res.append(r)
        return r

    def phase_barrier(self):
        self.cx.barrier(self.scratch_res)
        self.scratch_res = []
        self.ar.reset()

    def setup_weight_stream(self):
        ch = []
        for l in range(self.depth):
            for (c0, ncol, kind) in IN_CHUNKS:
                src = self.w_in[l, :, c0:c0 + ncol].rearrange("(kt p) n -> p kt n", p=128)
                ch.append([(lambda s, ncol=ncol: s[:, 0:16 * ncol].rearrange("p (k n) -> p k n", n=ncol), src)])
                if kind == "a1":
                    src2 = self.w_ukv[l].rearrange("(kt p) n -> p kt n", p=128)
                    ch.append([(lambda s: s[:, 0:4 * 1536].rearrange("p (k n) -> p k n", n=1536), src2)])
            src = self.w_uq[l].rearrange("(kt p) n -> p kt n", p=128)
            ch.append([(lambda s: s[:, 0:4 * 1152].rearrange("p (k n) -> p k n", n=1152), src)])
            for c in range(4 if self.need_out else 0):
                src = self.w_out[l, :, c * 512:(c + 1) * 512].rearrange("(kt p) n -> p kt n", p=128)
                ch.append([(lambda s: s[:, 0:8192].rearrange("p (k n) -> p k n", n=512), src)])
            for step in (FFN_STEPS if self.need_ffn else []):
                if step[0] == "up":
                    bi = 2 * step[1] + step[2]
                    sg = self.w_up[l, :, 256 * bi:256 * bi + 256].rearrange("(kt p) n -> p kt n", p=128)
                    su = self.w_up[l, :, DFF + 256 * bi:DFF + 256 * bi + 256].rearrange("(kt p) n -> p kt n", p=128)
                    ch.append([
                        (lambda s: s[:, 0:8192].rearrange("p (k n) -> p k n", n=512)[:, :, 0:256], sg),
                        (lambda s: s[:, 0:8192].rearrange("p (k n) -> p k n", n=512)[:, :, 256:512], su),
                    ])
                else:
                    gg = step[1]
                    sd = self.w_down[l, 512 * gg:512 * gg + 512, :].rearrange("(ct p) n -> p ct n", p=128)
                    ch.append([(lambda s: s[:, 0:8192].rearrange("p (k n) -> p k n", n=2048), sd)])
        self.wchunks = ch
        self.w_next_load = 0
        self.w_next_use = 0

    def _wload(self, k):
        slot = k % self.NSLOT
        pairs = [(fn(self.wsl[slot]), src) for (fn, src) in self.wchunks[k]]
        self.cx.dma("pool", pairs, writes=[self.rW[slot]])

    def wget(self):
        k = self.w_next_use
        self.w_next_use += 1
        while self.w_next_load < len(self.wchunks) and self.w_next_load <= k + self.NSLOT - 1:
            self._wload(self.w_next_load)
            self.w_next_load += 1
        slot = k % self.NSLOT
        return self.wsl[slot], self.rW[slot]

    def prologue(self):
        cx = self.cx
        for tt in range(NT):
            cx.dma("sp", [(self.X[:, tt, :], self.x_in[tt * 128:(tt + 1) * 128, :])], writes=[self.rX[tt]])
        rc = self.rC
        r1 = self.sres("c1")
        self.ident_f = self.ar.f32(128)
        self.bmask_f = self.ar.f32(256)
        cx.dma("sp", [(self.ident_f, self.ident_d[:, :]), (self.bmask_f, self.bmask_d[:, :]),
                      (self.mlr_rep[:], self.mlr_rep_d[:, :]), (self.mlr_col[:], self.mlr_col_d[:, :])],
               writes=[r1])
        cx.op("dve", lambda h: h.tensor_copy(self.ident[:], self.ident_f), reads=[r1], writes=[rc])
        cx.op("dve", lambda h: h.tensor_copy(self.bmask[:], self.bmask_f), reads=[r1], writes=[rc])
        cx.op("dve", lambda h: h.tensor_scalar(out=self.bmask01[:], in0=self.bmask_f, scalar1=0.0, scalar2=None, op0=ALU.is_equal),
              reads=[r1], writes=[rc])
        cx.op("dve", lambda h: h.memset(self.ones[:], 1.0), writes=[rc])
        cx.op("dve", lambda h: h.memset(self.ones_f[:], 1.0), writes=[rc])
        for j in range(3):
            cx.op("dve", lambda h: h.tensor_copy(self.onesE[:, j, :], self.ones[:]), reads=[rc], writes=[rc])
        cx.op("dve", lambda h: h.tensor_scalar(out=self.onesE[0:64, 2, :], in0=self.onesE[0:64, 2, :],
                                               scalar1=self.mlr_rep[0:64, 1:2], scalar2=None, op0=ALU.mult), reads=[r1, rc], writes=[rc])
        cx.op("dve", lambda h: h.memset(self.onesE[64:128, 2, :], 0.0), writes=[rc])
        cx.op("dve", lambda h: h.tensor_scalar(out=self.onesE[0:64, 0, :], in0=self.onesE[0:64, 0, :],
                                               scalar1=self.mlr_rep[0:64, 0:1], scalar2=None, op0=ALU.mult), reads=[r1, rc], writes=[rc])
        cx.op("dve", lambda h: h.tensor_scalar(out=self.onesE[64:128, 1, :], in0=self.onesE[64:128, 1, :],
                                               scalar1=self.mlr_rep[64:128, 1:2], scalar2=None, op0=ALU.mult), reads=[r1, rc], writes=[rc])
        for j in range(2):
            cx.op("dve", lambda h: h.tensor_scalar(out=self.onesLR[:, j, :], in0=self.ones[0:64, :],
                                                   scalar1=self.mlr_rep[0:64, j:j + 1], scalar2=None,
                                                   op0=ALU.mult), reads=[r1, rc], writes=[rc])

    def layer_params(self, l):
        cx = self.cx
        cx.dma("sp", [
            (self.gqk[:, 0, :], self.g_aq[l:l + 1, :].partition_broadcast(128)),
            (self.gqk[:, 1, :], self.g_ak[l:l + 1, :].partition_broadcast(128)),
            (self.gout[:], self.g_out[l]),
            (self.cvp[:].rearrange("p a b -> p (a b)"), self.convp[l]),
        ], writes=[self.rLP])

    def load_gain(self, src_row):
        self.cx.dma("sp", [(self.grep[:], src_row.partition_broadcast(128))], writes=[self.rG])

    def norm_transpose(self, xin, rxin, np_, dstT_fn, rdst, tmp, defer=None):
        cx = self.cx
        junk, rj, ss, rss, hn, rhn, pbanks = tmp["junk"], tmp["rj"], tmp["ss"], tmp["rss"], tmp["hn"], tmp["rhn"], tmp["pb"]
        cx.op("act", lambda h: h.activation(out=junk[0:np_, :], in_=xin, func=AF.Square, accum_out=ss[0:np_, 0:1]),
              reads=[rxin], writes=[rj, rss])
        cx.op("act", lambda h: h.activation(out=ss[0:np_, 1:2], in_=ss[0:np_, 0:1], func=AF.Sqrt, scale=1.0 / D, bias=EPS),
              reads=[rss], writes=[rss])
        cx.op("dve", lambda h: h.reciprocal(out=ss[0:np_, 2:3], in_=ss[0:np_, 1:2]), reads=[rss], writes=[rss])
        if tmp.get("rowscale") is not None:
            rs = tmp["rowscale"]
            cx.op("dve", lambda h: h.tensor_scalar(out=ss[0:np_, 2:3], in0=ss[0:np_, 2:3], scalar1=rs, scalar2=None, op0=ALU.mult),
                  reads=[rss, self.rC], writes=[rss])
        cx.op("dve", lambda h: h.scalar_tensor_tensor(out=hn[0:np_, :], in0=xin, scalar=ss[0:np_, 2:3], in1=self.grep[0:np_, :],
                                                      op0=ALU.mult, op1=ALU.mult), reads=[rxin, rss, self.rG], writes=[rhn])
        def stage_b():
            for half in range(2):
                bi = pbanks[half]
                pb = self.ps[bi][:].bitcast(BF16)
                for k in range(8):
                    kt = half * 8 + k
                    cx.op("pe", lambda h: h.transpose(out=pb[:, k * 128:k * 128 + np_], in_=hn[0:np_, kt * 128:(kt + 1) * 128],
                                                      identity=self.ident[0:np_, 0:np_]),
                          reads=[rhn, self.rC], writes=[self.rP[bi]], inc=(k == 7))
                src = pb[:, 0:1024].rearrange("p (a b) -> p a b", b=128)[:, :, 0:np_]
                dst = dstT_fn(half * 8, 8)
                eng = "act" if half == 0 else "dve"
                if eng == "act":
                    cx.op("act", lambda h: h.activation(out=dst, in_=src, func=AF.Copy), reads=[self.rP[bi]], writes=[rdst])
                else:
                    cx.op("dve", lambda h: h.tensor_copy(dst, src), reads=[self.rP[bi]], writes=[rdst])
        if defer is None:
            stage_b()
        else:
            defer.append(stage_b)

    def norm_tmp(self):
        ar = self.ar
        t = {}
        t["junk"] = ar.bf(D)
        t["rj"] = self.sres("junk")
        t["ss"] = ar.f32(8)
        t["rss"] = self.sres("ss")
        return t

    def finish(self):
        cx = self.cx
        cx.barrier(cx.all_res)
        cx.wait_all("pool", cx.all_res)

    def tt_copy_T(self, src_bf, rsrc, nblk, w, dst_fn, rdst, bank, eng):
        cx = self.cx
        pb = self.ps[bank][:].bitcast(BF16)
        for j in range(nblk):
            cx.op("pe", lambda h: h.transpose(out=pb[0:w, j * 128:(j + 1) * 128], in_=src_bf[:, j, :], identity=self.ident[:]),
                  reads=[rsrc, self.rC], writes=[self.rP[bank]], inc=(j == nblk - 1))
        src = pb[0:w, 0:nblk * 128].rearrange("p (a b) -> p a b", b=128)
        if eng == "act":
            cx.op("act", lambda h: h.activation(out=dst_fn(), in_=src, func=AF.Copy), reads=[self.rP[bank]], writes=[rdst])
        else:
            cx.op("dve", lambda h: h.tensor_copy(dst_fn(), src), reads=[self.rP[bank]], writes=[rdst])

    def _defer_T(self, defer, src_bf, rsrc, nblk, w, dst_fn, rdst, bank, eng):
        qsi = self._qsi

        def run():
            saved = self._qsi
            self._qsi = qsi
            self.tt_copy_T(src_bf, rsrc, nblk, w, dst_fn, rdst, bank, eng)
            self._qsi = saved
        defer.append(run)

    def phase1(self, l):
        cx, ar = self.cx, self.ar
        self.phase_barrier()
        if l == 0:
            self.load_gain(self.g_attn[l:l + 1, :])
        t = self.norm_tmp()
        hns = [ar.bf(D) for _ in range(2)]
        rhn = [self.sres("hn%d" % i) for i in range(2)]
        for tt in range(NT if l == 0 else 0):
            t["hn"], t["rhn"] = hns[tt % 2], rhn[tt % 2]
            t["pb"] = (6, 7) if tt % 2 == 0 else (4, 5)
            self.norm_transpose(self.X[:, tt, :], self.rX[tt], 128,
                                lambda k0, n, tt=tt: self.actT[:, k0:k0 + n, tt * 128:(tt + 1) * 128],
                                self.rAT[tt], t)
        if self.stop == "p1a":
            return
        self.phase_barrier()
        tabs = [ar.f32(448) for _ in range(3)]
        rtab = [self.sres("tab%d" % i) for i in range(3)]
        gc = ar.f32(1024).rearrange("p (a b) -> p a b", b=512)
        rgc = self.sres("gc")
        cx.dma("sp", [(gc[:, 0, :], self.g_cq[l:l + 1, :].partition_broadcast(128)),
                      (gc[:, 1, :], self.g_ckv[l:l + 1, :].partition_broadcast(128))], writes=[rgc])
        stg = [ar.f32(512) for _ in range(3)]
        rstg = [self.sres("stg%d" % i) for i in range(3)]
        tmp = [ar.f32(512) for _ in range(3)]
        rtmp = [self.sres("tmp%d" % i) for i in range(3)]
        roped = [ar.bf(512) for _ in range(3)]
        rrop = [self.sres("roped%d" % i) for i in range(3)]
        st = ar.f32(64)
        rst = self.sres("st")
        qstage_l = [ar.bf(4096) for _ in range(2)]
        rqs_l = [self.sres("qstage%d" % i) for i in range(2)]
        ckvnT = ar.bf(4096).rearrange("p (a b) -> p a b", b=T)
        cqnT = ckvnT
        rckv = self.sres("ckvnT")
        rcq = rckv
        self._qsi = 0
        self._qsn = 0

        class _QS:
            def __getitem__(s_, key):
                return qstage_l[self._qsi][key]

            def rearrange(s_, *a, **k):
                return qstage_l[self._qsi].rearrange(*a, **k)
        qstage = _QS()

        class _QV:
            def __init__(s_, b):
                s_.b = b

            def __getitem__(s_, key):
                return qstage_l[self._qsi].rearrange("p (a b) -> p a b", b=s_.b)[key]
        qs4 = _QV(T)
        vs8 = _QV(512)

        class _RQ:
            pass

        def cur_rqs():
            return rqs_l[self._qsi]

        def next_stage():
            self._qsn += 1
            self._qsi = self._qsn % 2
            return self._qsi
        self._cnt = 0

        def region(key, sub, n):
            g, off = KVREG[key]
            return self.kvb[g].rearrange("p c -> (p c)")[off + sub:off + sub + n]

        def exchange(g):
            cx.collective(self.kvb[g][:, :], self.kvg[g][:, :], reads=[self.r_kvb[k] for k in KVGRP_KEYS[g]],
                          writes=[self.r_kvg[g]])

        def load_tab(tt):
            i = self._cnt % 3
            self._cnt += 1
            cx.dma("sp", [(tabs[i], self.tab_d[:, tt, :])], writes=[rtab[i]])
            return tabs[i], rtab[i]

        mm_banks = [0, 1, 2, 3]
        tr_banks = [4, 5, 6, 7]
        self._mb = 0
        self._tb = 0

        def next_mb():
            b = mm_banks[self._mb % 4]
            self._mb += 1
            return b

        def next_tb():
            b = tr_banks[self._tb % 4]
            self._tb += 1
            return b

        def rope_small(src3, rsrc_list, nh, w, ctab, stab, tb, rtb, out3, rout, i):
            hw = w // 2
            t1 = tmp[0][:, 0:nh * w].rearrange("p (a b) -> p a b", b=w)
            t2 = tmp[1][:, 0:nh * w].rearrange("p (a b) -> p a b", b=w)
            cb = tb[:, ctab:ctab + w].unsqueeze(1).to_broadcast([128, nh, w])
            s0 = tb[:, stab:stab + hw].unsqueeze(1).to_broadcast([128, nh, hw])
            s1 = tb[:, stab + hw:stab + w].unsqueeze(1).to_broadcast([128, nh, hw])
            cx.op("dve", lambda h: h.tensor_tensor(out=t1, in0=src3, in1=cb, op=ALU.mult),
                  reads=rsrc_list + [rtb], writes=[rtmp[0]])
            cx.op("dve", lambda h: h.tensor_tensor(out=t2[:, :, 0:hw], in0=src3[:, :, hw:w], in1=s0, op=ALU.mult),
                  reads=rsrc_list + [rtb], writes=[rtmp[1]])
            cx.op("dve", lambda h: h.tensor_tensor(out=t2[:, :, hw:w], in0=src3[:, :, 0:hw], in1=s1, op=ALU.mult),
                  reads=rsrc_list + [rtb], writes=[rtmp[1]])
            cx.op("dve", lambda h: h.tensor_tensor(out=out3, in0=t1, in1=t2, op=ALU.add),
                  reads=[rtmp[0], rtmp[1]], writes=[rout])

        def latent_norm(bank, tt, i, gidx, cT, rcT, defer):
            s_, rs_ = stg[i], rstg[i]
            cx.op("act", lambda h: h.activation(out=s_, in_=self.ps[bank][:, 0:512], func=AF.Copy),
                  reads=[self.rP[bank]], writes=[rs_])
            cx.op("dve", lambda h: h.tensor_tensor(out=tmp[0], in0=s_, in1=s_, op=ALU.mult), reads=[rs_], writes=[rtmp[0]])
            cx.op("dve", lambda h: h.tensor_reduce(out=st[:, 0:1], in_=tmp[0], axis=AX.X, op=ALU.add), reads=[rtmp[0]], writes=[rst])
            cx.op("act", lambda h: h.activation(out=st[:, 1:2], in_=st[:, 0:1], func=AF.Sqrt, scale=1.0 / 512, bias=EPS),
                  reads=[rst], writes=[rst])
            cx.op("dve", lambda h: h.reciprocal(out=st[:, 2:3], in_=st[:, 1:2]), reads=[rst], writes=[rst])
            cx.op("dve", lambda h: h.scalar_tensor_tensor(out=roped[i], in0=s_, scalar=st[:, 2:3], in1=gc[:, gidx, :],
                                                          op0=ALU.mult, op1=ALU.mult), reads=[rs_, rst, rgc], writes=[rrop[i]])
            self._defer_T(defer, roped[i].rearrange("p (a b) -> p a b", b=128), rrop[i], 4, 128,
                          lambda: cT[:, 0:4, tt * 128:(tt + 1) * 128], rcT, next_tb(), "act")

        cstate = {}
        self._pc = 0

        def chunk_mm(c0, ncol, kind, tt):
            if tt == 0:
                W, rWs = self.wget()
                cstate[kind] = (W[:, 0:16 * ncol].rearrange("p (k n) -> p k n", n=ncol), rWs, {}, [None])
            Wv, rWs, banks, _q = cstate[kind]
            bank = next_mb()
            banks[tt] = bank
            for kt in range(16):
                cx.op("pe", lambda h: h.matmul(self.ps[bank][:, 0:ncol], self.actT[:, kt, tt * 128:(tt + 1) * 128], Wv[:, kt, :],
                                               start=(kt == 0), stop=(kt == 15)),
                      reads=[self.rAT[tt], rWs], writes=[self.rP[bank]], inc=(kt == 15))

        def chunk_post(c0, ncol, kind, tt, defer):
            if True:
                i = self._pc % 3
                self._pc += 1
                bank = cstate[kind][2][tt]
                if cstate[kind][3][0] is None:
                    cstate[kind][3][0] = next_stage()
                self._qsi = cstate[kind][3][0]
                psb = self.ps[bank]
                rpb = self.rP[bank]
                if kind in ("av", "bv"):
                    cx.op("act", lambda h: h.activation(out=vs8[:, tt, 0:ncol], in_=psb[:, 0:ncol], func=AF.Copy),
                          reads=[rpb], writes=[cur_rqs()])
                elif kind == "ckv":
                    latent_norm(bank, tt, i, 1, ckvnT, rckv, defer)
                elif kind == "cq":
                    latent_norm(bank, tt, i, 0, cqnT, rcq, defer)
                elif kind == "ckr":
                    tb, rtb = load_tab(tt)
                    src3 = psb[:, 0:64].rearrange("p (a b) -> p a b", b=64)
                    out3 = roped[i][:, 0:64].rearrange("p (a b) -> p a b", b=64)
                    rope_small(src3, [rpb], 1, 64, TAB_CM, TAB_SM, tb, rtb, out3, rrop[i], i)
                    self._defer_T(defer, out3, rrop[i], 1, 64,
                                   lambda: qstage[0:64, tt * 128:(tt + 1) * 128].rearrange("p (a b) -> p a b", b=128),
                                   cur_rqs(), next_tb(), "act")
                elif kind in ("bk", "bq0", "bq1", "bq2"):
                    tb, rtb = load_tab(tt)
                    ps3 = psb[:, 0:512].rearrange("p (a b) -> p a b", b=128)
                    r3 = roped[i].rearrange("p (a b) -> p a b", b=128)
                    cx.op("act", lambda h: h.activation(out=r3, in_=ps3, func=AF.Copy), reads=[rpb], writes=[rrop[i]])
                    rope_small(ps3[:, :, 0:32], [rpb], 4, 32, TAB_CB, TAB_SB, tb, rtb, r3[:, :, 0:32], rrop[i], i)
                    self._defer_T(defer, r3, rrop[i], 4, 128, lambda: qs4[:, 0:4, tt * 128:(tt + 1) * 128], cur_rqs(), next_tb(), "act")
                elif kind in ("a0", "a1"):
                    tb, rtb = load_tab(tt)
                    s_, rs_ = stg[i], rstg[i]
                    s3 = s_.rearrange("p (a b) -> p a b", b=128)
                    t0 = tmp[0].rearrange("p (a b) -> p a b", b=128)
                    cx.op("act", lambda h: h.activation(out=s_, in_=psb[:, 0:512], func=AF.Copy), reads=[rpb], writes=[rs_])
                    cx.op("dve", lambda h: h.tensor_tensor(out=tmp[0], in0=s_, in1=s_, op=ALU.mult), reads=[rs_], writes=[rtmp[0]])
                    cx.op("dve", lambda h: h.tensor_reduce(out=st[:, 0:4], in_=t0, axis=AX.X, op=ALU.add), reads=[rtmp[0]], writes=[rst])
                    cx.op("act", lambda h: h.activation(out=st[:, 4:8], in_=st[:, 0:4], func=AF.Sqrt, scale=1.0 / 128, bias=EPS),
                          reads=[rst], writes=[rst])
                    cx.op("dve", lambda h: h.reciprocal(out=st[:, 8:12], in_=st[:, 4:8]), reads=[rst], writes=[rst])
                    cx.op("dve", lambda h: h.tensor_tensor(out=t0, in0=s3, in1=st[:, 8:12].unsqueeze(2).to_broadcast([128, 4, 128]),
                                                           op=ALU.mult), reads=[rs_, rst], writes=[rtmp[0]])
                    x3 = tmp[2].rearrange("p (a b) -> p a b", b=128)
                    if kind == "a0":
                        cx.op("dve", lambda h: h.tensor_tensor(out=x3, in0=t0, in1=self.gqk[:, 0, :].unsqueeze(1).to_broadcast([128, 4, 128]),
                                                               op=ALU.mult), reads=[rtmp[0], self.rLP], writes=[rtmp[2]])
                    else:
                        for jj in range(2):
                            cx.op("dve", lambda h: h.tensor_tensor(out=x3[:, 2 * jj:2 * jj + 2, :], in0=t0[:, 2 * jj:2 * jj + 2, :],
                                                                   in1=self.gqk[:, jj, :].unsqueeze(1).to_broadcast([128, 2, 128]),
                                                                   op=ALU.mult), reads=[rtmp[0], self.rLP], writes=[rtmp[2]])
                    t1 = tmp[0].rearrange("p (a b) -> p a b", b=128)
                    cx.op("dve", lambda h: h.tensor_tensor(out=t1, in0=x3, in1=tb[:, TAB_CA:TAB_CA + 128].unsqueeze(1).to_broadcast([128, 4, 128]),
                                                           op=ALU.mult), reads=[rtmp[2], rtb], writes=[rtmp[0]])
                    x5 = tmp[2].rearrange("p (h f q e) -> p h f q e", h=4, f=2, q=2)
                    t5 = tmp[1].rearrange("p (h f q e) -> p h f q e", h=4, f=2, q=2)
                    sa = tb[:, TAB_SA:TAB_SA + 128].rearrange("p (f q e) -> p f q e", f=2, q=2)
                    for q in range(2):
                        sab = sa[:, :, q, :].unsqueeze(1).to_broadcast([128, 4, 2, 32])
                        cx.op("dve", lambda h: h.tensor_tensor(out=t5[:, :, :, q, :], in0=x5[:, :, :, 1 - q, :], in1=sab, op=ALU.mult),
                              reads=[rtmp[2], rtb], writes=[rtmp[1]])
                    r3 = roped[i].rearrange("p (a b) -> p a b", b=128)
                    cx.op("dve", lambda h: h.tensor_tensor(out=roped[i], in0=tmp[0], in1=tmp[1], op=ALU.add),
                          reads=[rtmp[0], rtmp[1]], writes=[rrop[i]])
                    self._defer_T(defer, r3, rrop[i], 4, 128, lambda: qs4[:, 0:4, tt * 128:(tt + 1) * 128], cur_rqs(), next_tb(), "act")
                else:
                    raise ValueError(kind)
        def chunk_store(c0, ncol, kind):
            self._qsi = cstate[kind][3][0]
            if kind == "av":
                cx.dma("sp", [(region("va", 0, T * 256).rearrange("(tt p c) -> p tt c", p=128, c=256), vs8[:, :, 0:256])],
                       reads=[cur_rqs()], writes=[self.r_kvb["va"]])
            elif kind == "bv":
                cx.dma("sp", [(region("vb", 0, T * 512).rearrange("(tt p c) -> p tt c", p=128, c=512), vs8[:, :, 0:512])],
                       reads=[cur_rqs()], writes=[self.r_kvb["vb"]])
            elif kind == "ckr":
                cx.dma("sp", [(region("kr", 0, 64 * T).rearrange("(p t) -> p t", t=T), qstage[0:64, 0:T])],
                       reads=[cur_rqs()], writes=[self.r_kvb["kr"]])
            elif kind == "bk":
                cx.dma("sp", [(region("kb", 0, 4 * 128 * T).rearrange("(h p t) -> p h t", p=128, t=T), qs4[:, 0:4, :])],
                       reads=[cur_rqs()], writes=[self.r_kvb["kb"]])
            elif kind in ("bq0", "bq1", "bq2"):
                g = int(kind[2])
                cx.dma("sp", [(self.qb_d[4 * g:4 * g + 4].rearrange("h p t -> p h t"), qs4[:, 0:4, :])],
                       reads=[cur_rqs()], writes=[self.r_qd["qb"]])
            elif kind == "a0":
                cx.dma("sp", [(self.qa_d[0:4].rearrange("h p t -> p h t"), qs4[:, 0:4, :])],
                       reads=[cur_rqs()], writes=[self.r_qd["qa"]])
            elif kind == "a1":
                cx.dma("sp", [(self.qa_d[4:6].rearrange("h p t -> p h t"), qs4[:, 0:2, :])],
                       reads=[cur_rqs()], writes=[self.r_qd["qa"]])
                cx.dma("sp", [(region("ka", 0, 2 * 128 * T).rearrange("(h p t) -> p h t", p=128, t=T), qs4[:, 2:4, :])],
                       reads=[cur_rqs()], writes=[self.r_kvb["ka"]])

        def latent_kv_up():
            W, rWs = self.wget()
            Wv = W[:, 0:4 * 1536].rearrange("p (k n) -> p k n", n=1536)
            for hg, heads in enumerate(((0, 1, 2, 3), (4, 5))):
                next_stage()
                for j, hd in enumerate(heads):
                    for c2 in range(2):
                        bank = next_mb()
                        for kt in range(4):
                            cx.op("pe", lambda h: h.matmul(self.ps[bank][:, 0:512], Wv[:, kt, hd * 256:hd * 256 + 128],
                                                           ckvnT[:, kt, c2 * 512:(c2 + 1) * 512], start=(kt == 0), stop=(kt == 3)),
                                  reads=[rckv, rWs], writes=[self.rP[bank]], inc=(kt == 3))
                        cx.op("act", lambda h: h.activation(out=qs4[:, j, c2 * 512:(c2 + 1) * 512], in_=self.ps[bank][:, 0:512], func=AF.Copy),
                              reads=[self.rP[bank]], writes=[cur_rqs()])
                nh = len(heads)
                cx.dma("sp", [(region("kc", hg * 4 * 128 * T, nh * 128 * T).rearrange("(h p t) -> p h t", p=128, t=T), qs4[:, 0:nh, :])],
                       reads=[cur_rqs()], writes=[self.r_kvb["kc"]])
            exchange(1)
            W6 = Wv.rearrange("p k (h c) -> p k h c", c=256)
            vc_reg = region("vc", 0, T * 768).rearrange("(tt p c) -> p tt c", p=128, c=768)
            for half in range(2):
                next_stage()
                vs = qstage[:, 0:8 * 384].rearrange("p (a b) -> p a b", b=384)
                for tt in range(NT):
                    bank = next_mb()
                    for kt in range(4):
                        cx.op("pe", lambda h: h.matmul(self.ps[bank][:, 0:384].rearrange("p (a b) -> p a b", b=128),
                                                       ckvnT[:, kt, tt * 128:(tt + 1) * 128],
                                                       W6[:, kt, 3 * half:3 * half + 3, 128:256], start=(kt == 0), stop=(kt == 3)),
                              reads=[rckv, rWs], writes=[self.rP[bank]], inc=(kt == 3))
                    cx.op("act", lambda h: h.activation(out=vs[:, tt, :], in_=self.ps[bank][:, 0:384], func=AF.Copy),
                          reads=[self.rP[bank]], writes=[cur_rqs()])
                cx.dma("sp", [(vc_reg[:, :, half * 384:(half + 1) * 384], vs)], reads=[cur_rqs()], writes=[self.r_kvb["vc"]])
            exchange(3)

        def latent_q_up():
            W, rWs = self.wget()
            Wv = W[:, 0:4 * 1152].rearrange("p (k n) -> p k n", n=1152)
            for hg, heads in enumerate(((0, 1, 2, 3), (4, 5))):
                next_stage()
                for j, hd in enumerate(heads):
                    for c2 in range(2):
                        bank = next_mb()
                        for kt in range(4):
                            cx.op("pe", lambda h: h.matmul(self.ps[bank][:, 0:512], Wv[:, kt, hd * 192:hd * 192 + 128],
                                                           cqnT[:, kt, c2 * 512:(c2 + 1) * 512], start=(kt == 0), stop=(kt == 3)),
                                  reads=[rcq, rWs], writes=[self.rP[bank]], inc=(kt == 3))
                        cx.op("act", lambda h: h.activation(out=qs4[:, j, c2 * 512:(c2 + 1) * 512], in_=self.ps[bank][:, 0:512], func=AF.Copy),
                              reads=[self.rP[bank]], writes=[cur_rqs()])
                nh = len(heads)
                cx.dma("sp", [(self.qcn_d[4 * hg:4 * hg + nh].rearrange("h p t -> p h t"), qs4[:, 0:nh, :])],
                       reads=[cur_rqs()], writes=[self.r_qd["qcn"]])
            W6 = Wv.rearrange("p k (h c) -> p k h c", c=192)
            for tt in range(NT):
                i = tt % 3
                tb, rtb = load_tab(tt)
                bank = next_mb()
                for kt in range(4):
                    cx.op("pe", lambda h: h.matmul(self.ps[bank][:, 0:384].rearrange("p (a b) -> p a b", b=64),
                                                   cqnT[:, kt, tt * 128:(tt + 1) * 128], W6[:, kt, :, 128:192],
                                                   start=(kt == 0), stop=(kt == 3)),
                          reads=[rcq, rWs], writes=[self.rP[bank]], inc=(kt == 3))
                src3 = self.ps[bank][:, 0:384].rearrange("p (a b) -> p a b", b=64)
                out3 = roped[i][:, 0:384].rearrange("p (a b) -> p a b", b=64)
                rope_small(src3, [self.rP[bank]], 6, 64, TAB_CM, TAB_SM, tb, rtb, out3, rrop[i], i)
                self.tt_copy_T(out3, rrop[i], 6, 64, lambda: self.actT[0:64, 0:6, tt * 128:(tt + 1) * 128],
                               self.rAT[tt], next_tb(), "act")
            cx.dma("sp", [(self.qcr_d.rearrange("h p t -> p h t"), self.actT[0:64, 0:6, :])],
                   reads=self.rAT, writes=[self.r_qd["qcr"]])

        lim = getattr(self, "p1_limit", None)
        items = []
        for ci_, (c0, ncol, kind) in enumerate(IN_CHUNKS):
            if lim is not None and ci_ >= lim:
                break
            for tt in range(NT):
                items.append((c0, ncol, kind, tt))

        def after_chunk(kind):
            if kind == "av":
                exchange(2)
            if kind == "a1":
                exchange(0)
                latent_kv_up()

        pendA = []
        pendB = []

        def do_A(it):
            d_ = []
            chunk_post(*it, d_)
            pendB.append((it, d_))

        def do_B():
            it, d_ = pendB.pop(0)
            for f in d_:
                f()
            if it[3] == NT - 1:
                chunk_store(it[0], it[1], it[2])
                after_chunk(it[2])

        def drain():
            while pendA:
                do_A(pendA.pop(0))
            while pendB:
                do_B()

        for it in items:
            if pendA and pendA[-1][2] == "a1" and it[2] != "a1":
                drain()
            chunk_mm(*it)
            pendA.append(it)
            while len(pendA) > 1:
                do_A(pendA.pop(0))
            while len(pendB) > 1:
                do_B()
        drain()
        if lim is not None:
            return
        latent_q_up()
        if "kv" in self.dbg:
            for g in range(4):
                r = cx.res("kvdbg%d" % g)
                cx.dma("sp", [(self.kvdbg[g][:, :], self.kvg[g][:, :])], reads=[self.r_kvg[g]], writes=[r])

    def kv_region(self, rank, key, sub, n):
        g, off = KVREG[key]
        off += sub
        if rank is None:
            return self.kvb[g].rearrange("p c -> (p c)")[off:off + n]
        base = rank * 128 * KVG_COLS[g]
        return self.kvg[g].rearrange("p c -> (p c)")[base + off:base + off + n]

    def rkvg(self, key):
        return self.r_kvg[KVREG[key][0]]

    def group_norm_out(self, stat_bank, width, heads_ft, on_fn, ron, c2, tmpA, rtmpA):
        cx = self.cx
        lnb, rstd = tmpA
        rlnb, rrstd = rtmpA
        cx.op("act", lambda h: h.activation(out=lnb, in_=self.ps[stat_bank][:, 0:512], func=AF.Ln, scale=1.0 / width, bias=EPS),
              reads=[self.rP[stat_bank]], writes=[rlnb])
        cx.op("act", lambda h: h.activation(out=rstd, in_=lnb, func=AF.Exp, scale=-0.5), reads=[rlnb], writes=[rrstd])
        for (hh, ft) in heads_ft:
            cx.op("dve", lambda h: h.scalar_tensor_tensor(out=self.actT[:, ft, c2 * 512:(c2 + 1) * 512], in0=on_fn(hh),
                                                          scalar=self.gout[:, ft:ft + 1], in1=rstd, op0=ALU.mult, op1=ALU.mult),
                  reads=[ron, rrstd, self.rLP], writes=self.rAT[4 * c2:4 * c2 + 4])

    def phase2(self, l):
        cx, ar = self.cx, self.ar
        self.phase_barrier()
        KT = [ar.bf(2048) for _ in range(2)]
        rKT = [self.sres("KT%d" % i) for i in range(2)]
        KrT = ar.bf(2048)
        rKr = self.sres("KrT")
        V = [ar.bf(2048).rearrange("p (a b) -> p a b", b=128) for _ in range(2)]
        rV = [self.sres("V%d" % i) for i in range(2)]
        Qn = [ar.bf(512) for _ in range(2)]
        Qr = [ar.bf(512) for _ in range(2)]
        rQ = [self.sres("Q%d" % i) for i in range(2)]
        NPT = 3
        PT = [ar.bf(512) for _ in range(NPT)]
        rPT = [self.sres("PT%d" % i) for i in range(NPT)]
        accP = [ar.f32(512) for _ in range(2)]
        raccP = [self.sres("accP%d" % i) for i in range(2)]
        onb = ar.f32(3072).rearrange("p (a b) -> p a b", b=512)
        ron = self.sres("onb")
        lnb = ar.f32(512)
        rden = ar.f32(512)
        rln = self.sres("lnb")
        rrd = self.sres("rden")
        sq = [ar.bf(512) for _ in range(2)]
        rsq = [self.sres("sq%d" % i) for i in range(2)]
        S_B = [0, 1, 2]
        O_B = [3, 4]
        U_B = [5, 6]
        STAT = 7
        rzp = self.sres("zpad")
        cx.op("dve", lambda h: h.memset(KrT[64:128, :], 0.0), writes=[rKr])
        for i_ in range(2):
            cx.op("dve", lambda h: h.memset(Qr[i_][64:128, :], 0.0), writes=[rQ[i_]])
        self._hc = 0
        self._kvc = 0
        pending = []

        def flush_stat(upto=3):
            for item in pending:
                while item and (3 - len(item)) < upto:
                    item.pop(0)()
            while pending and not pending[0]:
                pending.pop(0)

        def dense_core(hh, nh_in_mixer, first, last, KTb, rKTb, Vb, rVb, qi, scale, rope):
            k = self._hc
            self._hc += 1
            ob, ub = O_B[k % 2], U_B[k % 2]

            def emit_S(j):
                sbk = S_B[j % 3]
                cx.op("pe", lambda h: h.matmul(self.ps[sbk][:, 0:512], KTb[:, j * 128:(j + 1) * 128], Qn[qi][:, 0:512],
                                               start=True, stop=(not rope)),
                      reads=[rKTb, rQ[qi]], writes=[self.rP[sbk]], inc=(not rope))
                if rope:
                    cx.op("pe", lambda h: h.matmul(self.ps[sbk][:, 0:512], KrT[:, j * 128:(j + 1) * 128], Qr[qi][:, 0:512],
                                                   start=False, stop=True),
                          reads=[rKr, rQ[qi]], writes=[self.rP[sbk]])
            emit_S(0)
            emit_S(1)
            accp, raccp = accP[k % 2], raccP[k % 2]
            for j in range(16):
                sbk = S_B[j % 3]
                p = (k * 16 + j) % NPT
                cx.op("act", lambda h: h.activation(out=PT[p], in_=self.ps[sbk][:, 0:512], func=AF.Exp, scale=scale),
                      reads=[self.rP[sbk]], writes=[rPT[p]])
                if j + 2 < 16:
                    emit_S(j + 2)
                cx.op("pe", lambda h: h.matmul(self.ps[ob][:, 0:512], Vb[:, j, :], PT[p], start=(j == 0), stop=(j == 15)),
                      reads=[rVb, rPT[p]], writes=[self.rP[ob]], inc=True)
                if j % 2 == 0:
                    if j == 0:
                        cx.op("dve", lambda h: h.tensor_copy(accp, PT[p]), reads=[rPT[p]], writes=[raccp])
                    else:
                        cx.op("dve", lambda h: h.tensor_tensor(out=accp, in0=PT[p], in1=accp, op=ALU.add), reads=[rPT[p]], writes=[raccp])
                else:
                    cx.op("pe", lambda h: h.matmul(self.ps[ub][:, 0:512], self.ones[:], PT[p], start=(j == 1), stop=False),
                          reads=[self.rC, rPT[p]], writes=[self.rP[ub]], inc=True)
                if j == 3:
                    flush_stat(1)
                elif j == 5:
                    flush_stat(2)
                elif j == 9:
                    flush_stat(3)
            s = k % 2

            def st1(ub=ub, accp=accp, raccp=raccp):
                cx.op("pe", lambda h: h.matmul(self.ps[ub][:, 0:512], self.ones_f[:], accp, start=False, stop=True),
                      reads=[self.rC, raccp], writes=[self.rP[ub]], inc=True)

            def st2(hh=hh, ob=ob, ub=ub):
                cx.op("act", lambda h: h.activation(out=lnb, in_=self.ps[ub][:, 0:512], func=AF.Ln), reads=[self.rP[ub]], writes=[rln])
                cx.op("act", lambda h: h.activation(out=rden, in_=lnb, func=AF.Exp, scale=-1.0), reads=[rln], writes=[rrd])
                cx.op("dve", lambda h: h.tensor_tensor(out=onb[:, hh, :], in0=self.ps[ob][:, 0:512], in1=rden, op=ALU.mult),
                      reads=[self.rP[ob], rrd], writes=[ron])

            def st3(hh=hh, s=s, first=first, last=last):
                cx.op("dve", lambda h: h.tensor_tensor(out=sq[s], in0=onb[:, hh, :], in1=onb[:, hh, :], op=ALU.mult), reads=[ron], writes=[rsq[s]])
                cx.op("pe", lambda h: h.matmul(self.ps[STAT][:, 0:512], self.ones[:], sq[s], start=first, stop=last),
                      reads=[self.rC, rsq[s]], writes=[self.rP[STAT]], inc=True)
            pending.append([st1, st2, st3])

        def load_kv(which, hd):
            i = self._kvc % 2
            self._kvc += 1
            kk = "ka" if which == "a" else "kc"
            vk, wv = ("va", 256) if which == "a" else ("vc", 768)
            pk = []
            pv = []
            for r in range(2):
                ksrc = self.kv_region(r, kk, hd * 128 * T, 128 * T).rearrange("(p t) -> p t", t=T)
                pk.append((KT[i][:, r * T:(r + 1) * T], ksrc))
                vsrc = self.kv_region(r, vk, 0, T * wv).rearrange("(tt p c) -> p tt c", p=128, c=wv)[:, :, hd * 128:(hd + 1) * 128]
                pv.append((V[i][:, r * 8:(r + 1) * 8, :], vsrc))
            cx.dma("sp", pk, reads=[self.rkvg(kk)], writes=[rKT[i]])
            cx.dma("sp", pv, reads=[self.rkvg(vk)], writes=[rV[i]])
            return KT[i], rKT[i], V[i], rV[i]

        self._qc = 0

        def load_q(which, hd, c2):
            i = self._qc % 2
            self._qc += 1
            if which == "a":
                cx.dma("sp", [(Qn[i], self.qa_d[hd, :, c2 * 512:(c2 + 1) * 512])], reads=[self.r_qd["qa"]], writes=[rQ[i]])
            else:
                cx.dma("sp", [(Qn[i], self.qcn_d[hd, :, c2 * 512:(c2 + 1) * 512]),
                              (Qr[i][0:64, :], self.qcr_d[hd, :, c2 * 512:(c2 + 1) * 512])],
                       reads=[self.r_qd["qcn"], self.r_qd["qcr"]], writes=[rQ[i]])
            return i

        if "skipA" not in self.dbg:
            sc_a = 128.0 ** -0.5
            for c2 in range(2):
                for g in range(2):
                    KTb, rKTb, Vb, rVb = load_kv("a", g)
                    for hh in range(3 * g, 3 * g + 3):
                        qi = load_q("a", hh, c2)
                        dense_core(hh, 6, hh == 0, hh == 5, KTb, rKTb, Vb, rVb, qi, sc_a, False)
                flush_stat()
                self.group_norm_out(STAT, 768, [(hh, hh) for hh in range(6)], lambda hh: onb[:, hh, :], ron, c2,
                                    (lnb, rden), (rln, rrd))
        if "skipC" not in self.dbg:
            sc_c = 192.0 ** -0.5
            pk = []
            for r in range(2):
                pk.append((KrT[0:64, r * T:(r + 1) * T], self.kv_region(r, "kr", 0, 64 * T).rearrange("(p t) -> p t", t=T)))
            cx.dma("sp", pk, reads=[self.rkvg("kr")], writes=[rKr])
            for c2 in range(2):
                for hh in range(6):
                    KTb, rKTb, Vb, rVb = load_kv("c", hh)
                    qi = load_q("c", hh, c2)
                    dense_core(hh, 6, hh == 0, hh == 5, KTb, rKTb, Vb, rVb, qi, sc_c, True)
                flush_stat()
                self.group_norm_out(STAT, 768, [(hh, 10 + hh) for hh in range(6)], lambda hh: onb[:, hh, :], ron, c2,
                                    (lnb, rden), (rln, rrd))
        if "skipB" not in self.dbg:
            self.mixer_b(l)
        if "y" in self.dbg:
            r = cx.res("ydbg")
            cx.dma("sp", [(self.ydbg.rearrange("f p t -> p f t"), self.actT[:, :, :])], reads=self.rAT, writes=[r])

    def mixer_b(self, l):
        cx, ar = self.cx, self.ar
        self.phase_barrier()
        kte_w = ar.f32(1536)
        KTe = kte_w.bitcast(BF16)
        rKTe = self.sres("KTe")
        Vt_l = [ar.bf(32 * 128) for _ in range(2)]
        rVt_l = [self.sres("Vt%d" % i) for i in range(2)]
        Qb = [ar.bf(1024) for _ in range(2)]
        rQb = [self.sres("Qb%d" % i) for i in range(2)]
        Pm = [ar.bf(512) for _ in range(2)]
        rPm = [self.sres("Pm%d" % i) for i in range(2)]
        accO = ar.f32(4096).rearrange("p (a b) -> p a b", b=T)
        raccO = [self.sres("accO%d" % i) for i in range(4)]
        accS = ar.f32(1024)
        raccS = self.sres("accS")
        lnb = kte_w[:, 0:512]
        rstd = kte_w[:, 512:1024]
        rln = self.sres("lnbB")
        rrs = self.sres("rstdB")
        sq = [ar.bf(512) for _ in range(2)]
        rsq = [self.sres("sqB%d" % i) for i in range(2)]
        S_B = [0, 1]
        O_B = [2, 3]
        U_B = [4, 5]
        STAT = [6, 7]
        scale = 128.0 ** -0.5
        mb1 = self.bmask[:, 0:128]
        mb2 = self.bmask[:, 128:256]
        nqc = 0
        cnt = {"sb": 0, "bg": 0}
        pend_norm = []
        for hd in range(4):
            pk = [(KTe[:, 0:T], self.kv_region(0, "kb", hd * 128 * T, 128 * T).rearrange("(p t) -> p t", t=T)),
                  (KTe[:, T:2 * T], self.kv_region(None, "kb", hd * 128 * T, 128 * T).rearrange("(p t) -> p t", t=T)),
                  (KTe[:, 2 * T:3 * T], self.kv_region(1, "kb", hd * 128 * T, 128 * T).rearrange("(p t) -> p t", t=T))]
            cx.dma("sp", pk, reads=[self.rkvg("kb"), self.r_kvb["kb"]], writes=[rKTe])
            hc = slice(hd * 128, (hd + 1) * 128)
            for g, d in enumerate((1, 4, 16)):
                Lh = T // d
                nqb = Lh // 64
                QN = min(128, Lh)
                nqt = Lh // QN
                ntile = (nqb + 3) // 2
                qi = nqc % 2
                Vt, rVt = Vt_l[nqc % 2], rVt_l[nqc % 2]
                nqc += 1
                cx.dma("sp", [(Qb[qi], self.qb_d[g * 4 + hd])], reads=[self.r_qd["qb"]], writes=[rQb[qi]])
                Vt4 = Vt[:, 0:d * ntile * 128].rearrange("p (r u c) -> p r u c", r=d, u=ntile)
                vG0 = self.kv_region(0, "vb", 0, T * 512).rearrange("(t c) -> t c", c=512)
                vG1 = self.kv_region(1, "vb", 0, T * 512).rearrange("(t c) -> t c", c=512)
                vOwn = self.kv_region(None, "vb", 0, T * 512)
                mr = nqb + 1
                er, ur = mr % 2, mr // 2
                pairs = [(Vt4[0:64, :, 0, :], vG0[T - 64 * d:T, :].rearrange("(p r) c -> p r c", r=d)[:, :, hc]),
                         (Vt4[64 * er:64 * er + 64, :, ur, :], vG1[0:64 * d, :].rearrange("(p r) c -> p r c", r=d)[:, :, hc])]
                if d < 16:
                    vo = vOwn.rearrange("(u two p r c) -> two p r u c", two=2, p=64, r=d, c=512)
                    for rr in range(d):
                        pairs.append((Vt4[64:128, rr, 0:nqb // 2, :], vo[0, :, rr, :, hc]))
                        pairs.append((Vt4[0:64, rr, 1:nqb // 2 + 1, :], vo[1, :, rr, :, hc]))
                else:
                    pairs.append((Vt4[64:128, :, 0, :], vOwn.rearrange("(p r c) -> p r c", r=d, c=512)[:, :, hc]))
                cx.dma("sp", pairs, reads=[self.rkvg("vb"), self.r_kvb["vb"]], writes=[rVt])
                cx.op("dve", lambda h: h.tensor_scalar(out=Vt4[0:64, :, 0, :], in0=Vt4[0:64, :, 0, :],
                                                       scalar1=self.mlr_rep[0:64, 0:1], scalar2=None, op0=ALU.mult),
                      reads=[rVt, self.rC], writes=[rVt])
                cx.op("dve", lambda h: h.tensor_scalar(out=Vt4[64 * er:64 * er + 64, :, ur, :], in0=Vt4[64 * er:64 * er + 64, :, ur, :],
                                                       scalar1=self.mlr_rep[64 * er:64 * er + 64, 1:2], scalar2=None, op0=ALU.mult),
                      reads=[rVt, self.rC], writes=[rVt])
                if d == 16:
                    cx.op("dve", lambda h: h.memset(Vt4[64:128, :, 1, :], 0.0), writes=[rVt])
                qtiles = [(r, qt) for r in range(d) for qt in range(nqt)]
                nb = 256 // QN
                batches = [qtiles[i:i + nb] for i in range(0, len(qtiles), nb)]
                K2 = 128 if d < 16 else 64

                def ksl(r, u, n):
                    us = T + r + d * (128 * u - 64)
                    return KTe[:, us:us + (n - 1) * d + 1:d]

                def emit_S(bi):
                    sbk = S_B[cnt["sb"] % 2]
                    cnt["sb"] += 1
                    batch = batches[bi]
                    for t_, (r, qt) in enumerate(batch):
                        qs = r + d * QN * qt
                        qsl = Qb[qi][:, qs:qs + (QN - 1) * d + 1:d]
                        c1 = t_ * QN
                        c2 = nb * QN + t_ * QN
                        cx.op("pe", lambda h: h.matmul(self.ps[sbk][:, c1:c1 + QN], ksl(r, qt, 128), qsl, start=True, stop=True),
                              reads=[rKTe, rQb[qi]], writes=[self.rP[sbk]], inc=False)
                        cx.op("pe", lambda h: h.matmul(self.ps[sbk][0:K2, c2:c2 + QN], ksl(r, qt + 1, K2), qsl, start=True, stop=True),
                              reads=[rKTe, rQb[qi]], writes=[self.rP[sbk]], inc=(t_ == len(batch) - 1))
                    return sbk

                def emit_PV(bi, sbk, k_):
                    batch = batches[bi]
                    p_, rp_ = Pm[k_ % 2], rPm[k_ % 2]
                    if K2 == 128:
                        cx.op("act", lambda h: h.activation(out=p_, in_=self.ps[sbk][:, 0:512], func=AF.Exp, scale=scale),
                              reads=[self.rP[sbk]], writes=[rp_])
                    else:
                        cx.op("dve", lambda h: h.memset(p_[64:128, 256:512], 0.0), writes=[rp_])
                        cx.op("act", lambda h: h.activation(out=p_[:, 0:256], in_=self.ps[sbk][:, 0:256], func=AF.Exp, scale=scale),
                              reads=[self.rP[sbk]], writes=[rp_])
                        cx.op("act", lambda h: h.activation(out=p_[0:64, 256:512], in_=self.ps[sbk][0:64, 256:512], func=AF.Exp, scale=scale),
                              reads=[self.rP[sbk]], writes=[rp_])
                    m4 = self.bmask01[:].rearrange("p (m q) -> p m q", m=2)[:, :, 0:QN].unsqueeze(2).to_broadcast([128, 2, nb, QN])
                    p4 = p_.rearrange("p (m t q) -> p m t q", m=2, t=nb)
                    cx.op("dve", lambda h: h.tensor_tensor(out=p4, in0=p4, in1=m4, op=ALU.mult), reads=[self.rC], writes=[rp_])
                    fpos = (bi * 256) % 512
                    if fpos == 0:
                        cnt["bg"] += 1
                    ob, ub = O_B[cnt["bg"] % 2], U_B[cnt["bg"] % 2]
                    for t_, (r, qt) in enumerate(batch):
                        oc = fpos + t_ * QN
                        c1 = t_ * QN
                        c2 = nb * QN + t_ * QN
                        cx.op("pe", lambda h: h.matmul(self.ps[ob][:, oc:oc + QN], Vt4[:, r, qt, :], p_[:, c1:c1 + QN], start=True, stop=False),
                              reads=[rVt, rp_], writes=[self.rP[ob]], inc=False)
                        cx.op("pe", lambda h: h.matmul(self.ps[ob][:, oc:oc + QN], Vt4[:, r, qt + 1, :], p_[:, c2:c2 + QN],
                                                       start=False, stop=True),
                              reads=[rVt, rp_], writes=[self.rP[ob]], inc=False)
                        o1 = self.onesE[:, 0, :] if qt == 0 else self.ones[:]
                        if qt + 1 == ntile - 1:
                            o2 = self.onesE[:, 1, :] if K2 == 128 else self.onesE[:, 2, :]
                        else:
                            o2 = self.ones[:, :]
                        cx.op("pe", lambda h: h.matmul(self.ps[ub][:, oc:oc + QN], o1, p_[:, c1:c1 + QN], start=True, stop=False),
                              reads=[self.rC, rp_], writes=[self.rP[ub]], inc=False)
                        cx.op("pe", lambda h: h.matmul(self.ps[ub][:, oc:oc + QN], o2, p_[:, c2:c2 + QN], start=False, stop=True),
                              reads=[self.rC, rp_], writes=[self.rP[ub]], inc=True)
                    if fpos == 256:
                        evac(bi, ob, ub)

                def evac(bi, ob, ub):
                    f0 = (bi * 256) - 256
                    if d == 1:
                        ov = accO[:, hd, f0:f0 + 512]
                        sv = accS[:, f0:f0 + 512]
                        pso = self.ps[ob][:, 0:512]
                        psu = self.ps[ub][:, 0:512]
                    else:
                        rA = f0 // Lh
                        nres_b = 512 // Lh
                        ov = accO[:, hd, :].rearrange("p (i r) -> p r i", r=d)[:, rA:rA + nres_b, :]
                        sv = accS.rearrange("p (i r) -> p r i", r=d)[:, rA:rA + nres_b, :]
                        pso = self.ps[ob][:, 0:512].rearrange("p (r i) -> p r i", i=Lh)
                        psu = self.ps[ub][:, 0:512].rearrange("p (r i) -> p r i", i=Lh)
                    if g == 0:
                        cx.op("dve", lambda h: h.tensor_copy(ov, pso), reads=[self.rP[ob]], writes=[raccO[hd]])
                        cx.op("act", lambda h: h.activation(out=sv, in_=psu, func=AF.Copy), reads=[self.rP[ub]], writes=[raccS])
                    else:
                        cx.op("dve", lambda h: h.tensor_tensor(out=ov, in0=pso, in1=ov, op=ALU.add),
                              reads=[self.rP[ob]], writes=[raccO[hd]])
                        cx.op("dve", lambda h: h.tensor_tensor(out=sv, in0=psu, in1=sv, op=ALU.add),
                              reads=[self.rP[ub]], writes=[raccS])

                sb_next = emit_S(0)
                for bi in range(len(batches)):
                    sb_cur = sb_next
                    if bi + 1 < len(batches):
                        sb_next = emit_S(bi + 1)
                    emit_PV(bi, sb_cur, cnt["sb"] + bi)
                    if g == 0 and bi == 0:
                        while pend_norm:
                            pend_norm.pop(0)()
            def head_norm(hd=hd):
                for c2 in range(2):
                    cs = slice(c2 * 512, (c2 + 1) * 512)
                    cx.op("act", lambda h: h.activation(out=accS[:, cs], in_=accS[:, cs], func=AF.Ln), reads=[raccS], writes=[raccS])
                    cx.op("act", lambda h: h.activation(out=accS[:, cs], in_=accS[:, cs], func=AF.Exp, scale=-1.0), reads=[raccS], writes=[raccS])
                    cx.op("dve", lambda h: h.tensor_tensor(out=accO[:, hd, cs], in0=accO[:, hd, cs], in1=accS[:, cs], op=ALU.mult),
                          reads=[raccS], writes=[raccO[hd]])
                    s_ = (hd * 2 + c2) % 2
                    cx.op("dve", lambda h: h.tensor_tensor(out=sq[s_], in0=accO[:, hd, cs], in1=accO[:, hd, cs], op=ALU.mult),
                          reads=[raccO[hd]], writes=[rsq[s_]])
                    cx.op("pe", lambda h: h.matmul(self.ps[STAT[c2]][:, 0:512], self.ones[:], sq[s_], start=(hd == 0), stop=(hd == 3)),
                          reads=[self.rC, rsq[s_]], writes=[self.rP[STAT[c2]]], inc=True)
            pend_norm.append(head_norm)
        while pend_norm:
            pend_norm.pop(0)()
        ronall = self.sres("accO_all")
        cx.wait_all("dve", raccO)
        cx.wait_all("act", [rKTe])
        for c2 in range(2):
            self.group_norm_out(STAT[c2], 512, [(hh, 6 + hh) for hh in range(4)],
                                lambda hh: accO[:, hh, c2 * 512:(c2 + 1) * 512], ronall, c2, (lnb, rstd), (rln, rrs))

    def phase3(self, l):
        cx, ar = self.cx, self.ar
        self.phase_barrier()
        self.load_gain(self.g_ffn[l:l + 1, :])
        t = self.norm_tmp()
        hns = [ar.bf(D) for _ in range(2)]
        rhn = [self.sres("hn%d" % i) for i in range(2)]
        nb = 0
        dq = []
        for c in range(4):
            W, rWs = self.wget()
            Wv = W[:, 0:8192].rearrange("p (k n) -> p k n", n=512)
            for tt in range(NT):
                bank = nb % 4
                nb += 1
                for ft in range(16):
                    cx.op("pe", lambda h: h.matmul(self.ps[bank][:, 0:512], self.actT[:, ft, tt * 128:(tt + 1) * 128], Wv[:, ft, :],
                                                   start=(ft == 0), stop=(ft == 15)),
                          reads=[self.rAT[tt], rWs], writes=[self.rP[bank]], inc=(ft == 15))
                xs = self.X[:, tt, c * 512:(c + 1) * 512]
                cx.op("dve", lambda h: h.tensor_tensor(out=xs, in0=self.ps[bank][:, 0:512], in1=xs, op=ALU.add),
                      reads=[self.rP[bank]], writes=[self.rX[tt]])
                if c == 3:
                    while len(dq) > 0:
                        dq.pop(0)()
                    self._ffn_norm_tile(tt, t, hns, rhn, dq)
        while dq:
            dq.pop(0)()
        cx.dma("sp", [(self.halo_b[0:1, :], self.X[0:1, 0, :]), (self.halo_b[1:2, :], self.X[127:128, NT - 1, :])],
               reads=[self.rX[0], self.rX[NT - 1]], writes=[self.r_halo_b])
        cx.collective(self.halo_b[:, :], self.halo_g[:, :], reads=[self.r_halo_b], writes=[self.r_halo_g])
        if "x" in self.dbg:
            r = cx.res("xdbg")
            cx.dma("sp", [(self.xdbg.rearrange("(tt p) c -> p tt c", p=128), self.X[:, :, :])], reads=self.rX, writes=[r])

    def _ffn_norm_tile(self, tt, t, hns, rhn, defer=None):
        t = dict(t)
        t["hn"], t["rhn"] = hns[tt % 2], rhn[tt % 2]
        t["pb"] = (6, 7) if tt % 2 == 0 else (4, 5)
        self.norm_transpose(self.X[:, tt, :], self.rX[tt], 128,
                            lambda k0, n, tt=tt: self.actT[:, k0:k0 + n, tt * 128:(tt + 1) * 128],
                            self.rAT[tt], t, defer)

    def phase4(self, l):
        cx, ar = self.cx, self.ar
        self.phase_barrier()
        t = self.norm_tmp()
        hns = [ar.bf(D) for _ in range(1)]
        rhn = [self.sres("hn%d" % i) for i in range(1)]
        hrow = ar.f32(D)
        rhrow = self.sres("hrow")
        cx.dma("sp", [(hrow[0:2, :], self.halo_g[1:3, :])], reads=[self.r_halo_g], writes=[rhrow])
        t["hn"], t["rhn"] = hns[0], rhn[0]
        t["pb"] = (6, 7)
        t["rowscale"] = self.mlr_col[0:2, 0:1]
        self.norm_transpose(hrow[0:2, :], rhrow, 2, lambda k0, n: self.actT_full[:, k0:k0 + n, 0:T + 2:T + 1], self.rAH, t)
        self.phase_barrier()
        hbuf = [ar.f32(1026) for _ in range(2)]
        rhb = [self.sres("hbuf%d" % i) for i in range(2)]
        tg = ar.f32(T)
        tu = ar.f32(T)
        gs = ar.f32(T)
        rtg, rtu, rgs = self.sres("tg"), self.sres("tu"), self.sres("gs")
        aT = [ar.bf(4 * T).rearrange("p (a b) -> p a b", b=T) for _ in range(2)]
        raT = [self.sres("aT%d" % i) for i in range(2)]
        last_layer = (l == self.depth - 1)
        tail_ok = (self.stop is None)
        tn = self.norm_tmp()
        if last_layer:
            obf = [ar.f32(D)]
            robf = [self.sres("obf0")]
        else:
            hns2 = [ar.bf(D) for _ in range(2)]
            rhn2 = [self.sres("hn2_%d" % i) for i in range(2)]

        tdq = []

        def tail_tile(tt):
            if not tail_ok:
                return
            if last_layer:
                self._final_tile(tt, tn, obf[0], robf[0])
            else:
                while tdq:
                    tdq.pop(0)()
                t2 = dict(tn)
                t2["hn"], t2["rhn"] = hns2[tt % 2], rhn2[tt % 2]
                t2["pb"] = (0, 1) if tt % 2 == 0 else (2, 3)
                self.norm_transpose(self.X[:, tt, :], self.rX[tt], 128,
                                    lambda k0, n, tt=tt: self.actT[:, k0:k0 + n, tt * 128:(tt + 1) * 128],
                                    self.rAT[tt], t2, tdq)
        cnt = {"nt": 0, "db": 0}
        CW = (T + 2) // 3

        def up_tile(Wv, rWs, col0, n_idx, tdst, rtd):
            k = cnt["nt"]
            cnt["nt"] += 1
            banks = (0, 1, 2) if k % 2 == 0 else (3, 4, 5)
            hb_, rhb_ = hbuf[k % 2], rhb[k % 2]
            for c3, bank in enumerate(banks):
                for kt in range(16):
                    cx.op("pe", lambda h: h.matmul(self.ps[bank][:, 0:CW], Wv[:, kt, col0:col0 + 128],
                                                   self.actT_full[:, kt, c3 * CW:(c3 + 1) * CW], start=(kt == 0), stop=(kt == 15)),
                          reads=self.rAT + [self.rAH, rWs], writes=[self.rP[bank]], inc=(kt == 15))
            w0 = self.cvp[:, n_idx, 0:1]
            w1 = self.cvp[:, n_idx, 1:2]
            w2 = self.cvp[:, n_idx, 2:3]
            bb = self.cvp[:, n_idx, 3:4]
            for c3, bank in enumerate(banks):
                cx.op("act", lambda h: h.activation(out=hb_[:, c3 * CW:(c3 + 1) * CW], in_=self.ps[bank][:, 0:CW], func=AF.Copy),
                      reads=[self.rP[bank]], writes=[rhb_])
                lo = 1 if c3 == 0 else 0
                hi = CW - 1 if c3 == 2 else CW
                t0_ = c3 * CW + lo - 1
                cx.op("act", lambda h: h.activation(out=tdst[:, t0_:t0_ + (hi - lo)], in_=self.ps[bank][:, lo:hi], func=AF.Identity,
                                                    scale=w1, bias=bb),
                      reads=[self.rP[bank], self.rLP], writes=[rtd])
            cx.op("dve", lambda h: h.scalar_tensor_tensor(out=tdst, in0=hb_[:, 0:T], scalar=w0, in1=tdst, op0=ALU.mult, op1=ALU.add),
                  reads=[rhb_, self.rLP], writes=[rtd])
            cx.op("dve", lambda h: h.scalar_tensor_tensor(out=tdst, in0=hb_[:, 2:T + 2], scalar=w2, in1=tdst, op0=ALU.mult, op1=ALU.add),
                  reads=[rhb_, self.rLP], writes=[rtd])

        for step in FFN_STEPS:
            if step[0] == "up":
                gg, b2 = step[1], step[2]
                bi = 2 * gg + b2
                W, rWs = self.wget()
                Wv = W[:, 0:8192].rearrange("p (k n) -> p k n", n=512)
                for j in range(2):
                    ci = 2 * b2 + j
                    ct = 2 * bi + j
                    up_tile(Wv, rWs, j * 128, ct, tg, rtg)
                    cx.op("act", lambda h: h.activation(out=gs, in_=tg, func=AF.Silu), reads=[rtg], writes=[rgs])
                    up_tile(Wv, rWs, 256 + j * 128, 44 + ct, tu, rtu)
                    cx.op("dve", lambda h: h.tensor_tensor(out=aT[gg % 2][:, ci, :], in0=gs, in1=tu, op=ALU.mult),
                          reads=[rgs, rtu], writes=[raT[gg % 2]])
            else:
                gg = step[1]
                W, rWs = self.wget()
                Wd = W[:, 0:8192].rearrange("p (k n) -> p k n", n=2048)
                if gg == 10 and tail_ok:
                    self.load_gain(self.g_final[0:1, :] if last_layer else self.g_attn[l + 1:l + 2, :])
                for n4 in range(4):
                    for tt in range(NT):
                        bank = 6 + cnt["db"] % 2
                        cnt["db"] += 1
                        for ci in range(4):
                            cx.op("pe", lambda h: h.matmul(self.ps[bank][:, 0:512], aT[gg % 2][:, ci, tt * 128:(tt + 1) * 128],
                                                           Wd[:, ci, n4 * 512:(n4 + 1) * 512], start=(ci == 0), stop=(ci == 3)),
                                  reads=[raT[gg % 2], rWs], writes=[self.rP[bank]], inc=(ci == 3))
                        xs = self.X[:, tt, n4 * 512:(n4 + 1) * 512]
                        cx.op("dve", lambda h: h.tensor_tensor(out=xs, in0=self.ps[bank][:, 0:512], in1=xs, op=ALU.add),
                              reads=[self.rP[bank]], writes=[self.rX[tt]])
                        if gg == 10 and n4 == 3:
                            tail_tile(tt)
        while tdq:
            tdq.pop(0)()
        if "x2" in self.dbg:
            r = cx.res("xdbg2")
            cx.dma("sp", [(self.xdbg.rearrange("(tt p) c -> p tt c", p=128), self.X[:, :, :])], reads=self.rX, writes=[r])

    def _final_tile(self, tt, t, ob, rob):
        cx = self.cx
        ss, rss = t["ss"], t["rss"]
        xin = self.X[:, tt, :]
        cx.op("act", lambda h: h.activation(out=t["junk"], in_=xin, func=AF.Square, accum_out=ss[:, 0:1]),
              reads=[self.rX[tt]], writes=[t["rj"], rss])
        cx.op("act", lambda h: h.activation(out=ss[:, 1:2], in_=ss[:, 0:1], func=AF.Sqrt, scale=1.0 / D, bias=EPS),
              reads=[rss], writes=[rss])
        cx.op("dve", lambda h: h.reciprocal(out=ss[:, 2:3], in_=ss[:, 1:2]), reads=[rss], writes=[rss])
        cx.op("dve", lambda h: h.scalar_tensor_tensor(out=ob, in0=xin, scalar=ss[:, 2:3], in1=self.grep[:], op0=ALU.mult, op1=ALU.mult),
              reads=[self.rX[tt], rss, self.rG], writes=[rob])
        ro = cx.res("out%d" % tt)
        cx.dma("sp", [(self.out_d[tt * 128:(tt + 1) * 128, :], ob)], reads=[rob], writes=[ro])

    def epilogue(self):
        pass


_CACHE = {}


def _get_program(depth=DEPTH, dbg=None, stop=None):
    key = (depth, tuple(sorted(dbg or ())), stop)
    if key not in _CACHE:
        b = Builder(depth=depth, dbg=dbg, stop=stop)
        nc = b.build()
        _CACHE[key] = (nc, b)
    return _CACHE[key]


def make_in_maps(inputs):
    f = lambda a: np.ascontiguousarray(np.asarray(a, dtype=np.float32))
    x = f(inputs["x"])
    shared = {
        "w_in": f(inputs["w_in"]), "w_uq": f(inputs["w_uq"]), "w_ukv": f(inputs["w_ukv"]),
        "w_out": f(inputs["w_out"]), "w_up": f(inputs["w_up"]), "w_down": f(inputs["w_down"]),
        "attn_norm": f(inputs["attn_norm"]), "ffn_norm": f(inputs["ffn_norm"]),
        "final_norm": f(inputs["final_norm"]).reshape(1, D),
        "a_q_norm": f(inputs["a_q_norm"]), "a_k_norm": f(inputs["a_k_norm"]),
        "c_q_norm": f(inputs["c_q_norm"]), "c_kv_norm": f(inputs["c_kv_norm"]),
    }
    on = f(inputs["out_norm"])
    shared["out_norm_t"] = np.ascontiguousarray(on.reshape(DEPTH, 16, 128).transpose(0, 2, 1))
    cw = f(inputs["conv_w"])
    cb = f(inputs["conv_b"])
    cp = np.concatenate([cw, cb[:, None, :]], axis=1)
    cp = cp.reshape(DEPTH, 4, 88, 128).transpose(0, 3, 2, 1)
    shared["convp"] = np.ascontiguousarray(cp.reshape(DEPTH, 128, 88 * 4))
    maps = []
    for c in range(NCORES):
        b, h = c // 2, c % 2
        m = dict(shared)
        m["x"] = np.ascontiguousarray(x[b, h * T:(h + 1) * T, :])
        m.update(_consts_for_core(h))
        maps.append(m)
    return maps


def kernel(**inputs):
    nc, _ = _get_program()
    maps = make_in_maps(inputs)
    res = run_bass_kernel_spmd(nc, maps, core_ids=list(range(NCORES)))
    out = np.empty((4, S, D), np.float32)
    for c in range(NCORES):
        b, h = c // 2, c % 2
        out[b, h * T:(h + 1) * T, :] = res.results[c]["out"]
    return out
```
